# Optimizing a Trainium2 kernel written in Bass

```python
import jax
import jax.numpy as jnp
from jax import lax

D_MODEL = 1024
BATCH = 2
SEQ = 16384
DEPTH = 2

GRID_W = 64
CTX_LEN = 256
Q_BLOCK = 128
WINDOW = 128
ROPE_BASE = 10000.0
EPS = 1e-6
NEG_INF = -1e30

MLA_HEADS = 8
MLA_NOPE = 64
MLA_ROPE = 32
MLA_V = 64
MLA_Q_RANK = 256
MLA_KV_RANK = 256

GQA_HEADS = 8
GQA_KV_HEADS = 2
GQA_DIM = 64

WIN_HEADS = 8
WIN_KV_HEADS = 2
WIN_DIM = 64

CONV_CH = 512
CONV_K = 3

AB_MIX = MLA_HEADS * MLA_V + GQA_HEADS * GQA_DIM
AB_SPLIT = (MLA_Q_RANK, MLA_KV_RANK, MLA_ROPE,
            GQA_HEADS * GQA_DIM, GQA_KV_HEADS * GQA_DIM, GQA_KV_HEADS * GQA_DIM,
            AB_MIX)
AB_IN = sum(AB_SPLIT)
CD_MIX = WIN_HEADS * WIN_DIM + CONV_CH
CD_SPLIT = (WIN_HEADS * WIN_DIM, WIN_KV_HEADS * WIN_DIM, WIN_KV_HEADS * WIN_DIM,
            CONV_CH, CONV_CH, CONV_CH, CD_MIX)
CD_IN = sum(CD_SPLIT)

kernel_name = 'hybrid_mla_gqa_window_conv_dit'


def rmsnorm(x, gain=None):
    xf = x.astype(jnp.float32)
    y = xf * lax.rsqrt(jnp.mean(xf * xf, axis=-1, keepdims=True) + EPS)
    if gain is not None:
        y = y * gain.astype(jnp.float32)
    return y.astype(x.dtype)


def adaln(x, mod):
    shift, scale, gate = jnp.split(mod, 3, axis=-1)
    return rmsnorm(x) * (1.0 + scale) + shift, gate


def split_cols(t, sizes):
    out, off = [], 0
    for n in sizes:
        out.append(t[..., off:off + n])
        off += n
    return out


def grid_positions(n):
    rows = n // GRID_W
    row = jnp.repeat(jnp.arange(rows, dtype=jnp.float32), GRID_W)
    col = jnp.tile(jnp.arange(GRID_W, dtype=jnp.float32), rows)
    return row, col


def rope_1d(x, pos):
    half = x.shape[-1] // 2
    inv = ROPE_BASE ** (-jnp.arange(half, dtype=jnp.float32) / half)
    ang = pos[:, None] * inv[None, :]
    cos = jnp.cos(ang)[:, None, :].astype(x.dtype)
    sin = jnp.sin(ang)[:, None, :].astype(x.dtype)
    x1, x2 = x[..., :half], x[..., half:]
    return jnp.concatenate([x1 * cos - x2 * sin, x2 * cos + x1 * sin], axis=-1)


def axial_rope(x, row, col):
    half = x.shape[-1] // 2
    return jnp.concatenate([rope_1d(x[..., :half], row), rope_1d(x[..., half:], col)], axis=-1)


def rope_tail(x, rot_dim, row, col):
    return jnp.concatenate([x[..., :-rot_dim], axial_rope(x[..., -rot_dim:], row, col)], axis=-1)


def ctx_attention(q, k, v, sink=None):
    b, l, h, d = q.shape
    hk = k.shape[2]
    g = h // hk
    qg = q.reshape(b, l, hk, g, d)
    s = jnp.einsum('bqkgd,bnkd->bkgqn', qg, k, preferred_element_type=jnp.float32) * (d ** -0.5)
    if sink is not None:
        sk = jnp.broadcast_to(sink.reshape(hk, g).astype(jnp.float32)[None, :, :, None, None], (b, hk, g, l, 1))
        p = jax.nn.softmax(jnp.concatenate([s, sk], axis=-1), axis=-1)[..., :l]
    else:
        p = jax.nn.softmax(s, axis=-1)
    o = jnp.einsum('bkgqn,bnkd->bqkgd', p.astype(v.dtype), v)
    return o.reshape(b, l, h, v.shape[-1])


def joint_dense_attention(q, k, v, kc, vc):
    b, s, h, d = q.shape
    hk = k.shape[2]
    g = h // hk
    nb = s // Q_BLOCK
    kk = jnp.concatenate([kc, k], axis=1)
    vv = jnp.concatenate([vc, v], axis=1)
    qb = jnp.moveaxis(q.reshape(b, nb, Q_BLOCK, hk, g, d), 1, 0)

    def block(qi):
        sc = jnp.einsum('bqkgd,bnkd->bkgqn', qi, kk, preferred_element_type=jnp.float32) * (d ** -0.5)
        p = jax.nn.softmax(sc, axis=-1)
        return jnp.einsum('bkgqn,bnkd->bqkgd', p.astype(vv.dtype), vv)

    o = lax.map(block, qb)
    return jnp.moveaxis(o, 0, 1).reshape(b, s, h, v.shape[-1])


def joint_window_attention(q, k, v, kc, vc, sink):
    b, s, h, d = q.shape
    hk = k.shape[2]
    g = h // hk
    nb = s // Q_BLOCK
    l = kc.shape[1]
    pad = ((0, 0), (Q_BLOCK, Q_BLOCK), (0, 0), (0, 0))
    kp, vp = jnp.pad(k, pad), jnp.pad(v, pad)

    def bands(t):
        return jnp.concatenate(
            [t[:, j * Q_BLOCK: j * Q_BLOCK + s].reshape(b, nb, Q_BLOCK, hk, t.shape[-1]) for j in range(3)], axis=2)

    kb = jnp.moveaxis(bands(kp), 1, 0)
    vb = jnp.moveaxis(bands(vp), 1, 0)
    qb = jnp.moveaxis(q.reshape(b, nb, Q_BLOCK, hk, g, d), 1, 0)
    blk = jnp.arange(nb)[:, None, None] * Q_BLOCK
    qpos = blk + jnp.arange(Q_BLOCK)[None, :, None]
    kpos = blk - Q_BLOCK + jnp.arange(3 * Q_BLOCK)[None, None, :]
    mask = (jnp.abs(qpos - kpos) <= WINDOW) & (kpos >= 0) & (kpos < s)
    sink_l = sink.reshape(hk, g).astype(jnp.float32)
    scale = d ** -0.5

    def block(args):
        qi, ki, vi, mi = args
        s_loc = jnp.einsum('bqkgd,bmkd->bkgqm', qi, ki, preferred_element_type=jnp.float32) * scale
        s_loc = jnp.where(mi[None, None, None], s_loc, NEG_INF)
        s_ctx = jnp.einsum('bqkgd,blkd->bkgql', qi, kc, preferred_element_type=jnp.float32) * scale
        s_snk = jnp.broadcast_to(sink_l[None, :, :, None, None], (b, hk, g, Q_BLOCK, 1))
        p = jax.nn.softmax(jnp.concatenate([s_ctx, s_loc, s_snk], axis=-1), axis=-1)
        return (jnp.einsum('bkgql,blkd->bqkgd', p[..., :l].astype(vc.dtype), vc)
                + jnp.einsum('bkgqm,bmkd->bqkgd', p[..., l:l + 3 * Q_BLOCK].astype(vi.dtype), vi))

    o = lax.map(block, (qb, kb, vb, mask))
    return jnp.moveaxis(o, 0, 1).reshape(b, s, h, v.shape[-1])


def short_conv(u, w):
    ch = u.shape[-1]
    return lax.conv_general_dilated(u, w[:, None, :].astype(u.dtype), window_strides=(1,),
                                    padding=((CONV_K // 2, CONV_K // 2),),
                                    dimension_numbers=('NWC', 'WIO', 'NWC'), feature_group_count=ch)


def heads_ab(pp, cq_gain, ckv_gain, w_uq, w_ukv, q_gain, k_gain, qg_gain, kg_gain):
    cq, ckv, kr, gq, gk, gv, gate = split_cols(pp, AB_SPLIT)
    lead = pp.shape[:-1]
    qa = (rmsnorm(cq, cq_gain) @ w_uq).reshape(lead + (MLA_HEADS, MLA_NOPE + MLA_ROPE))
    kv = (rmsnorm(ckv, ckv_gain) @ w_ukv).reshape(lead + (MLA_HEADS, MLA_NOPE + MLA_V))
    kr_h = jnp.broadcast_to(kr[..., None, :], lead + (MLA_HEADS, MLA_ROPE))
    ka = jnp.concatenate([kv[..., :MLA_NOPE], kr_h], axis=-1)
    va = kv[..., MLA_NOPE:]
    qb = rmsnorm(gq.reshape(lead + (GQA_HEADS, GQA_DIM)), qg_gain)
    kb = rmsnorm(gk.reshape(lead + (GQA_KV_HEADS, GQA_DIM)), kg_gain)
    vb = gv.reshape(lead + (GQA_KV_HEADS, GQA_DIM))
    return rmsnorm(qa, q_gain), rmsnorm(ka, k_gain), va, qb, kb, vb, gate


def ab_layer(x, xc, mod, mod_c, w_in, w_out, cq_gain, ckv_gain, w_uq, w_ukv, q_gain, k_gain,
             qg_gain, kg_gain, row, col, update_ctx):
    b, s, _ = x.shape
    h, gate = adaln(x, mod)
    hc, gate_c = adaln(xc, mod_c)
    wts = (cq_gain, ckv_gain, w_uq, w_ukv, q_gain, k_gain, qg_gain, kg_gain)
    qa, ka, va, qb, kb, vb, g = heads_ab(h @ w_in, *wts)
    qac, kac, vac, qbc, kbc, vbc, gc = heads_ab(hc @ w_in, *wts)
    qa, ka = rope_tail(qa, MLA_ROPE, row, col), rope_tail(ka, MLA_ROPE, row, col)
    qb, kb = axial_rope(qb, row, col), axial_rope(kb, row, col)
    oa = joint_dense_attention(qa, ka, va, kac, vac).reshape(b, s, -1)
    ob = joint_dense_attention(qb, kb, vb, kbc, vbc).reshape(b, s, -1)
    mix = jnp.concatenate([oa, ob], axis=-1) * jax.nn.silu(g)
    x = x + gate * (mix @ w_out)
    if update_ctx:
        l = xc.shape[1]
        oac = ctx_attention(qac, kac, vac).reshape(b, l, -1)
        obc = ctx_attention(qbc, kbc, vbc).reshape(b, l, -1)
        mixc = jnp.concatenate([oac, obc], axis=-1) * jax.nn.silu(gc)
        xc = xc + gate_c * (mixc @ w_out)
    return x, xc


def heads_cd(pp, q_gain, k_gain):
    pq, pk, pv, gb, gcc, gh, gate = split_cols(pp, CD_SPLIT)
    lead = pp.shape[:-1]
    q = rmsnorm(pq.reshape(lead + (WIN_HEADS, WIN_DIM)), q_gain)
    k = rmsnorm(pk.reshape(lead + (WIN_KV_HEADS, WIN_DIM)), k_gain)
    v = pv.reshape(lead + (WIN_KV_HEADS, WIN_DIM))
    return q, k, v, gb, gcc, gh, gate


def cd_layer(x, xc, mod, mod_c, w_in, w_out, q_gain, k_gain, sink, conv_w, row, col, update_ctx):
    b, s, _ = x.shape
    h, gate = adaln(x, mod)
    hc, gate_c = adaln(xc, mod_c)
    q, k, v, gb, gcv, gh, g = heads_cd(h @ w_in, q_gain, k_gain)
    qc, kc, vc, gbc, gcvc, ghc, gc = heads_cd(hc @ w_in, q_gain, k_gain)
    q, k = axial_rope(q, row, col), axial_rope(k, row, col)
    oc = joint_window_attention(q, k, v, kc, vc, sink).reshape(b, s, -1)
    od = gb * short_conv(gcv * gh, conv_w)
    mix = jnp.concatenate([oc, od], axis=-1) * jax.nn.silu(g)
    x = x + gate * (mix @ w_out)
    if update_ctx:
        l = xc.shape[1]
        occ = ctx_attention(qc, kc, vc, sink).reshape(b, l, -1)
        odc = gbc * short_conv(gcvc * ghc, conv_w)
        mixc = jnp.concatenate([occ, odc], axis=-1) * jax.nn.silu(gc)
        xc = xc + gate_c * (mixc @ w_out)
    return x, xc


def setup_inputs(seed: int = 0) -> dict:
    key = jax.random.key(seed)
    ks = iter(jax.random.split(key, 32))
    f32 = jnp.float32
    n_ab = (DEPTH + 1) // 2
    n_cd = DEPTH // 2

    def nrm(shape, scale):
        return jax.random.normal(next(ks), shape, f32) * scale

    def gain(shape):
        return 1.0 + 0.1 * jax.random.normal(next(ks), shape, f32)

    return {
        'x': nrm((BATCH, SEQ, D_MODEL), 1.0),
        'c': nrm((BATCH, D_MODEL), 1.0),
        'ctx': nrm((BATCH, CTX_LEN, D_MODEL), 1.0),
        'c_ctx': nrm((D_MODEL,), 1.0),
        'mod_w': nrm((DEPTH, D_MODEL, 3 * D_MODEL), D_MODEL ** -0.5),
        'mod_b': nrm((DEPTH, 3 * D_MODEL), 0.02),
        'ab_w_in': nrm((n_ab, D_MODEL, AB_IN), D_MODEL ** -0.5),
        'ab_w_out': nrm((n_ab, AB_MIX, D_MODEL), AB_MIX ** -0.5),
        'mla_cq_gain': gain((n_ab, MLA_Q_RANK)),
        'mla_ckv_gain': gain((n_ab, MLA_KV_RANK)),
        'mla_w_uq': nrm((n_ab, MLA_Q_RANK, MLA_HEADS * (MLA_NOPE + MLA_ROPE)), MLA_Q_RANK ** -0.5),
        'mla_w_ukv': nrm((n_ab, MLA_KV_RANK, MLA_HEADS * (MLA_NOPE + MLA_V)), MLA_KV_RANK ** -0.5),
        'mla_q_gain': gain((n_ab, MLA_NOPE + MLA_ROPE)),
        'mla_k_gain': gain((n_ab, MLA_NOPE + MLA_ROPE)),
        'gqa_q_gain': gain((n_ab, GQA_DIM)),
        'gqa_k_gain': gain((n_ab, GQA_DIM)),
        'cd_w_in': nrm((n_cd, D_MODEL, CD_IN), D_MODEL ** -0.5),
        'cd_w_out': nrm((n_cd, CD_MIX, D_MODEL), CD_MIX ** -0.5),
        'win_q_gain': gain((n_cd, WIN_DIM)),
        'win_k_gain': gain((n_cd, WIN_DIM)),
        'win_sink': nrm((n_cd, WIN_HEADS), 0.5),
        'conv_w': nrm((n_cd, CONV_K, CONV_CH), CONV_K ** -0.5),
    }


def reference(x, c, ctx, c_ctx, mod_w, mod_b, ab_w_in, ab_w_out, mla_cq_gain, mla_ckv_gain,
              mla_w_uq, mla_w_ukv, mla_q_gain, mla_k_gain, gqa_q_gain, gqa_k_gain,
              cd_w_in, cd_w_out, win_q_gain, win_k_gain, win_sink, conv_w):
    n = x.shape[1]
    row, col = grid_positions(n)
    sc = jax.nn.silu(c)
    scc = jax.nn.silu(c_ctx)
    xc = ctx
    for i in range(DEPTH):
        mod = (sc @ mod_w[i] + mod_b[i])[:, None, :]
        mod_c = scc @ mod_w[i] + mod_b[i]
        j = i // 2
        update = i < DEPTH - 1
        if i % 2 == 0:
            x, xc = ab_layer(x, xc, mod, mod_c, ab_w_in[j], ab_w_out[j], mla_cq_gain[j], mla_ckv_gain[j],
                             mla_w_uq[j], mla_w_ukv[j], mla_q_gain[j], mla_k_gain[j],
                             gqa_q_gain[j], gqa_k_gain[j], row, col, update)
        else:
            x, xc = cd_layer(x, xc, mod, mod_c, cd_w_in[j], cd_w_out[j], win_q_gain[j], win_k_gain[j],
                             win_sink[j], conv_w[j], row, col, update)
    return x
```

```python
import numpy as np
from contextlib import ExitStack
import concourse.bass as bass
import concourse.mybir as mybir
from concourse.bass_utils import run_bass_kernel_spmd

F32 = mybir.dt.float32
BF16 = mybir.dt.bfloat16
ALU = mybir.AluOpType
AF = mybir.ActivationFunctionType
AX = mybir.AxisListType

D = 1024
KC = 8
L = 256
EPS = 1e-6
GRID_W = 64
NCORES = 8


class Tn:
    def __init__(self, h, const=False, psum=False):
        self.h = h
        self.w = {}
        self.r = {}
        self.const = const
        self.psum = psum

    def __getitem__(self, k):
        return self.h[k]


class FW:
    def __init__(self, nc, es):
        self.nc = nc
        self.E = {'sp': nc.sync, 'act': nc.scalar, 'pool': nc.gpsimd, 'dve': nc.vector, 'pe': nc.tensor}
        self.sem = {}
        self.tot = {}
        self.seen = {e: {} for e in self.E}
        for e in self.E:
            self.sem[e] = es.enter_context(nc.semaphore('s_' + e))
            self.tot[e] = 0
        self.ring = {}
        self.rpos = {}
        for e, n in (('sp', 30), ('pool', 24), ('act', 8)):
            keys = []
            for i in range(n):
                k = 'd_%s%d' % (e, i)
                self.sem[k] = es.enter_context(nc.semaphore(k))
                self.tot[k] = 0
                keys.append(k)
            self.ring[e] = keys
            self.rpos[e] = 0
        self.nins = {e: 0 for e in self.E}

    def _wait(self, e, deps):
        for k, v in deps.items():
            if v <= 0:
                continue
            if k == e and e == 'pe':
                continue
            if self.seen[e].get(k, 0) >= v:
                continue
            assert v <= self.tot[k], "wait on unclosed group %s %d>%d (eng %s)" % (k, v, self.tot[k], e)
            self.E[e].wait_ge(self.sem[k], v)
            self.seen[e][k] = v

    @staticmethod
    def _deps(e, reads, writes, accw):
        d = {}
        for b in reads:
            for k, v in b.w.items():
                if d.get(k, 0) < v:
                    d[k] = v
            if b.psum:
                for k, v in b.r.items():
                    if k != e and d.get(k, 0) < v:
                        d[k] = v
        for b in writes:
            for k, v in b.w.items():
                if d.get(k, 0) < v:
                    d[k] = v
            for k, v in b.r.items():
                if d.get(k, 0) < v:
                    d[k] = v
        for b in accw:
            for k, v in b.r.items():
                if d.get(k, 0) < v:
                    d[k] = v
        return d

    def op(self, e, fn, reads=(), writes=(), inc=True):
        self._wait(e, self._deps(e, reads, writes, ()))
        ins = fn(self.E[e])
        self.nins[e] += 1
        val = self.tot[e] + 1
        if inc:
            ins.then_inc(self.sem[e], 1)
            self.tot[e] = val
        for b in reads:
            if not b.const and b.r.get(e, 0) < val:
                b.r[e] = val
        for b in writes:
            b.w = {e: val}
            b.r = {}
        return ins

    def dma(self, e, pairs, reads=(), writes=(), accw=()):
        k = self.ring[e][self.rpos[e]]
        self.rpos[e] = (self.rpos[e] + 1) % len(self.ring[e])
        deps = self._deps(e, reads, writes, accw)
        if deps.get(k, 0) < self.tot[k]:
            deps[k] = self.tot[k]
        self._wait(e, deps)
        val = self.tot[k] + 16 * len(pairs)
        for (o, i) in pairs:
            self.E[e].dma_start(out=o, in_=i).then_inc(self.sem[k], 16)
            self.nins[e] += 1
        self.tot[k] = val
        for b in reads:
            if not b.const and b.r.get(k, 0) < val:
                b.r[k] = val
        for b in writes:
            b.w = {k: val}
            b.r = {}
        for b in accw:
            if b.w.get(k, 0) < val:
                b.w[k] = val

    def barrier(self, engines=None):
        for e in (engines or self.E):
            self._wait(e, dict(self.tot))


class Slots:
    def __init__(self, items):
        self.items = items
        self.i = 0

    def next(self):
        t = self.items[self.i]
        self.i = (self.i + 1) % len(self.items)
        return t


class _Stop(Exception):
    pass


def build(S, stop=None, dbg_out=()):
    T_own = S // 4
    T_q = T_own + 256
    NQT = T_q // 128
    NK = L + S
    NKT = NK // 128

    nc = bass.Bass("TRN2", target_bir_lowering=False)

    def din(name, shape, dt=F32):
        return Tn(nc.dram_tensor(name, list(shape), dt, kind="ExternalInput").ap(), const=True)

    def dscr(name, shape, dt):
        if name in dbg_out:
            return Tn(nc.dram_tensor(name, list(shape), dt, kind="ExternalOutput").ap())
        return Tn(nc.dram_tensor(name, list(shape), dt).ap())

    xq = din("xq", [T_q, D])
    xkv = din("xkv", [S, D])
    ctxb = din("ctxb", [L, D])
    cT = din("cT", [128, KC, 2])
    mod_w = din("mod_w", [2, D, 3 * D])
    mod_b = din("mod_b", [2, 3 * D])
    wkv0 = din("wkv0", [D, 544])
    wq0 = din("wq0", [D, 768])
    wg0 = din("wg0", [D, 1024])
    wout0 = din("wout0", [D, D])
    w_uq = din("w_uq", [256, 768])
    w_ukv = din("w_ukv", [256, 1024])
    cq_gain = din("cq_gain", [128, 2])
    ckv_gain = din("ckv_gain", [128, 2])
    gains0 = din("gains0", [1, 96 + 96 + 64 + 64])
    wt1 = din("wt1", [D, 768])
    wf1 = din("wf1", [D, 2560])
    wout1 = din("wout1", [D, D])
    gains1 = din("gains1", [1, 128])
    sink = din("sink", [1, 8])
    convw = din("convw", [128, 4, 3])
    ropeq = din("ropeq", [T_q, 192])
    ropek = din("ropek", [S, 192])
    valid = din("valid", [128, 2])
    out = Tn(nc.dram_tensor("out", [T_own, D], F32, kind="ExternalOutput").ap())

    modbc = dscr("modbc", [12, 128, D], F32)
    kTa = dscr("kTa", [8, 96, NK], BF16)
    vA = dscr("vA", [8, 128, NKT, 64], BF16)
    kTb = dscr("kTb", [2, 64, NK], BF16)
    vB = dscr("vB", [2, 128, NKT, 64], BF16)
    qTa = dscr("qTa", [8, 96, T_q], BF16)
    qTb = dscr("qTb", [8, 64, T_q], BF16)
    gT0 = dscr("gT0", [8, 128, T_q], BF16)
    mixT0 = dscr("mixT0", [8, 128, T_q], BF16)
    x1 = dscr("x1", [T_q, D], F32)
    qTa_c = dscr("qTa_c", [8, 96, L], BF16)
    qTb_c = dscr("qTb_c", [8, 64, L], BF16)
    gT0_c = dscr("gT0_c", [8, 128, L], BF16)
    mixT0_c = dscr("mixT0_c", [8, 128, L], BF16)
    xc1 = dscr("xc1", [L, D], F32)
    gbT = dscr("gbT", [4, 128, T_q], BF16)
    sgT = dscr("sgT", [8, 128, T_q], BF16)
    qT1 = dscr("qT1", [4, 128, T_q], BF16)
    uT = dscr("uT", [4, 128, T_q], BF16)

    es = ExitStack()
    try:
      with es:
        fw = FW(nc, es)

        def chk(name):
            if stop == name:
                fw.barrier()
                build.nins = dict(fw.nins)
                raise _Stop()

        uid = [0]

        def sb(stk, name, shape, dt, const=False):
            uid[0] += 1
            return Tn(stk.enter_context(nc.sbuf_tensor("%s_%d" % (name, uid[0]), list(shape), dt)), const=const)

        def ps(stk, name, shape, dt=F32):
            uid[0] += 1
            return Tn(stk.enter_context(nc.psum_tensor("%s_%d" % (name, uid[0]), list(shape), dt)), psum=True)

        def slots(stk, name, n, shape, dt, psum=False):
            return Slots([(ps if psum else sb)(stk, "%s%d" % (name, i), shape, dt) for i in range(n)])

        ident = sb(es, "ident", [128, 128], BF16)
        fw.op('pool', lambda g: g.memset(ident[:], 0.0), writes=[ident])
        fw.op('pool', lambda g: g.affine_select(out=ident[:], in_=ident[:], pattern=[[-1, 128]],
                                                compare_op=ALU.not_equal, fill=1.0, base=0,
                                                channel_multiplier=1), reads=[ident], writes=[ident])
        ident.const = True
        validt = sb(es, "validt", [128, 2], F32)
        fw.dma('sp', [(validt[:], valid[:, :])], writes=[validt])
        validt.const = True

        with ExitStack() as st:
            cTt = sb(st, "cTt", [128, KC, 2], F32)
            scT = sb(st, "scT", [128, KC, 2], F32)
            mws = slots(st, "mws", 2, [128, KC, 512], F32)
            modsb = sb(st, "modsb", [2, 2, 3 * D], F32)
            modbias = sb(st, "modbias", [2, 2, 3 * D], F32)
            sel = sb(st, "sel", [2, 2, 128], F32)
            selw = sb(st, "selw", [2, 128], F32)
            pm = slots(st, "pm", 2, [2, 512], F32, psum=True)
            pb = slots(st, "pb", 2, [128, 1024], F32, psum=True)
            bcs = slots(st, "bcs", 2, [128, D], F32)

            fw.dma('sp', [(cTt[:], cT[:, :, :])], writes=[cTt])
            fw.op('act', lambda a: a.activation(out=scT[:], in_=cTt[:], func=AF.Silu), reads=[cTt], writes=[scT])
            for i in range(2):
                fw.dma('sp', [(modbias[0:1, i, :], mod_b[i:i + 1, :]), (modbias[1:2, i, :], mod_b[i:i + 1, :])],
                       accw=[modbias])
            fw.op('pool', lambda g: g.memset(sel[:], 0.0), writes=[sel])
            for who in range(2):
                fw.op('pool', lambda g, who=who: g.affine_select(
                    out=sel[:, who, :], in_=sel[:, who, :], pattern=[[0, 128]], compare_op=ALU.not_equal,
                    fill=1.0, base=-who, channel_multiplier=1), reads=[sel], writes=[sel])
            for i in range(2):
                for n in range(6):
                    mw = mws.next()
                    fw.dma('sp', [(mw[:], mod_w[i, :, n * 512:(n + 1) * 512].rearrange("(k p) n -> p k n", p=128))],
                           writes=[mw])
                    p = pm.next()
                    for k in range(KC):
                        fw.op('pe', lambda t, k=k, p=p, mw=mw: t.matmul(p[:], lhsT=scT[:, k, :], rhs=mw[:, k, :],
                                                                       start=(k == 0), stop=(k == KC - 1)),
                              reads=[scT, mw], writes=[p], inc=(k == KC - 1))
                    fw.op('dve', lambda v, p=p, i=i, n=n: v.tensor_tensor(
                        out=modsb[:, i, n * 512:(n + 1) * 512], in0=p[:], in1=modbias[:, i, n * 512:(n + 1) * 512],
                        op=ALU.add), reads=[p, modbias], writes=[modsb])
            for i in range(2):
                for who in range(2):
                    for which in range(3):
                        p = pb.next()
                        for n in range(2):
                            fw.op('pe', lambda t, p=p, n=n, i=i, who=who, which=which: t.matmul(
                                p[:, n * 512:(n + 1) * 512], lhsT=sel[:, who, :],
                                rhs=modsb[:, i, which * D + n * 512: which * D + (n + 1) * 512],
                                start=True, stop=True), reads=[sel, modsb], writes=[p], inc=(n == 1))
                        bc = bcs.next()
                        if which == 1:
                            fw.op('dve', lambda v, p=p, bc=bc: v.tensor_scalar(out=bc[:], in0=p[:], scalar1=1.0,
                                                                              scalar2=None, op0=ALU.add),
                                  reads=[p], writes=[bc])
                        else:
                            fw.op('dve', lambda v, p=p, bc=bc: v.tensor_copy(out=bc[:], in_=p[:]), reads=[p], writes=[bc])
                        fw.dma('pool', [(modbc[i * 6 + who * 3 + which, :, :], bc[:])], reads=[bc], accw=[modbc])
            fw.barrier()

        def load_w_bf16(stk, name, src_ap_fn, ncols, stage, gain=None, kchunks=KC):
            w = sb(stk, name, [128, kchunks, ncols], BF16)
            for k in range(kchunks):
                s_ = stage.next()
                fw.dma('sp', [(s_[:, 0:ncols], src_ap_fn(k))], writes=[s_])
                if gain is None:
                    fw.op('pool', lambda g, k=k, s_=s_: g.tensor_copy(out=w[:, k, :], in_=s_[:, 0:ncols]),
                          reads=[s_], writes=[w])
                else:
                    fw.op('dve', lambda v, k=k, s_=s_: v.tensor_scalar(out=w[:, k, :], in0=s_[:, 0:ncols],
                                                                      scalar1=gain[:, k:k + 1], scalar2=None,
                                                                      op0=ALU.mult), reads=[s_, gain], writes=[w])
            return w

        def load_bc(stk, name, idx):
            t = sb(stk, name, [128, D], F32)
            fw.dma('sp', [(t[:], modbc[idx, :, :])], reads=[modbc], writes=[t])
            return t

        class Front:
            def __init__(self, stk, Jmax, tp):
                self.xt = slots(stk, "f_xt", 2, [128, Jmax, D], F32)
                self.junk = sb(stk, "f_junk", [128, D], BF16)
                self.ss = slots(stk, "f_ss", 2, [128, Jmax], F32)
                self.rs = slots(stk, "f_rs", 2, [128, Jmax], F32)
                self.t32 = slots(stk, "f_t32", 1, [128, D], F32)
                self.hb = slots(stk, "f_hb", 2, [128, D], BF16)
                self.hT = slots(stk, "f_hT", 2, [128, KC, Jmax * 128], BF16)
                self.tp = tp

            def run(self, src, r0, J, sc1, sh):
                xt = self.xt.next()
                fw.dma('sp', [(xt[:, 0:J, :], src[r0:r0 + J * 128, :].rearrange("(j p) d -> p j d", p=128))],
                       reads=[src], writes=[xt])
                ss = self.ss.next()
                rs = self.rs.next()
                for j in range(J):
                    fw.op('act', lambda a, j=j: a.activation(out=self.junk[:], in_=xt[:, j, :], func=AF.Square,
                                                            accum_out=ss[:, j:j + 1]),
                          reads=[xt], writes=[self.junk, ss])
                fw.op('act', lambda a: a.activation(out=ss[:, 0:J], in_=ss[:, 0:J], func=AF.Sqrt, bias=EPS,
                                                    scale=1.0 / D), reads=[ss], writes=[ss])
                fw.op('dve', lambda v: v.reciprocal(out=rs[:, 0:J], in_=ss[:, 0:J]), reads=[ss], writes=[rs])
                hT = self.hT.next()
                for j in range(J):
                    t32 = self.t32.next()
                    hb = self.hb.next()
                    fw.op('dve', lambda v, j=j, t32=t32: v.scalar_tensor_tensor(
                        out=t32[:], in0=xt[:, j, :], scalar=rs[:, j:j + 1], in1=sc1[:], op0=ALU.mult, op1=ALU.mult),
                        reads=[xt, rs, sc1], writes=[t32])
                    fw.op('pool', lambda g, t32=t32, hb=hb: g.tensor_tensor(out=hb[:], in0=t32[:], in1=sh[:], op=ALU.add),
                          reads=[t32, sh], writes=[hb])
                    p = self.tp.next()
                    for k in range(KC):
                        fw.op('pe', lambda t, k=k, p=p, hb=hb: t.transpose(out=p[:, k * 128:(k + 1) * 128],
                                                                          in_=hb[:, k * 128:(k + 1) * 128],
                                                                          identity=ident[:]),
                              reads=[hb, ident], writes=[p], inc=(k == KC - 1))
                    fw.op('act', lambda a, j=j, p=p: a.copy(out=hT[:, :, j * 128:(j + 1) * 128],
                                                          in_=p[:].rearrange("p (k t) -> p k t", k=KC)),
                          reads=[p], writes=[hT])
                return hT

        def grp_rstd(src_ap, sq, ssum, rstd, n, Dh, rd):
            sqv = sq[:, 0:n * Dh].rearrange("p (n d) -> p n d", d=Dh)
            fw.op('dve', lambda v: v.tensor_tensor(out=sqv, in0=src_ap, in1=src_ap, op=ALU.mult),
                  reads=rd, writes=[sq])
            fw.op('dve', lambda v: v.tensor_reduce(out=ssum[:, 0:n], in_=sqv, axis=AX.X, op=ALU.add),
                  reads=[sq], writes=[ssum])
            fw.op('act', lambda a: a.activation(out=ssum[:, 0:n], in_=ssum[:, 0:n], func=AF.Sqrt, bias=EPS,
                                                scale=1.0 / Dh), reads=[ssum], writes=[ssum])
            fw.op('dve', lambda v: v.reciprocal(out=rstd[:, 0:n], in_=ssum[:, 0:n]), reads=[ssum], writes=[rstd])

        def rope(y, J, G, Dh, o, R, tab, cofs, tmp1, tmp2):
            q4 = R // 4
            yr = y[:, 0:J, :, o:o + R]
            C = tab[:, 0:J, cofs:cofs + R].unsqueeze(2).to_broadcast([128, J, G, R])
            t1 = tmp1[:, 0:J * G * R].rearrange("p (j g r) -> p j g r", j=J, g=G)
            t2 = tmp2[:, 0:J * G * R].rearrange("p (j g r) -> p j g r", j=J, g=G)
            fw.op('dve', lambda v: v.tensor_tensor(out=t1, in0=yr, in1=C, op=ALU.mult), reads=[y, tab], writes=[tmp1])
            for a in range(2):
                for hf in range(2):
                    dst = t2[:, :, :, a * 2 * q4 + hf * q4: a * 2 * q4 + (hf + 1) * q4]
                    srcv = y[:, 0:J, :, o + a * 2 * q4 + (1 - hf) * q4: o + a * 2 * q4 + (2 - hf) * q4]
                    sn = tab[:, 0:J, cofs + R + a * 2 * q4 + hf * q4: cofs + R + a * 2 * q4 + (hf + 1) * q4] \
                        .unsqueeze(2).to_broadcast([128, J, G, q4])
                    fw.op('dve', lambda v, dst=dst, srcv=srcv, sn=sn: v.tensor_tensor(out=dst, in0=srcv, in1=sn, op=ALU.mult),
                          reads=[y, tab], writes=[tmp2])
            fw.op('dve', lambda v: v.tensor_tensor(out=yr, in0=t1, in1=t2, op=ALU.add),
                  reads=[tmp1, tmp2], writes=[y])

        def phase_kv0():
            with ExitStack() as st:
                tp = slots(st, "tp", 2, [128, 1024], BF16, psum=True)
                pj = slots(st, "pj", 2, [128, 1024], F32, psum=True)
                fr = Front(st, 4, tp)
                stage = slots(st, "wst", 2, [128, 1024], F32)
                gck = sb(st, "gck", [128, 2], F32)
                fw.dma('sp', [(gck[:], ckv_gain[:, :])], writes=[gck])
                wkv = load_w_bf16(st, "wkv", lambda k: wkv0[k * 128:(k + 1) * 128, :], 544, stage)
                wuk = load_w_bf16(st, "wuk", lambda k: w_ukv[k * 128:(k + 1) * 128, :], 1024, stage, gain=gck, kchunks=2)
                gbc = sb(st, "gbc", [128, 320], F32)
                fw.dma('sp', [(gbc[:], gains0[0:1, :].partition_broadcast(128))], writes=[gbc])
                bc = {0: (load_bc(st, "sc1t", 1), load_bc(st, "sht", 0)), 1: (load_bc(st, "sc1c", 4), load_bc(st, "shc", 3))}
                tabs = slots(st, "tab", 2, [128, 4, 192], F32)
                stp = slots(st, "stp", 1, [128, 4, 544], F32)
                sq = sb(st, "sq", [128, 3072], F32)
                ssum = slots(st, "ssum", 4, [128, 32], F32)
                rstd = slots(st, "rstd", 4, [128, 32], F32)
                ckn = slots(st, "ckn", 2, [128, 4, 256], BF16)
                cTt = slots(st, "cTt", 2, [128, 2, 512], BF16)
                stK = slots(st, "stK", 1, [128, 4, 8, 96], F32)
                Kn = slots(st, "Kn", 1, [128, 4, 8, 96], BF16)
                Vst = slots(st, "Vst", 2, [128, 4, 8, 64], BF16)
                kTst = slots(st, "kTst", 2, [96, 8, 512], BF16)
                gkn = slots(st, "gkn", 1, [128, 4, 2, 64], F32)
                gkb = slots(st, "gkb", 2, [128, 4, 128], BF16)
                Vbst = slots(st, "Vbst", 2, [128, 4, 2, 64], BF16)
                kTbst = slots(st, "kTbst", 2, [128, 512], BF16)
                tmp1 = sq
                tmp2 = sb(st, "tmp2", [128, 1024], F32)

                units = [(ctxb, 0, 2, 1, None, 0)]
                for u in range(S // 512):
                    units.append((xkv, u * 512, 4, 0, u * 512, 2 + u * 4))
                for (src, r0, J, who, rp0, t0) in units:
                    sc1, sh = bc[who]
                    hT = fr.run(src, r0, J, sc1, sh)
                    chk('kv0_a')
                    tab = None
                    if rp0 is not None:
                        tab = tabs.next()
                        fw.dma('sp', [(tab[:, 0:J, :], ropek[rp0:rp0 + J * 128, :].rearrange("(j p) c -> p j c", p=128))],
                               writes=[tab])
                    s_ = stp.next()
                    for j in range(J):
                        p = pj.next()
                        for k in range(KC):
                            fw.op('pe', lambda t, k=k, p=p, j=j: t.matmul(p[:, 0:512], lhsT=hT[:, k, j * 128:(j + 1) * 128],
                                                                         rhs=wkv[:, k, 0:512], start=(k == 0),
                                                                         stop=(k == KC - 1)),
                                  reads=[hT, wkv], writes=[p], inc=False)
                        for k in range(KC):
                            fw.op('pe', lambda t, k=k, p=p, j=j: t.matmul(p[:, 512:544], lhsT=hT[:, k, j * 128:(j + 1) * 128],
                                                                         rhs=wkv[:, k, 512:544], start=(k == 0),
                                                                         stop=(k == KC - 1)),
                                  reads=[hT, wkv], writes=[p], inc=(k == KC - 1))
                        fw.op('act', lambda a, p=p, j=j: a.copy(out=s_[:, j, :], in_=p[:, 0:544]), reads=[p], writes=[s_])
                    sm = ssum.next()
                    rsd = rstd.next()
                    sqv = sq[:, 0:J * 256].rearrange("p (j c) -> p j c", j=J)
                    fw.op('dve', lambda v: v.tensor_tensor(out=sqv, in0=s_[:, 0:J, 0:256], in1=s_[:, 0:J, 0:256], op=ALU.mult),
                          reads=[s_], writes=[sq])
                    fw.op('dve', lambda v: v.tensor_reduce(out=sm[:, 0:J], in_=sqv, axis=AX.X, op=ALU.add),
                          reads=[sq], writes=[sm])
                    fw.op('act', lambda a: a.activation(out=sm[:, 0:J], in_=sm[:, 0:J], func=AF.Sqrt, bias=EPS,
                                                        scale=1.0 / 256), reads=[sm], writes=[sm])
                    fw.op('dve', lambda v: v.reciprocal(out=rsd[:, 0:J], in_=sm[:, 0:J]), reads=[sm], writes=[rsd])
                    cn = ckn.next()
                    fw.op('dve', lambda v: v.tensor_tensor(out=cn[:, 0:J, :], in0=s_[:, 0:J, 0:256],
                                                           in1=rsd[:, 0:J].unsqueeze(2).to_broadcast([128, J, 256]),
                                                           op=ALU.mult), reads=[s_, rsd], writes=[cn])
                    chk('kv0_b')
                    ct = cTt.next()
                    for j in range(J):
                        p = tp.next()
                        for k2 in range(2):
                            fw.op('pe', lambda t, k2=k2, p=p, j=j: t.transpose(out=p[:, k2 * 128:(k2 + 1) * 128],
                                                                              in_=cn[:, j, k2 * 128:(k2 + 1) * 128],
                                                                              identity=ident[:]),
                                  reads=[cn, ident], writes=[p], inc=(k2 == 1))
                        fw.op('act', lambda a, p=p, j=j: a.copy(out=ct[:, :, j * 128:(j + 1) * 128],
                                                              in_=p[:, 0:256].rearrange("p (k t) -> p k t", k=2)),
                              reads=[p], writes=[ct])
                    sk = stK.next()
                    vs = Vst.next()
                    for j in range(J):
                        p = pj.next()
                        for n in range(2):
                            for k2 in range(2):
                                fw.op('pe', lambda t, k2=k2, n=n, p=p, j=j: t.matmul(
                                    p[:, n * 512:(n + 1) * 512], lhsT=ct[:, k2, j * 128:(j + 1) * 128],
                                    rhs=wuk[:, k2, n * 512:(n + 1) * 512], start=(k2 == 0), stop=(k2 == 1)),
                                    reads=[ct, wuk], writes=[p], inc=(n == 1 and k2 == 1))
                        pv_ = p[:].rearrange("p (h c) -> p h c", h=8)
                        fw.op('act', lambda a, j=j, pv_=pv_: a.copy(out=sk[:, j, :, 0:64], in_=pv_[:, :, 0:64]),
                              reads=[p], writes=[sk])
                        fw.op('dve', lambda v, j=j, pv_=pv_: v.tensor_copy(out=vs[:, j, :, :], in_=pv_[:, :, 64:128]),
                              reads=[p], writes=[vs])
                    fw.op('pool', lambda g: g.tensor_copy(out=sk[:, 0:J, :, 64:96],
                                                          in_=s_[:, 0:J, 512:544].unsqueeze(2).to_broadcast([128, J, 8, 32])),
                          reads=[s_], writes=[sk])
                    chk('kv0_c')
                    sm = ssum.next()
                    rsd = rstd.next()
                    skv = sk[:, 0:J, :, :].rearrange("p j h c -> p (j h) c")
                    grp_rstd(skv, sq, sm, rsd, J * 8, 96, [sk])
                    fw.op('dve', lambda v: v.tensor_tensor(out=skv, in0=skv,
                                                           in1=rsd[:, 0:J * 8].unsqueeze(2).to_broadcast([128, J * 8, 96]),
                                                           op=ALU.mult), reads=[sk, rsd], writes=[sk])
                    fw.op('dve', lambda v: v.tensor_tensor(out=skv, in0=skv,
                                                           in1=gbc[:, 96:192].unsqueeze(1).to_broadcast([128, J * 8, 96]),
                                                           op=ALU.mult), reads=[sk, gbc], writes=[sk])
                    if tab is not None:
                        rope(sk, J, 8, 96, 64, 32, tab, 0, tmp1, tmp2)
                    kn = Kn.next()
                    fw.op('pool', lambda g: g.tensor_copy(out=kn[:, 0:J], in_=sk[:, 0:J]), reads=[sk], writes=[kn])
                    chk('kv0_d')
                    kt = kTst.next()
                    for j in range(J):
                        p = tp.next()
                        for h in range(8):
                            fw.op('pe', lambda t, h=h, p=p, j=j: t.transpose(out=p[0:96, h * 128:(h + 1) * 128],
                                                                            in_=kn[:, j, h, :], identity=ident[:]),
                                  reads=[kn, ident], writes=[p], inc=(h == 7))
                        fw.op('act', lambda a, p=p, j=j: a.copy(out=kt[:, :, j * 128:(j + 1) * 128],
                                                              in_=p[0:96, :].rearrange("p (h t) -> p h t", h=8)),
                              reads=[p], writes=[kt])
                    k0 = t0 * 128
                    fw.dma('pool', [(kTa[:, :, k0:k0 + J * 128].rearrange("h d t -> d h t"), kt[:, :, 0:J * 128])],
                           reads=[kt], accw=[kTa])
                    chk('kv0_e')
                    fw.dma('pool', [(vA[h, :, t0:t0 + J, :], vs[:, 0:J, h, :]) for h in range(8)], reads=[vs], accw=[vA])
                    chk('kv0_f')
                    gk = gkn.next()
                    fw.op('pool', lambda g: g.tensor_copy(out=gk[:, 0:J].rearrange("p j g c -> p j (g c)"),
                                                          in_=s_[:, 0:J, 256:384]), reads=[s_], writes=[gk])
                    vb = Vbst.next()
                    fw.op('pool', lambda g: g.tensor_copy(out=vb[:, 0:J].rearrange("p j g c -> p j (g c)"),
                                                          in_=s_[:, 0:J, 384:512]), reads=[s_], writes=[vb])
                    sm = ssum.next()
                    rsd = rstd.next()
                    gkv = gk[:, 0:J, :, :].rearrange("p j g c -> p (j g) c")
                    grp_rstd(gkv, sq, sm, rsd, J * 2, 64, [gk])
                    fw.op('dve', lambda v: v.tensor_tensor(out=gkv, in0=gkv,
                                                           in1=rsd[:, 0:J * 2].unsqueeze(2).to_broadcast([128, J * 2, 64]),
                                                           op=ALU.mult), reads=[gk, rsd], writes=[gk])
                    fw.op('dve', lambda v: v.tensor_tensor(out=gkv, in0=gkv,
                                                           in1=gbc[:, 256:320].unsqueeze(1).to_broadcast([128, J * 2, 64]),
                                                           op=ALU.mult), reads=[gk, gbc], writes=[gk])
                    if tab is not None:
                        rope(gk, J, 2, 64, 0, 64, tab, 64, tmp1, tmp2)
                    gb_ = gkb.next()
                    fw.op('pool', lambda g: g.tensor_copy(out=gb_[:, 0:J, :], in_=gk[:, 0:J].rearrange("p j g c -> p j (g c)")),
                          reads=[gk], writes=[gb_])
                    ktb = kTbst.next()
                    p = tp.next()
                    for j in range(J):
                        fw.op('pe', lambda t, p=p, j=j: t.transpose(out=p[:, j * 128:(j + 1) * 128], in_=gb_[:, j, :],
                                                                   identity=ident[:]),
                              reads=[gb_, ident], writes=[p], inc=(j == J - 1))
                    fw.op('act', lambda a, p=p: a.copy(out=ktb[:, 0:J * 128], in_=p[:, 0:J * 128]), reads=[p], writes=[ktb])
                    fw.dma('pool', [(kTb[g, :, k0:k0 + J * 128], ktb[g * 64:(g + 1) * 64, 0:J * 128]) for g in range(2)],
                           reads=[ktb], accw=[kTb])
                    fw.dma('pool', [(vB[g, :, t0:t0 + J, :], vb[:, 0:J, g, :]) for g in range(2)], reads=[vb], accw=[vB])
                fw.barrier()

        def phase_q0(src, units, who, rtab, qTa_d, qTb_d, gT_d):
            with ExitStack() as st:
                tp = slots(st, "tp", 2, [128, 1024], BF16, psum=True)
                pj = slots(st, "pj", 2, [128, 1024], F32, psum=True)
                fm = slots(st, "fm", 2, [128, 512], F32, psum=True)
                fr = Front(st, 2, tp)
                stage = slots(st, "wst", 2, [128, 1024], F32)
                gcq = sb(st, "gcq", [128, 2], F32)
                fw.dma('sp', [(gcq[:], cq_gain[:, :])], writes=[gcq])
                wq = load_w_bf16(st, "wq", lambda k: wq0[k * 128:(k + 1) * 128, :], 768, stage)
                wg = load_w_bf16(st, "wg", lambda k: wg0[k * 128:(k + 1) * 128, :], 1024, stage)
                wuq = load_w_bf16(st, "wuq", lambda k: w_uq[k * 128:(k + 1) * 128, :], 768, stage, gain=gcq, kchunks=2)
                gbc = sb(st, "gbc", [128, 320], F32)
                fw.dma('sp', [(gbc[:], gains0[0:1, :].partition_broadcast(128))], writes=[gbc])
                fw.op('dve', lambda v: v.tensor_scalar(out=gbc[:, 0:96], in0=gbc[:, 0:96], scalar1=96.0 ** -0.5,
                                                       scalar2=None, op0=ALU.mult), reads=[gbc], writes=[gbc])
                fw.op('dve', lambda v: v.tensor_scalar(out=gbc[:, 192:256], in0=gbc[:, 192:256], scalar1=64.0 ** -0.5,
                                                       scalar2=None, op0=ALU.mult), reads=[gbc], writes=[gbc])
                sc1 = load_bc(st, "sc1", who * 3 + 1)
                sh = load_bc(st, "sh", who * 3 + 0)
                tabs = slots(st, "tab", 2, [128, 2, 192], F32)
                stp = slots(st, "stp", 1, [128, 2, 768], F32)
                sq = sb(st, "sq", [128, 1536], F32)
                ssum = slots(st, "ssum", 4, [128, 32], F32)
                rstd = slots(st, "rstd", 4, [128, 32], F32)
                cqn = slots(st, "cqn", 2, [128, 2, 256], BF16)
                cTt = slots(st, "cTt", 2, [128, 2, 256], BF16)
                stQ = slots(st, "stQ", 1, [128, 2, 8, 96], F32)
                Qn = slots(st, "Qn", 1, [128, 2, 8, 96], BF16)
                qTst = slots(st, "qTst", 2, [96, 8, 256], BF16)
                gqn = slots(st, "gqn", 1, [128, 2, 8, 64], F32)
                gqb = slots(st, "gqb", 1, [128, 2, 512], BF16)
                qTbst = slots(st, "qTbst", 2, [128, 4, 256], BF16)
                gTst = slots(st, "gTst", 2, [128, 8, 256], BF16)
                tmp1 = sq
                tmp2 = sb(st, "tmp2", [128, 1024], F32)
                for (r0, J) in units:
                    hT = fr.run(src, r0, J, sc1, sh)
                    tab = None
                    if rtab is not None:
                        tab = tabs.next()
                        fw.dma('sp', [(tab[:, 0:J, :], rtab[r0:r0 + J * 128, :].rearrange("(j p) c -> p j c", p=128))],
                               writes=[tab])
                    s_ = stp.next()
                    for j in range(J):
                        p = pj.next()
                        for (c0, c1) in ((0, 512), (512, 768)):
                            for k in range(KC):
                                fw.op('pe', lambda t, k=k, p=p, j=j, c0=c0, c1=c1: t.matmul(
                                    p[:, c0:c1], lhsT=hT[:, k, j * 128:(j + 1) * 128], rhs=wq[:, k, c0:c1],
                                    start=(k == 0), stop=(k == KC - 1)),
                                    reads=[hT, wq], writes=[p], inc=(c0 == 512 and k == KC - 1))
                        fw.op('act', lambda a, p=p, j=j: a.copy(out=s_[:, j, :], in_=p[:, 0:768]), reads=[p], writes=[s_])
                    gt = gTst.next()
                    for c in range(8):
                        p = fm.next()
                        for k in range(KC):
                            fw.op('pe', lambda t, k=k, p=p, c=c: t.matmul(p[:, 0:J * 128], lhsT=wg[:, k, c * 128:(c + 1) * 128],
                                                                         rhs=hT[:, k, 0:J * 128], start=(k == 0),
                                                                         stop=(k == KC - 1)),
                                  reads=[hT, wg], writes=[p], inc=(k == KC - 1))
                        fw.op('act', lambda a, p=p, c=c: a.activation(out=gt[:, c, 0:J * 128], in_=p[:, 0:J * 128], func=AF.Silu),
                              reads=[p], writes=[gt])
                    fw.dma('pool', [(gT_d[:, :, r0:r0 + J * 128].rearrange("c p t -> p c t"), gt[:, :, 0:J * 128])],
                           reads=[gt], accw=[gT_d])
                    sm = ssum.next()
                    rsd = rstd.next()
                    sqv = sq[:, 0:J * 256].rearrange("p (j c) -> p j c", j=J)
                    fw.op('dve', lambda v: v.tensor_tensor(out=sqv, in0=s_[:, 0:J, 0:256], in1=s_[:, 0:J, 0:256], op=ALU.mult),
                          reads=[s_], writes=[sq])
                    fw.op('dve', lambda v: v.tensor_reduce(out=sm[:, 0:J], in_=sqv, axis=AX.X, op=ALU.add),
                          reads=[sq], writes=[sm])
                    fw.op('act', lambda a: a.activation(out=sm[:, 0:J], in_=sm[:, 0:J], func=AF.Sqrt, bias=EPS,
                                                        scale=1.0 / 256), reads=[sm], writes=[sm])
                    fw.op('dve', lambda v: v.reciprocal(out=rsd[:, 0:J], in_=sm[:, 0:J]), reads=[sm], writes=[rsd])
                    cn = cqn.next()
                    fw.op('dve', lambda v: v.tensor_tensor(out=cn[:, 0:J, :], in0=s_[:, 0:J, 0:256],
                                                           in1=rsd[:, 0:J].unsqueeze(2).to_broadcast([128, J, 256]),
                                                           op=ALU.mult), reads=[s_, rsd], writes=[cn])
                    ct = cTt.next()
                    for j in range(J):
                        p = tp.next()
                        for k2 in range(2):
                            fw.op('pe', lambda t, k2=k2, p=p, j=j: t.transpose(out=p[:, k2 * 128:(k2 + 1) * 128],
                                                                              in_=cn[:, j, k2 * 128:(k2 + 1) * 128],
                                                                              identity=ident[:]),
                                  reads=[cn, ident], writes=[p], inc=(k2 == 1))
                        fw.op('act', lambda a, p=p, j=j: a.copy(out=ct[:, :, j * 128:(j + 1) * 128],
                                                              in_=p[:, 0:256].rearrange("p (k t) -> p k t", k=2)),
                              reads=[p], writes=[ct])
                    sQ = stQ.next()
                    for j in range(J):
                        p = pj.next()
                        for n in range(2):
                            for k2 in range(2):
                                fw.op('pe', lambda t, k2=k2, n=n, p=p, j=j: t.matmul(
                                    p[:, n * 512:n * 512 + 384], lhsT=ct[:, k2, j * 128:(j + 1) * 128],
                                    rhs=wuq[:, k2, n * 384:(n + 1) * 384], start=(k2 == 0), stop=(k2 == 1)),
                                    reads=[ct, wuq], writes=[p], inc=(n == 1 and k2 == 1))
                        fw.op('act', lambda a, j=j, p=p: a.copy(
                            out=sQ[:, j, :, :].rearrange("p (a h) c -> p a (h c)", a=2),
                            in_=p[:].rearrange("p (a n) -> p a n", a=2)[:, :, 0:384]), reads=[p], writes=[sQ])
                    sm = ssum.next()
                    rsd = rstd.next()
                    sQv = sQ[:, 0:J, :, :].rearrange("p j h c -> p (j h) c")
                    grp_rstd(sQv, sq, sm, rsd, J * 8, 96, [sQ])
                    fw.op('dve', lambda v: v.tensor_tensor(out=sQv, in0=sQv,
                                                           in1=rsd[:, 0:J * 8].unsqueeze(2).to_broadcast([128, J * 8, 96]),
                                                           op=ALU.mult), reads=[sQ, rsd], writes=[sQ])
                    fw.op('dve', lambda v: v.tensor_tensor(out=sQv, in0=sQv,
                                                           in1=gbc[:, 0:96].unsqueeze(1).to_broadcast([128, J * 8, 96]),
                                                           op=ALU.mult), reads=[sQ, gbc], writes=[sQ])
                    if tab is not None:
                        rope(sQ, J, 8, 96, 64, 32, tab, 0, tmp1, tmp2)
                    qn = Qn.next()
                    fw.op('pool', lambda g: g.tensor_copy(out=qn[:, 0:J], in_=sQ[:, 0:J]), reads=[sQ], writes=[qn])
                    qt = qTst.next()
                    for j in range(J):
                        p = tp.next()
                        for h in range(8):
                            fw.op('pe', lambda t, h=h, p=p, j=j: t.transpose(out=p[0:96, h * 128:(h + 1) * 128],
                                                                            in_=qn[:, j, h, :], identity=ident[:]),
                                  reads=[qn, ident], writes=[p], inc=(h == 7))
                        fw.op('act', lambda a, p=p, j=j: a.copy(out=qt[:, :, j * 128:(j + 1) * 128],
                                                              in_=p[0:96, :].rearrange("p (h t) -> p h t", h=8)),
                              reads=[p], writes=[qt])
                    fw.dma('pool', [(qTa_d[:, :, r0:r0 + J * 128].rearrange("h d t -> d h t"), qt[:, :, 0:J * 128])],
                           reads=[qt], accw=[qTa_d])
                    gq = gqn.next()
                    fw.op('pool', lambda g: g.tensor_copy(out=gq[:, 0:J].rearrange("p j h c -> p j (h c)"),
                                                          in_=s_[:, 0:J, 256:768]), reads=[s_], writes=[gq])
                    sm = ssum.next()
                    rsd = rstd.next()
                    gqv = gq[:, 0:J, :, :].rearrange("p j h c -> p (j h) c")
                    grp_rstd(gqv, sq, sm, rsd, J * 8, 64, [gq])
                    fw.op('dve', lambda v: v.tensor_tensor(out=gqv, in0=gqv,
                                                           in1=rsd[:, 0:J * 8].unsqueeze(2).to_broadcast([128, J * 8, 64]),
                                                           op=ALU.mult), reads=[gq, rsd], writes=[gq])
                    fw.op('dve', lambda v: v.tensor_tensor(out=gqv, in0=gqv,
                                                           in1=gbc[:, 192:256].unsqueeze(1).to_broadcast([128, J * 8, 64]),
                                                           op=ALU.mult), reads=[gq, gbc], writes=[gq])
                    if tab is not None:
                        rope(gq, J, 8, 64, 0, 64, tab, 64, tmp1, tmp2)
                    gb_ = gqb.next()
                    fw.op('pool', lambda g: g.tensor_copy(out=gb_[:, 0:J, :], in_=gq[:, 0:J].rearrange("p j h c -> p j (h c)")),
                          reads=[gq], writes=[gb_])
                    qtb = qTbst.next()
                    for j in range(J):
                        p = tp.next()
                        for m in range(4):
                            fw.op('pe', lambda t, m=m, p=p, j=j: t.transpose(out=p[:, m * 128:(m + 1) * 128],
                                                                            in_=gb_[:, j, m * 128:(m + 1) * 128],
                                                                            identity=ident[:]),
                                  reads=[gb_, ident], writes=[p], inc=(m == 3))
                        fw.op('act', lambda a, p=p, j=j: a.copy(out=qtb[:, :, j * 128:(j + 1) * 128],
                                                              in_=p[:, 0:512].rearrange("p (m t) -> p m t", m=4)),
                              reads=[p], writes=[qtb])
                    fw.dma('pool', [(qTb_d[:, :, r0:r0 + J * 128].rearrange("(m two) d t -> (two d) m t", two=2),
                                     qtb[:, :, 0:J * 128])], reads=[qtb], accw=[qTb_d])
                fw.barrier()

        def phase_attn0(T, kt0, kt1, qTa_d, qTb_d, gT_d, mix_d):
            nkt = kt1 - kt0
            with ExitStack() as st:
                pss = slots(st, "pss", 3, [128, 2, 512], F32, psum=True)
                pso = slots(st, "pso", 2, [128, 512], F32, psum=True)
                kT = sb(st, "kT", [128, nkt * 128], BF16)
                Ve = sb(st, "Ve", [128, nkt, 128], BF16)
                Vo = sb(st, "Vo", [128, nkt, 128], BF16)
                qTs = slots(st, "qTs", 2, [128, T], BF16)
                gTs = slots(st, "gTs", 2, [128, T], BF16)
                mxs = slots(st, "mxs", 2, [128, T], BF16)
                pT = slots(st, "pT", 3, [128, 2, 512], BF16)
                Rs = slots(st, "Rs", 2, [128, 512], F32)
                t32 = slots(st, "t32", 2, [128, 512], F32)
                fw.op('pool', lambda g: g.memset(Ve[:, :, 64:128], 1.0), writes=[Ve])
                fw.op('pool', lambda g: g.memset(Vo[:, :, 0:64], 1.0), writes=[Vo])
                qsup = []
                q0 = 0
                while q0 < T:
                    wq_ = min(512, T - q0)
                    qsup.append((q0, wq_))
                    q0 += wq_
                for c in range(8):
                    gt = gTs.next()
                    fw.dma('sp', [(gt[:], gT_d[c, :, :])], reads=[gT_d], writes=[gt])
                    mx = mxs.next()
                    for half in range(2):
                        hh = 2 * c + half
                        if c < 4:
                            d = 96
                            ksrc, vsrc, qsrc = kTa[hh], vA[hh], qTa_d[hh]
                            newk = True
                            newv = True
                        else:
                            d = 64
                            qh = hh - 8
                            g_ = qh // 4
                            ksrc, vsrc, qsrc = kTb[g_], vB[g_], qTb_d[qh]
                            newk = (qh % 4 == 0)
                            newv = (qh % 4 < 2)
                        V = Ve if half == 0 else Vo
                        vcol = 0 if half == 0 else 64
                        if newk:
                            nsp = 4 if nkt >= 8 else 1
                            stp_ = (nkt * 128) // nsp
                            for i in range(nsp):
                                fw.dma('sp', [(kT[0:d, i * stp_:(i + 1) * stp_], ksrc[:, kt0 * 128 + i * stp_: kt0 * 128 + (i + 1) * stp_])],
                                       reads=[kTa, kTb], writes=[kT] if i == 0 else (), accw=() if i == 0 else [kT])
                        if newv:
                            fw.dma('sp', [(V[:, :, vcol:vcol + 64], vsrc[:, kt0:kt1, :])], reads=[vA, vB], writes=[V])
                        qT = qTs.next()
                        fw.dma('sp', [(qT[0:d, :], qsrc[:, :])], reads=[qTa_d, qTb_d], writes=[qT])
                        for (q0, wq_) in qsup:
                            po = pso.next()
                            pairs = [(i, min(2, nkt - i)) for i in range(0, nkt, 2)]

                            def qk(pi):
                                i0, n = pairs[pi]
                                p = pss.next()
                                for i in range(n):
                                    fw.op('pe', lambda t, i=i, p=p: t.matmul(
                                        p[:, i, 0:wq_], lhsT=kT[0:d, (i0 + i) * 128:(i0 + i + 1) * 128],
                                        rhs=qT[0:d, q0:q0 + wq_], start=True, stop=True),
                                        reads=[kT, qT], writes=[p], inc=(i == n - 1))
                                return p

                            pend = qk(0)
                            for pi in range(len(pairs)):
                                i0, n = pairs[pi]
                                p = pend
                                pt = pT.next()
                                fw.op('act', lambda a, p=p, pt=pt, n=n: a.activation(out=pt[:, 0:n, 0:wq_], in_=p[:, 0:n, 0:wq_],
                                                                                    func=AF.Exp), reads=[p], writes=[pt])
                                if pi + 1 < len(pairs):
                                    pend = qk(pi + 1)
                                for i in range(n):
                                    kt_ = i0 + i
                                    fw.op('pe', lambda t, i=i, pt=pt, kt_=kt_: t.matmul(
                                        po[:, 0:wq_], lhsT=V[:, kt_, :], rhs=pt[:, i, 0:wq_],
                                        start=(kt_ == 0), stop=(kt_ == nkt - 1)),
                                        reads=[V, pt], writes=[po], inc=(i == n - 1))
                            R = Rs.next()
                            tt = t32.next()
                            o0, s0 = (0, 64) if half == 0 else (64, 0)
                            fw.op('dve', lambda v, R=R: v.reciprocal(out=R[o0:o0 + 64, 0:wq_], in_=po[s0:s0 + 64, 0:wq_]),
                                  reads=[po], writes=[R])
                            fw.op('dve', lambda v, R=R, tt=tt: v.tensor_tensor(out=tt[o0:o0 + 64, 0:wq_], in0=po[o0:o0 + 64, 0:wq_],
                                                                               in1=R[o0:o0 + 64, 0:wq_], op=ALU.mult),
                                  reads=[po, R], writes=[tt])
                            fw.op('pool', lambda g, tt=tt: g.tensor_tensor(out=mx[o0:o0 + 64, q0:q0 + wq_],
                                                                          in0=tt[o0:o0 + 64, 0:wq_],
                                                                          in1=gt[o0:o0 + 64, q0:q0 + wq_], op=ALU.mult),
                                  reads=[tt, gt], writes=[mx])
                    fw.dma('pool', [(mix_d[c, :, :], mx[:])], reads=[mx], accw=[mix_d])
                fw.barrier()

        def phase_out(src, units, mix_d, wout_d, gate_idx, dst, dst_r0=None, st_outer=None):
            with ExitStack() as st:
                pj = slots(st, "pj", 2, [128, 1024], F32, psum=True)
                stage = slots(st, "wst", 2, [128, 1024], F32)
                wo = load_w_bf16(st, "wo", lambda k: wout_d[k * 128:(k + 1) * 128, :], 1024, stage)
                gbc_ = load_bc(st, "gatebc", gate_idx)
                mxs = slots(st, "mxs", 2, [128, 8, 512], BF16)
                xts = slots(st, "xts", 2, [128, 4, D], F32)
                t32 = slots(st, "t32", 2, [128, D], F32)
                xo = slots(st, "xo", 2, [128, 4, D], F32)
                for (r0, J) in units:
                    mx = mxs.next()
                    fw.dma('sp', [(mx[:, :, 0:J * 128], mix_d[:, :, r0:r0 + J * 128].rearrange("c p t -> p c t"))],
                           reads=[mix_d], writes=[mx])
                    xt = xts.next()
                    fw.dma('sp', [(xt[:, 0:J, :], src[r0:r0 + J * 128, :].rearrange("(j p) d -> p j d", p=128))],
                           reads=[src], writes=[xt])
                    xo_ = xo.next()
                    for j in range(J):
                        p = pj.next()
                        for n in range(2):
                            for k in range(KC):
                                fw.op('pe', lambda t, k=k, n=n, p=p, j=j: t.matmul(
                                    p[:, n * 512:(n + 1) * 512], lhsT=mx[:, k, j * 128:(j + 1) * 128],
                                    rhs=wo[:, k, n * 512:(n + 1) * 512], start=(k == 0), stop=(k == KC - 1)),
                                    reads=[mx, wo], writes=[p], inc=(n == 1 and k == KC - 1))
                        tt = t32.next()
                        fw.op('dve', lambda v, p=p, tt=tt: v.tensor_tensor(out=tt[:], in0=p[:], in1=gbc_[:], op=ALU.mult),
                              reads=[p, gbc_], writes=[tt])
                        fw.op('pool', lambda g, tt=tt, j=j: g.tensor_tensor(out=xo_[:, j, :], in0=tt[:], in1=xt[:, j, :], op=ALU.add),
                              reads=[tt, xt], writes=[xo_])
                    d0 = r0 if dst_r0 is None else r0 - dst_r0
                    fw.dma('pool', [(dst[d0:d0 + J * 128, :].rearrange("(j p) d -> p j d", p=128), xo_[:, 0:J, :])],
                           reads=[xo_], accw=[dst])
                fw.barrier()

        units_q = [(u * 512, 4) for u in range(T_q // 512)]
        if T_q % 512:
            units_q.append(((T_q // 512) * 512, (T_q % 512) // 128))
        units_q2 = [(u * 256, 2) for u in range(T_q // 256)]
        chk('setup')
        phase_kv0()
        chk('kv0')
        phase_q0(xq, units_q2, 0, ropeq, qTa, qTb, gT0)
        phase_q0(ctxb, [(0, 2)], 1, None, qTa_c, qTb_c, gT0_c)
        chk('q0')
        phase_attn0(T_q, 0, NKT, qTa, qTb, gT0, mixT0)
        phase_attn0(L, 0, 2, qTa_c, qTb_c, gT0_c, mixT0_c)
        chk('attn0')
        phase_out(xq, units_q, mixT0, wout0, 2, x1)
        phase_out(ctxb, [(0, 2)], mixT0_c, wout0, 5, xc1)
        chk('out0')

        with ExitStack() as st:
            KT = sb(st, "KT", [128, T_q], BF16)
            VX = sb(st, "VX", [128, NQT, 2, 128], BF16)
            KTc = sb(st, "KTc", [128, L], BF16)
            VXc = sb(st, "VXc", [128, 2, 2, 128], BF16)
            g1bc = sb(st, "g1bc", [128, 128], F32)
            fw.dma('sp', [(g1bc[:], gains1[0:1, :].partition_broadcast(128))], writes=[g1bc])
            fw.op('dve', lambda v: v.tensor_scalar(out=g1bc[:, 0:64], in0=g1bc[:, 0:64], scalar1=64.0 ** -0.5,
                                                   scalar2=None, op0=ALU.mult), reads=[g1bc], writes=[g1bc])
            fw.op('pool', lambda g: g.memset(VXc[:, :, :, 64:128], 1.0), writes=[VXc])

            with ExitStack() as s2:
                tp = slots(s2, "tp", 2, [128, 1024], BF16, psum=True)
                pj = slots(s2, "pj", 2, [128, 1024], F32, psum=True)
                fm = slots(s2, "fm", 2, [128, 512], F32, psum=True)
                fr = Front(s2, 2, tp)
                stage = slots(s2, "wst", 2, [128, 1024], F32)
                wt = load_w_bf16(s2, "wt", lambda k: wt1[k * 128:(k + 1) * 128, :], 768, stage)
                wf = sb(s2, "wf", [128, KC, 2560], BF16)
                for k in range(KC):
                    for (c0, c1) in ((0, 1024), (1024, 2048), (2048, 2560)):
                        s_ = stage.next()
                        fw.dma('sp', [(s_[:, 0:c1 - c0], wf1[k * 128:(k + 1) * 128, c0:c1])], writes=[s_])
                        fw.op('pool', lambda g, k=k, s_=s_, c0=c0, c1=c1: g.tensor_copy(out=wf[:, k, c0:c1], in_=s_[:, 0:c1 - c0]),
                              reads=[s_], writes=[wf])
                bcs = {0: (load_bc(s2, "sc1t", 7), load_bc(s2, "sht", 6)), 1: (load_bc(s2, "sc1c", 10), load_bc(s2, "shc", 9))}
                tabs = slots(s2, "tab", 2, [128, 2, 192], F32)
                stp = slots(s2, "stp", 1, [128, 2, 768], F32)
                ssum = slots(s2, "ssum", 4, [128, 40], F32)
                rstd = slots(s2, "rstd", 4, [128, 40], F32)
                qkn = slots(s2, "qkn", 1, [128, 2, 10, 64], F32)
                qkb = slots(s2, "qkb", 1, [128, 2, 640], BF16)
                tmp1 = sb(s2, "tmp1", [128, 1280], F32)
                tmp2 = sb(s2, "tmp2", [128, 1280], F32)
                qst = slots(s2, "qst", 2, [128, 4, 256], BF16)
                ust = slots(s2, "ust", 2, [128, 4, 256], BF16)
                gcs = slots(s2, "gcs", 2, [128, 256], BF16)
                gbst = slots(s2, "gbst", 2, [128, 4, 256], BF16)
                sgst = slots(s2, "sgst", 2, [128, 8, 256], BF16)

                units1 = [(xc1, 0, 2, 1, None)] + [(x1, r0, J, 0, r0) for (r0, J) in units_q2]
                for (src, r0, J, who, rp0) in units1:
                    sc1, sh = bcs[who]
                    hT = fr.run(src, r0, J, sc1, sh)
                    tab = None
                    if rp0 is not None:
                        tab = tabs.next()
                        fw.dma('sp', [(tab[:, 0:J, :], ropeq[rp0:rp0 + J * 128, :].rearrange("(j p) c -> p j c", p=128))],
                               writes=[tab])
                    s_ = stp.next()
                    for j in range(J):
                        p = pj.next()
                        for (c0, c1) in ((0, 512), (512, 768)):
                            for k in range(KC):
                                fw.op('pe', lambda t, k=k, p=p, j=j, c0=c0, c1=c1: t.matmul(
                                    p[:, c0:c1], lhsT=hT[:, k, j * 128:(j + 1) * 128], rhs=wt[:, k, c0:c1],
                                    start=(k == 0), stop=(k == KC - 1)),
                                    reads=[hT, wt], writes=[p], inc=(c0 == 512 and k == KC - 1))
                        fw.op('act', lambda a, p=p, j=j: a.copy(out=s_[:, j, :], in_=p[:, 0:768]), reads=[p], writes=[s_])
                    qk = qkn.next()
                    fw.op('pool', lambda g: g.tensor_copy(out=qk[:, 0:J].rearrange("p j h c -> p j (h c)"), in_=s_[:, 0:J, 0:640]),
                          reads=[s_], writes=[qk])
                    sm = ssum.next()
                    rsd = rstd.next()
                    qkv = qk[:, 0:J, :, :].rearrange("p j h c -> p (j h) c")
                    sq2 = tmp1[:, 0:J * 640].rearrange("p (n c) -> p n c", c=64)
                    fw.op('dve', lambda v: v.tensor_tensor(out=sq2, in0=qkv, in1=qkv, op=ALU.mult), reads=[qk], writes=[tmp1])
                    fw.op('dve', lambda v: v.tensor_reduce(out=sm[:, 0:J * 10], in_=sq2, axis=AX.X, op=ALU.add),
                          reads=[tmp1], writes=[sm])
                    fw.op('act', lambda a: a.activation(out=sm[:, 0:J * 10], in_=sm[:, 0:J * 10], func=AF.Sqrt, bias=EPS,
                                                        scale=1.0 / 64), reads=[sm], writes=[sm])
                    fw.op('dve', lambda v: v.reciprocal(out=rsd[:, 0:J * 10], in_=sm[:, 0:J * 10]), reads=[sm], writes=[rsd])
                    fw.op('dve', lambda v: v.tensor_tensor(out=qkv, in0=qkv,
                                                           in1=rsd[:, 0:J * 10].unsqueeze(2).to_broadcast([128, J * 10, 64]),
                                                           op=ALU.mult), reads=[qk, rsd], writes=[qk])
                    fw.op('dve', lambda v: v.tensor_tensor(out=qk[:, 0:J, 0:8, :], in0=qk[:, 0:J, 0:8, :],
                                                           in1=g1bc[:, 0:64].unsqueeze(1).unsqueeze(1).to_broadcast([128, J, 8, 64]),
                                                           op=ALU.mult), reads=[qk, g1bc], writes=[qk])
                    fw.op('dve', lambda v: v.tensor_tensor(out=qk[:, 0:J, 8:10, :], in0=qk[:, 0:J, 8:10, :],
                                                           in1=g1bc[:, 64:128].unsqueeze(1).unsqueeze(1).to_broadcast([128, J, 2, 64]),
                                                           op=ALU.mult), reads=[qk, g1bc], writes=[qk])
                    if tab is not None:
                        rope(qk, J, 10, 64, 0, 64, tab, 64, tmp1, tmp2)
                    qb = qkb.next()
                    fw.op('pool', lambda g: g.tensor_copy(out=qb[:, 0:J, :], in_=qk[:, 0:J].rearrange("p j h c -> p j (h c)")),
                          reads=[qk], writes=[qb])
                    if who == 0:
                        tl0 = r0 // 128
                        for j in range(J):
                            tl = tl0 + j
                            fw.op('pool', lambda g, j=j, tl=tl: g.tensor_copy(
                                out=VX[:, tl, :, 0:64], in_=s_[:, j, 640:768].rearrange("p (g c) -> p g c", g=2)),
                                reads=[s_], writes=[VX])
                            fw.op('pool', lambda g, tl=tl: g.memset(VX[:, tl, :, 64:128], 1.0), reads=[], writes=[VX])
                            if tl == 0 or tl == NQT - 1:
                                vc = 0 if tl == 0 else 1
                                fw.op('dve', lambda v, tl=tl, vc=vc: v.tensor_scalar(
                                    out=VX[:, tl, :, :], in0=VX[:, tl, :, :], scalar1=validt[:, vc:vc + 1], scalar2=None,
                                    op0=ALU.mult), reads=[VX, validt], writes=[VX])
                    else:
                        for j in range(J):
                            fw.op('pool', lambda g, j=j: g.tensor_copy(
                                out=VXc[:, j, :, 0:64], in_=s_[:, j, 640:768].rearrange("p (g c) -> p g c", g=2)),
                                reads=[s_], writes=[VXc])
                    qs = qst.next() if who == 0 else None
                    for j in range(J):
                        p = tp.next()
                        for m in range(4):
                            fw.op('pe', lambda t, m=m, p=p, j=j: t.transpose(
                                out=p[0:64, m * 128:(m + 1) * 128], in_=qb[:, j, m * 64:(m + 1) * 64], identity=ident[:]),
                                reads=[qb, ident], writes=[p], inc=False)
                            fw.op('pe', lambda t, m=m, p=p, j=j: t.transpose(
                                out=p[64:128, m * 128:(m + 1) * 128], in_=qb[:, j, (m + 4) * 64:(m + 5) * 64], identity=ident[:]),
                                reads=[qb, ident], writes=[p], inc=False)
                        fw.op('pe', lambda t, p=p, j=j: t.transpose(out=p[:, 512:640], in_=qb[:, j, 512:640], identity=ident[:]),
                              reads=[qb, ident], writes=[p], inc=True)
                        if who == 0:
                            c0 = r0 + j * 128
                            fw.op('act', lambda a, p=p, j=j: a.copy(out=qs[:, :, j * 128:(j + 1) * 128],
                                                                   in_=p[:, 0:512].rearrange("p (m t) -> p m t", m=4)),
                                  reads=[p], writes=[qs])
                            fw.op('act', lambda a, p=p, c0=c0: a.copy(out=KT[:, c0:c0 + 128], in_=p[:, 512:640]),
                                  reads=[p], writes=[KT])
                        else:
                            fw.op('act', lambda a, p=p, j=j: a.copy(out=KTc[:, j * 128:(j + 1) * 128], in_=p[:, 512:640]),
                                  reads=[p], writes=[KTc])
                    if who == 1:
                        continue
                    fw.dma('pool', [(qT1[:, :, r0:r0 + J * 128].rearrange("m p t -> p m t"), qs[:, :, 0:J * 128])],
                           reads=[qs], accw=[qT1])
                    us = ust.next()
                    gb_ = gbst.next()
                    sg_ = sgst.next()
                    W_ = J * 128
                    for c in range(4):
                        p = fm.next()
                        for k in range(KC):
                            fw.op('pe', lambda t, k=k, p=p, c=c: t.matmul(p[:, 0:W_], lhsT=wf[:, k, c * 128:(c + 1) * 128],
                                                                         rhs=hT[:, k, 0:W_], start=(k == 0), stop=(k == KC - 1)),
                                  reads=[hT, wf], writes=[p], inc=(k == KC - 1))
                        fw.op('act', lambda a, p=p, c=c: a.copy(out=gb_[:, c, 0:W_], in_=p[:, 0:W_]), reads=[p], writes=[gb_])
                        p = fm.next()
                        for k in range(KC):
                            fw.op('pe', lambda t, k=k, p=p, c=c: t.matmul(p[:, 0:W_], lhsT=wf[:, k, 512 + c * 128:512 + (c + 1) * 128],
                                                                         rhs=hT[:, k, 0:W_], start=(k == 0), stop=(k == KC - 1)),
                                  reads=[hT, wf], writes=[p], inc=(k == KC - 1))
                        gc_ = gcs.next()
                        fw.op('act', lambda a, p=p, gc_=gc_: a.copy(out=gc_[:, 0:W_], in_=p[:, 0:W_]), reads=[p], writes=[gc_])
                        p = fm.next()
                        for k in range(KC):
                            fw.op('pe', lambda t, k=k, p=p, c=c: t.matmul(p[:, 0:W_], lhsT=wf[:, k, 1024 + c * 128:1024 + (c + 1) * 128],
                                                                         rhs=hT[:, k, 0:W_], start=(k == 0), stop=(k == KC - 1)),
                                  reads=[hT, wf], writes=[p], inc=(k == KC - 1))
                        fw.op('dve', lambda v, p=p, gc_=gc_, c=c: v.tensor_tensor(out=us[:, c, 0:W_], in0=p[:, 0:W_],
                                                                                 in1=gc_[:, 0:W_], op=ALU.mult),
                              reads=[p, gc_], writes=[us])
                    for c in range(8):
                        p = fm.next()
                        for k in range(KC):
                            fw.op('pe', lambda t, k=k, p=p, c=c: t.matmul(p[:, 0:W_], lhsT=wf[:, k, 1536 + c * 128:1536 + (c + 1) * 128],
                                                                         rhs=hT[:, k, 0:W_], start=(k == 0), stop=(k == KC - 1)),
                                  reads=[hT, wf], writes=[p], inc=(k == KC - 1))
                        fw.op('act', lambda a, p=p, c=c: a.activation(out=sg_[:, c, 0:W_], in_=p[:, 0:W_], func=AF.Silu),
                              reads=[p], writes=[sg_])
                    if r0 == 0:
                        fw.op('dve', lambda v: v.tensor_scalar(out=us[:, :, 0:128], in0=us[:, :, 0:128], scalar1=validt[:, 0:1],
                                                               scalar2=None, op0=ALU.mult), reads=[us, validt], writes=[us])
                    if r0 + W_ == T_q:
                        fw.op('dve', lambda v: v.tensor_scalar(out=us[:, :, W_ - 128:W_], in0=us[:, :, W_ - 128:W_],
                                                               scalar1=validt[:, 1:2], scalar2=None, op0=ALU.mult),
                              reads=[us, validt], writes=[us])
                    fw.dma('pool', [(uT[:, :, r0:r0 + W_].rearrange("c p t -> p c t"), us[:, :, 0:W_])], reads=[us], accw=[uT])
                    fw.dma('pool', [(gbT[:, :, r0:r0 + W_].rearrange("c p t -> p c t"), gb_[:, :, 0:W_])], reads=[gb_], accw=[gbT])
                    fw.dma('pool', [(sgT[:, :, r0:r0 + W_].rearrange("c p t -> p c t"), sg_[:, :, 0:W_])], reads=[sg_], accw=[sgT])
                fw.barrier()

            chk('d1')
            with ExitStack() as s3:
                pss = slots(s3, "pss", 2, [128, 2, 512], F32, psum=True)
                pso = slots(s3, "pso", 2, [128, 512], F32, psum=True)
                pj = slots(s3, "pj", 1, [128, 1024], F32, psum=True)
                stage = slots(s3, "wst", 2, [128, 1024], F32)
                wo = load_w_bf16(s3, "wo", lambda k: wout1[k * 128:(k + 1) * 128, :], 1024, stage)
                gate1 = load_bc(s3, "gate1", 8)
                cw = sb(s3, "cw", [128, 4, 3], F32)
                fw.dma('sp', [(cw[:], convw[:, :, :])], writes=[cw])
                esk = sb(s3, "esk", [128, 8], F32)
                fw.dma('sp', [(esk[:], sink[0:1, :].partition_broadcast(128))], writes=[esk])
                fw.op('act', lambda a: a.activation(out=esk[:], in_=esk[:], func=AF.Exp), reads=[esk], writes=[esk])
                esf = sb(s3, "esf", [128, 2, 4, 128], F32)
                fw.op('pool', lambda g: g.memset(esf[:], 0.0), writes=[esf])
                for g_ in range(2):
                    fw.op('dve', lambda v, g_=g_: v.tensor_tensor(
                        out=esf[:, g_, :, :], in0=esf[:, g_, :, :],
                        in1=esk[:, g_ * 4:(g_ + 1) * 4].unsqueeze(2).to_broadcast([128, 4, 128]), op=ALU.add),
                        reads=[esf, esk], writes=[esf])
                mprev = sb(s3, "mprev", [128, 128], BF16)
                mnext = sb(s3, "mnext", [128, 128], BF16)
                fw.op('pool', lambda g: g.memset(mprev[:], 1.0), writes=[mprev])
                fw.op('pool', lambda g: g.memset(mnext[:], 1.0), writes=[mnext])
                fw.op('pool', lambda g: g.affine_select(out=mprev[:], in_=mprev[:], pattern=[[-1, 128]], compare_op=ALU.is_ge,
                                                        fill=0.0, base=0, channel_multiplier=1), reads=[mprev], writes=[mprev])
                fw.op('pool', lambda g: g.affine_select(out=mnext[:], in_=mnext[:], pattern=[[1, 128]], compare_op=ALU.is_ge,
                                                        fill=0.0, base=0, channel_multiplier=-1), reads=[mnext], writes=[mnext])
                pT = slots(s3, "pT", 3, [128, 2, 512], BF16)
                Rs = slots(s3, "Rs", 2, [128, 512], F32)
                gbs = slots(s3, "gbs", 2, [128, 4, 512], BF16)
                sgs = slots(s3, "sgs", 2, [128, 8, 512], BF16)
                mxu = slots(s3, "mxu", 2, [128, 8, 512], BF16)
                acc = slots(s3, "acc", 2, [128, 512], F32)
                xts = slots(s3, "xts", 2, [128, 4, D], F32)
                t32 = slots(s3, "t32", 2, [128, D], F32)
                xo = slots(s3, "xo", 2, [128, 4, D], F32)
                qsu = slots(s3, "qsu", 2, [128, 4, 512], BF16)
                usu = slots(s3, "usu", 2, [128, 4, 514], BF16)
                for u in range(T_own // 512):
                    r0 = 128 + u * 512
                    gb_ = gbs.next()
                    sg_ = sgs.next()
                    fw.dma('sp', [(gb_[:], gbT[:, :, r0:r0 + 512].rearrange("c p t -> p c t"))], reads=[gbT], writes=[gb_])
                    fw.dma('sp', [(sg_[:], sgT[:, :, r0:r0 + 512].rearrange("c p t -> p c t"))], reads=[sgT], writes=[sg_])
                    xt = xts.next()
                    fw.dma('sp', [(xt[:], x1[r0:r0 + 512, :].rearrange("(j p) d -> p j d", p=128))], reads=[x1], writes=[xt])
                    mx = mxu.next()
                    qs = qsu.next()
                    us = usu.next()
                    fw.dma('sp', [(qs[:], qT1[:, :, r0:r0 + 512].rearrange("m p t -> p m t"))], reads=[qT1], writes=[qs])
                    fw.dma('sp', [(us[:], uT[:, :, r0 - 1:r0 + 513].rearrange("c p t -> p c t"))], reads=[uT], writes=[us])
                    for c in range(4):
                        a_ = acc.next()
                        fw.op('dve', lambda v, c=c, a_=a_: v.tensor_scalar(out=a_[:], in0=us[:, c, 0:512], scalar1=cw[:, c, 0:1],
                                                                          scalar2=None, op0=ALU.mult), reads=[us, cw], writes=[a_])
                        fw.op('dve', lambda v, c=c, a_=a_: v.scalar_tensor_tensor(out=a_[:], in0=us[:, c, 1:513],
                                                                                 scalar=cw[:, c, 1:2], in1=a_[:], op0=ALU.mult,
                                                                                 op1=ALU.add), reads=[us, cw, a_], writes=[a_])
                        fw.op('dve', lambda v, c=c, a_=a_: v.scalar_tensor_tensor(out=a_[:], in0=us[:, c, 2:514],
                                                                                 scalar=cw[:, c, 2:3], in1=a_[:], op0=ALU.mult,
                                                                                 op1=ALU.add), reads=[us, cw, a_], writes=[a_])
                        fw.op('pool', lambda g, c=c, a_=a_: g.tensor_tensor(out=a_[:], in0=a_[:], in1=gb_[:, c, :], op=ALU.mult),
                              reads=[a_, gb_], writes=[a_])
                        fw.op('pool', lambda g, c=c, a_=a_: g.tensor_tensor(out=mx[:, 4 + c, :], in0=a_[:], in1=sg_[:, 4 + c, :], op=ALU.mult),
                              reads=[a_, sg_], writes=[mx])
                    for jb in range(4):
                        tl = r0 // 128 + jb
                        c0 = tl * 128
                        for g_ in range(2):
                            pr = slice(g_ * 64, (g_ + 1) * 64)
                            po = pso.next()
                            batches = [[('c', 0), ('c', 1)], [('l', tl - 1), ('l', tl)], [('l', tl + 1)]]
                            nmm = 5
                            cnt = 0
                            for bt in batches:
                                p = pss.next()
                                for i, (kind, ti) in enumerate(bt):
                                    lhs = KTc[pr, ti * 128:(ti + 1) * 128] if kind == 'c' else KT[pr, ti * 128:(ti + 1) * 128]
                                    fw.op('pe', lambda t, i=i, p=p, lhs=lhs: t.matmul(
                                        p[:, i, :], lhsT=lhs, rhs=qs[pr, :, jb * 128:(jb + 1) * 128], start=True, stop=True),
                                        reads=[KTc, KT, qs], writes=[p], inc=(i == len(bt) - 1))
                                pt = pT.next()
                                n = len(bt)
                                fw.op('act', lambda a, p=p, pt=pt, n=n: a.activation(out=pt[:, 0:n, :], in_=p[:, 0:n, :], func=AF.Exp),
                                      reads=[p], writes=[pt])
                                for i, (kind, ti) in enumerate(bt):
                                    if kind == 'l' and ti != tl:
                                        msk = mprev if ti < tl else mnext
                                        fw.op('dve', lambda v, i=i, pt=pt, msk=msk: v.tensor_tensor(
                                            out=pt[:, i, :].rearrange("p (m t) -> p m t", m=4),
                                            in0=pt[:, i, :].rearrange("p (m t) -> p m t", m=4),
                                            in1=msk[:].unsqueeze(1).to_broadcast([128, 4, 128]), op=ALU.mult),
                                            reads=[pt, msk], writes=[pt])
                                for i, (kind, ti) in enumerate(bt):
                                    lhs = VXc[:, ti, g_, :] if kind == 'c' else VX[:, ti, g_, :]
                                    fw.op('pe', lambda t, i=i, pt=pt, lhs=lhs, cnt=cnt: t.matmul(
                                        po[:], lhsT=lhs, rhs=pt[:, i, :], start=(cnt == 0), stop=(cnt == nmm - 1)),
                                        reads=[VXc, VX, pt], writes=[po], inc=(i == len(bt) - 1))
                                    cnt += 1
                            R = Rs.next()
                            fw.op('dve', lambda v, R=R, po=po, g_=g_: v.tensor_tensor(
                                out=R[0:64, :], in0=po[64:128, :], in1=esf[64:128, g_, :, :].rearrange("p m t -> p (m t)"),
                                op=ALU.add), reads=[po, esf], writes=[R])
                            fw.op('dve', lambda v, R=R: v.reciprocal(out=R[0:64, :], in_=R[0:64, :]), reads=[R], writes=[R])
                            fw.op('dve', lambda v, R=R, po=po: v.tensor_tensor(out=R[0:64, :], in0=po[0:64, :], in1=R[0:64, :], op=ALU.mult),
                                  reads=[po, R], writes=[R])
                            Rv = R[0:64, :].rearrange("p (c two t) -> p c two t", c=2, two=2)
                            for hf in range(2):
                                fw.op('pool', lambda g, hf=hf, Rv=Rv, g_=g_, jb=jb: g.tensor_copy(
                                    out=mx[hf * 64:(hf + 1) * 64, 2 * g_:2 * g_ + 2, jb * 128:(jb + 1) * 128],
                                    in_=Rv[:, :, hf, :]), reads=[R], writes=[mx])
                    fw.op('dve', lambda v: v.tensor_tensor(out=mx[:, 0:4, :], in0=mx[:, 0:4, :], in1=sg_[:, 0:4, :], op=ALU.mult),
                          reads=[mx, sg_], writes=[mx])
                    xo_ = xo.next()
                    for j in range(4):
                        p = pj.next()
                        for n in range(2):
                            for k in range(KC):
                                fw.op('pe', lambda t, k=k, n=n, p=p, j=j: t.matmul(
                                    p[:, n * 512:(n + 1) * 512], lhsT=mx[:, k, j * 128:(j + 1) * 128],
                                    rhs=wo[:, k, n * 512:(n + 1) * 512], start=(k == 0), stop=(k == KC - 1)),
                                    reads=[mx, wo], writes=[p], inc=(n == 1 and k == KC - 1))
                        tt = t32.next()
                        fw.op('dve', lambda v, p=p, tt=tt: v.tensor_tensor(out=tt[:], in0=p[:], in1=gate1[:], op=ALU.mult),
                              reads=[p, gate1], writes=[tt])
                        fw.op('pool', lambda g, tt=tt, j=j: g.tensor_tensor(out=xo_[:, j, :], in0=tt[:], in1=xt[:, j, :], op=ALU.add),
                              reads=[tt, xt], writes=[xo_])
                    fw.dma('pool', [(out[u * 512:(u + 1) * 512, :].rearrange("(j p) d -> p j d", p=128), xo_[:])],
                           reads=[xo_], accw=[out])
                fw.barrier()
        fw.barrier()
        build.nins = dict(fw.nins)
    except _Stop:
        pass
    return nc


def _rope_tables(pos_row, pos_col):
    f32 = np.float32

    def cs(pos, half):
        inv = (f32(10000.0) ** (-(np.arange(half, dtype=f32)) / f32(half))).astype(f32)
        ang = (pos.astype(f32)[:, None] * inv[None, :]).astype(f32)
        return np.cos(ang).astype(f32), np.sin(ang).astype(f32)

    cr8, sr8 = cs(pos_row, 8)
    cc8, sc8 = cs(pos_col, 8)
    cr16, sr16 = cs(pos_row, 16)
    cc16, sc16 = cs(pos_col, 16)
    C32 = np.concatenate([cr8, cr8, cc8, cc8], 1)
    S32 = np.concatenate([-sr8, sr8, -sc8, sc8], 1)
    C64 = np.concatenate([cr16, cr16, cc16, cc16], 1)
    S64 = np.concatenate([-sr16, sr16, -sc16, sc16], 1)
    return np.ascontiguousarray(np.concatenate([C32, S32, C64, S64], 1).astype(f32))


_NC_CACHE = {}


def run(inputs, S, stop=None, dbg=None, dbg_out=()):
    f32 = np.float32
    x = np.asarray(inputs['x'], f32)
    B = x.shape[0]
    assert x.shape[1] == S and B == 2
    T_own = S // 4
    T_q = T_own + 256
    if (S, stop) not in _NC_CACHE:
        _NC_CACHE[(S, stop)] = build(S, stop, dbg_out)
    nc = _NC_CACHE[(S, stop)]
    c = np.asarray(inputs['c'], f32)
    ctx = np.asarray(inputs['ctx'], f32)
    c_ctx = np.asarray(inputs['c_ctx'], f32)
    w_in0 = np.asarray(inputs['ab_w_in'], f32)[0]
    w_in1 = np.asarray(inputs['cd_w_in'], f32)[0]
    wkv0 = np.ascontiguousarray(np.concatenate([w_in0[:, 256:512], w_in0[:, 1056:1184], w_in0[:, 1184:1312], w_in0[:, 512:544]], 1))
    wq0 = np.ascontiguousarray(np.concatenate([w_in0[:, 0:256], w_in0[:, 544:1056]], 1))
    wg0 = np.ascontiguousarray(w_in0[:, 1312:2336])
    wt1 = np.ascontiguousarray(w_in1[:, 0:768])
    wf1 = np.ascontiguousarray(w_in1[:, 768:3328])
    gains0 = np.concatenate([np.asarray(inputs['mla_q_gain'], f32)[0], np.asarray(inputs['mla_k_gain'], f32)[0],
                             np.asarray(inputs['gqa_q_gain'], f32)[0], np.asarray(inputs['gqa_k_gain'], f32)[0]])[None, :]
    gains1 = np.concatenate([np.asarray(inputs['win_q_gain'], f32)[0], np.asarray(inputs['win_k_gain'], f32)[0]])[None, :]
    convw = np.ascontiguousarray(np.asarray(inputs['conv_w'], f32)[0].reshape(3, 4, 128).transpose(2, 1, 0))
    pos = np.arange(S)
    ropek = _rope_tables(pos // GRID_W, pos % GRID_W)
    shared = {
        "mod_w": np.ascontiguousarray(np.asarray(inputs['mod_w'], f32)),
        "mod_b": np.ascontiguousarray(np.asarray(inputs['mod_b'], f32)),
        "wkv0": wkv0, "wq0": wq0, "wg0": wg0,
        "wout0": np.ascontiguousarray(np.asarray(inputs['ab_w_out'], f32)[0]),
        "w_uq": np.ascontiguousarray(np.asarray(inputs['mla_w_uq'], f32)[0]),
        "w_ukv": np.ascontiguousarray(np.asarray(inputs['mla_w_ukv'], f32)[0]),
        "cq_gain": np.ascontiguousarray(np.asarray(inputs['mla_cq_gain'], f32)[0].reshape(2, 128).T),
        "ckv_gain": np.ascontiguousarray(np.asarray(inputs['mla_ckv_gain'], f32)[0].reshape(2, 128).T),
        "gains0": np.ascontiguousarray(gains0),
        "wt1": wt1, "wf1": wf1,
        "wout1": np.ascontiguousarray(np.asarray(inputs['cd_w_out'], f32)[0]),
        "gains1": np.ascontiguousarray(gains1),
        "sink": np.ascontiguousarray(np.asarray(inputs['win_sink'], f32)[0][None, :]),
        "convw": convw,
        "ropek": ropek,
    }
    in_maps = []
    for core in range(NCORES):
        b, qc = core // 4, core % 4
        start = qc * T_own
        lo, hi = start - 128, start + T_own + 128
        xq = np.zeros((T_q, D), f32)
        a, e = max(lo, 0), min(hi, S)
        xq[a - lo:e - lo] = x[b, a:e]
        pq = np.clip(np.arange(lo, hi), 0, S - 1)
        vcol = np.array([1.0 if lo >= 0 else 0.0, 1.0 if hi <= S else 0.0], f32)
        cvec = np.stack([c[b], c_ctx], 1).reshape(KC, 128, 2).transpose(1, 0, 2)
        m = dict(shared)
        m.update({
            "xq": xq, "xkv": np.ascontiguousarray(x[b]), "ctxb": np.ascontiguousarray(ctx[b]),
            "cT": np.ascontiguousarray(cvec.astype(f32)),
            "ropeq": _rope_tables(pq // GRID_W, pq % GRID_W),
            "valid": np.ascontiguousarray(np.broadcast_to(vcol[None, :], (128, 2)).astype(f32)),
        })
        in_maps.append(m)
    res = run_bass_kernel_spmd(nc, in_maps, core_ids=list(range(NCORES)))
    if dbg is not None:
        dbg.append(res)
    outp = np.zeros((B, S, D), f32)
    for core in range(NCORES):
        b, qc = core // 4, core % 4
        outp[b, qc * T_own:(qc + 1) * T_own] = res.results[core]["out"]
    return outp


def kernel(**inputs):
    return run(inputs, 16384)
```

```python
import numpy as np
from contextlib import ExitStack
import concourse.bass as bass
import concourse.mybir as mybir
from concourse.bass_utils import run_bass_kernel_spmd

F32 = mybir.dt.float32
BF16 = mybir.dt.bfloat16
ALU = mybir.AluOpType
AF = mybir.ActivationFunctionType
AX = mybir.AxisListType

D = 1024
KC = 8
L = 256
EPS = 1e-6
GRID_W = 64
NCORES = 8


class Tn:
    def __init__(self, h, const=False, psum=False):
        self.h = h
        self.w = {}
        self.r = {}
        self.const = const
        self.psum = psum

    def __getitem__(self, k):
        return self.h[k]


class FW:
    def __init__(self, nc, es):
        self.nc = nc
        self.E = {'sp': nc.sync, 'act': nc.scalar, 'pool': nc.gpsimd, 'dve': nc.vector, 'pe': nc.tensor}
        self.sem = {}
        self.tot = {}
        self.seen = {e: {} for e in self.E}
        for e in self.E:
            self.sem[e] = es.enter_context(nc.semaphore('s_' + e))
            self.tot[e] = 0
        self.ring = {}
        self.rpos = {}
        for e, n in (('sp', 30), ('pool', 24), ('act', 8)):
            keys = []
            for i in range(n):
                k = 'd_%s%d' % (e, i)
                self.sem[k] = es.enter_context(nc.semaphore(k))
                self.tot[k] = 0
                keys.append(k)
            self.ring[e] = keys
            self.rpos[e] = 0
        self.nins = {e: 0 for e in self.E}

    def _wait(self, e, deps):
        for k, v in deps.items():
            if v <= 0:
                continue
            if k == e and e == 'pe':
                continue
            if self.seen[e].get(k, 0) >= v:
                continue
            assert v <= self.tot[k], "wait on unclosed group %s %d>%d (eng %s)" % (k, v, self.tot[k], e)
            self.E[e].wait_ge(self.sem[k], v)
            self.seen[e][k] = v

    @staticmethod
    def _deps(e, reads, writes, accw):
        d = {}
        for b in reads:
            for k, v in b.w.items():
                if d.get(k, 0) < v:
                    d[k] = v
            if b.psum:
                for k, v in b.r.items():
                    if k != e and d.get(k, 0) < v:
                        d[k] = v
        for b in writes:
            for k, v in b.w.items():
                if d.get(k, 0) < v:
                    d[k] = v
            for k, v in b.r.items():
                if d.get(k, 0) < v:
                    d[k] = v
        for b in accw:
            for k, v in b.r.items():
                if d.get(k, 0) < v:
                    d[k] = v
        return d

    def op(self, e, fn, reads=(), writes=(), inc=True):
        self._wait(e, self._deps(e, reads, writes, ()))
        ins = fn(self.E[e])
        self.nins[e] += 1
        val = self.tot[e] + 1
        if inc:
            ins.then_inc(self.sem[e], 1)
            self.tot[e] = val
        for b in reads:
            if not b.const and b.r.get(e, 0) < val:
                b.r[e] = val
        for b in writes:
            b.w = {e: val}
            b.r = {}
        return ins

    def dma(self, e, pairs, reads=(), writes=(), accw=()):
        k = self.ring[e][self.rpos[e]]
        self.rpos[e] = (self.rpos[e] + 1) % len(self.ring[e])
        deps = self._deps(e, reads, writes, accw)
        if deps.get(k, 0) < self.tot[k]:
            deps[k] = self.tot[k]
        self._wait(e, deps)
        val = self.tot[k] + 16 * len(pairs)
        for (o, i) in pairs:
            self.E[e].dma_start(out=o, in_=i).then_inc(self.sem[k], 16)
            self.nins[e] += 1
        self.tot[k] = val
        for b in reads:
            if not b.const and b.r.get(k, 0) < val:
                b.r[k] = val
        for b in writes:
            b.w = {k: val}
            b.r = {}
        for b in accw:
            if b.w.get(k, 0) < val:
                b.w[k] = val

    def barrier(self, engines=None):
        for e in (engines or self.E):
            self._wait(e, dict(self.tot))


class Slots:
    def __init__(self, items):
        self.items = items
        self.i = 0

    def next(self):
        t = self.items[self.i]
        self.i = (self.i + 1) % len(self.items)
        return t


class _Stop(Exception):
    pass


def build(S, stop=None, dbg_out=()):
    T_own = S // 4
    T_q = T_own + 256
    NQT = T_q // 128
    NK = L + S
    NKT = NK // 128

    nc = bass.Bass("TRN2", target_bir_lowering=False)

    def din(name, shape, dt=F32):
        return Tn(nc.dram_tensor(name, list(shape), dt, kind="ExternalInput").ap(), const=True)

    def dscr(name, shape, dt):
        if name in dbg_out:
            return Tn(nc.dram_tensor(name, list(shape), dt, kind="ExternalOutput").ap())
        return Tn(nc.dram_tensor(name, list(shape), dt).ap())

    xq = din("xq", [T_q, D])
    xkv = din("xkv", [S, D])
    ctxb = din("ctxb", [L, D])
    cT = din("cT", [128, KC, 2])
    mod_w = din("mod_w", [2, D, 3 * D])
    mod_b = din("mod_b", [2, 3 * D])
    wkv0 = din("wkv0", [D, 544])
    wq0 = din("wq0", [D, 768])
    wg0 = din("wg0", [D, 1024])
    wout0 = din("wout0", [D, D])
    w_uq = din("w_uq", [256, 768])
    w_ukv = din("w_ukv", [256, 1024])
    cq_gain = din("cq_gain", [128, 2])
    ckv_gain = din("ckv_gain", [128, 2])
    gains0 = din("gains0", [1, 96 + 96 + 64 + 64])
    wt1 = din("wt1", [D, 768])
    wf1 = din("wf1", [D, 2560])
    wout1 = din("wout1", [D, D])
    gains1 = din("gains1", [1, 128])
    sink = din("sink", [1, 8])
    convw = din("convw", [128, 4, 3])
    ropeq = din("ropeq", [T_q, 192])
    ropek = din("ropek", [S, 192])
    valid = din("valid", [128, 2])
    out = Tn(nc.dram_tensor("out", [T_own, D], F32, kind="ExternalOutput").ap())

    modbc = dscr("modbc", [12, 128, D], F32)
    kTa = dscr("kTa", [8, 96, NK], BF16)
    vA = dscr("vA", [8, 128, NKT, 64], BF16)
    kTb = dscr("kTb", [2, 64, NK], BF16)
    vB = dscr("vB", [2, 128, NKT, 64], BF16)
    qTa = dscr("qTa", [8, 96, T_q], BF16)
    qTb = dscr("qTb", [8, 64, T_q], BF16)
    gT0 = dscr("gT0", [8, 128, T_q], BF16)
    mixT0 = dscr("mixT0", [8, 128, T_q], BF16)
    x1 = dscr("x1", [T_q, D], F32)
    qTa_c = dscr("qTa_c", [8, 96, L], BF16)
    qTb_c = dscr("qTb_c", [8, 64, L], BF16)
    gT0_c = dscr("gT0_c", [8, 128, L], BF16)
    mixT0_c = dscr("mixT0_c", [8, 128, L], BF16)
    xc1 = dscr("xc1", [L, D], F32)
    gbT = dscr("gbT", [4, 128, T_q], BF16)
    sgT = dscr("sgT", [8, 128, T_q], BF16)
    qT1 = dscr("qT1", [4, 128, T_q], BF16)
    uT = dscr("uT", [4, 128, T_q], BF16)

    es = ExitStack()
    try:
      with es:
        fw = FW(nc, es)

        def chk(name):
            if stop == name:
                fw.barrier()
                build.nins = dict(fw.nins)
                raise _Stop()

        uid = [0]

        def sb(stk, name, shape, dt, const=False):
            uid[0] += 1
            return Tn(stk.enter_context(nc.sbuf_tensor("%s_%d" % (name, uid[0]), list(shape), dt)), const=const)

        def ps(stk, name, shape, dt=F32):
            uid[0] += 1
            return Tn(stk.enter_context(nc.psum_tensor("%s_%d" % (name, uid[0]), list(shape), dt)), psum=True)

        def slots(stk, name, n, shape, dt, psum=False):
            return Slots([(ps if psum else sb)(stk, "%s%d" % (name, i), shape, dt) for i in range(n)])

        ident = sb(es, "ident", [128, 128], BF16)
        fw.op('pool', lambda g: g.memset(ident[:], 0.0), writes=[ident])
        fw.op('pool', lambda g: g.affine_select(out=ident[:], in_=ident[:], pattern=[[-1, 128]],
                                                compare_op=ALU.not_equal, fill=1.0, base=0,
                                                channel_multiplier=1), reads=[ident], writes=[ident])
        ident.const = True
        validt = sb(es, "validt", [128, 2], F32)
        fw.dma('sp', [(validt[:], valid[:, :])], writes=[validt])
        validt.const = True

        with ExitStack() as st:
            cTt = sb(st, "cTt", [128, KC, 2], F32)
            scT = sb(st, "scT", [128, KC, 2], F32)
            mws = slots(st, "mws", 2, [128, KC, 512], F32)
            modsb = sb(st, "modsb", [2, 2, 3 * D], F32)
            modbias = sb(st, "modbias", [2, 2, 3 * D], F32)
            sel = sb(st, "sel", [2, 2, 128], F32)
            selw = sb(st, "selw", [2, 128], F32)
            pm = slots(st, "pm", 2, [2, 512], F32, psum=True)
            pb = slots(st, "pb", 2, [128, 1024], F32, psum=True)
            bcs = slots(st, "bcs", 2, [128, D], F32)

            fw.dma('sp', [(cTt[:], cT[:, :, :])], writes=[cTt])
            fw.op('act', lambda a: a.activation(out=scT[:], in_=cTt[:], func=AF.Silu), reads=[cTt], writes=[scT])
            for i in range(2):
                fw.dma('sp', [(modbias[0:1, i, :], mod_b[i:i + 1, :]), (modbias[1:2, i, :], mod_b[i:i + 1, :])],
                       accw=[modbias])
            fw.op('pool', lambda g: g.memset(sel[:], 0.0), writes=[sel])
            for who in range(2):
                fw.op('pool', lambda g, who=who: g.affine_select(
                    out=sel[:, who, :], in_=sel[:, who, :], pattern=[[0, 128]], compare_op=ALU.not_equal,
                    fill=1.0, base=-who, channel_multiplier=1), reads=[sel], writes=[sel])
            for i in range(2):
                for n in range(6):
                    mw = mws.next()
                    fw.dma('sp', [(mw[:], mod_w[i, :, n * 512:(n + 1) * 512].rearrange("(k p) n -> p k n", p=128))],
                           writes=[mw])
                    p = pm.next()
                    for k in range(KC):
                        fw.op('pe', lambda t, k=k, p=p, mw=mw: t.matmul(p[:], lhsT=scT[:, k, :], rhs=mw[:, k, :],
                                                                       start=(k == 0), stop=(k == KC - 1)),
                              reads=[scT, mw], writes=[p], inc=(k == KC - 1))
                    fw.op('dve', lambda v, p=p, i=i, n=n: v.tensor_tensor(
                        out=modsb[:, i, n * 512:(n + 1) * 512], in0=p[:], in1=modbias[:, i, n * 512:(n + 1) * 512],
                        op=ALU.add), reads=[p, modbias], writes=[modsb])
            for i in range(2):
                for who in range(2):
                    for which in range(3):
                        p = pb.next()
                        for n in range(2):
                            fw.op('pe', lambda t, p=p, n=n, i=i, who=who, which=which: t.matmul(
                                p[:, n * 512:(n + 1) * 512], lhsT=sel[:, who, :],
                                rhs=modsb[:, i, which * D + n * 512: which * D + (n + 1) * 512],
                                start=True, stop=True), reads=[sel, modsb], writes=[p], inc=(n == 1))
                        bc = bcs.next()
                        if which == 1:
                            fw.op('dve', lambda v, p=p, bc=bc: v.tensor_scalar(out=bc[:], in0=p[:], scalar1=1.0,
                                                                              scalar2=None, op0=ALU.add),
                                  reads=[p], writes=[bc])
                        else:
                            fw.op('dve', lambda v, p=p, bc=bc: v.tensor_copy(out=bc[:], in_=p[:]), reads=[p], writes=[bc])
                        fw.dma('pool', [(modbc[i * 6 + who * 3 + which, :, :], bc[:])], reads=[bc], accw=[modbc])
            fw.barrier()

        def load_w_bf16(stk, name, src_ap_fn, ncols, stage, gain=None, kchunks=KC):
            w = sb(stk, name, [128, kchunks, ncols], BF16)
            for k in range(kchunks):
                s_ = stage.next()
                fw.dma('sp', [(s_[:, 0:ncols], src_ap_fn(k))], writes=[s_])
                if gain is None:
                    fw.op('pool', lambda g, k=k, s_=s_: g.tensor_copy(out=w[:, k, :], in_=s_[:, 0:ncols]),
                          reads=[s_], writes=[w])
                else:
                    fw.op('dve', lambda v, k=k, s_=s_: v.tensor_scalar(out=w[:, k, :], in0=s_[:, 0:ncols],
                                                                      scalar1=gain[:, k:k + 1], scalar2=None,
                                                                      op0=ALU.mult), reads=[s_, gain], writes=[w])
            return w

        def load_bc(stk, name, idx):
            t = sb(stk, name, [128, D], F32)
            fw.dma('sp', [(t[:], modbc[idx, :, :])], reads=[modbc], writes=[t])
            return t

        class Front:
            def __init__(self, stk, Jmax, tp):
                self.xt = slots(stk, "f_xt", 2, [128, Jmax, D], F32)
                self.junk = sb(stk, "f_junk", [128, D], BF16)
                self.ss = slots(stk, "f_ss", 2, [128, Jmax], F32)
                self.rs = slots(stk, "f_rs", 2, [128, Jmax], F32)
                self.t32 = slots(stk, "f_t32", 1, [128, D], F32)
                self.hb = slots(stk, "f_hb", 2, [128, D], BF16)
                self.hT = slots(stk, "f_hT", 2, [128, KC, Jmax * 128], BF16)
                self.tp = tp

            def run(self, src, r0, J, sc1, sh):
                xt = self.xt.next()
                fw.dma('sp', [(xt[:, 0:J, :], src[r0:r0 + J * 128, :].rearrange("(j p) d -> p j d", p=128))],
                       reads=[src], writes=[xt])
                ss = self.ss.next()
                rs = self.rs.next()
                for j in range(J):
                    fw.op('act', lambda a, j=j: a.activation(out=self.junk[:], in_=xt[:, j, :], func=AF.Square,
                                                            accum_out=ss[:, j:j + 1]),
                          reads=[xt], writes=[self.junk, ss])
                fw.op('act', lambda a: a.activation(out=ss[:, 0:J], in_=ss[:, 0:J], func=AF.Sqrt, bias=EPS,
                                                    scale=1.0 / D), reads=[ss], writes=[ss])
                fw.op('dve', lambda v: v.reciprocal(out=rs[:, 0:J], in_=ss[:, 0:J]), reads=[ss], writes=[rs])
                hT = self.hT.next()
                for j in range(J):
                    t32 = self.t32.next()
                    hb = self.hb.next()
                    fw.op('dve', lambda v, j=j, t32=t32: v.scalar_tensor_tensor(
                        out=t32[:], in0=xt[:, j, :], scalar=rs[:, j:j + 1], in1=sc1[:], op0=ALU.mult, op1=ALU.mult),
                        reads=[xt, rs, sc1], writes=[t32])
                    fw.op('pool', lambda g, t32=t32, hb=hb: g.tensor_tensor(out=hb[:], in0=t32[:], in1=sh[:], op=ALU.add),
                          reads=[t32, sh], writes=[hb])
                    p = self.tp.next()
                    for k in range(KC):
                        fw.op('pe', lambda t, k=k, p=p, hb=hb: t.transpose(out=p[:, k * 128:(k + 1) * 128],
                                                                          in_=hb[:, k * 128:(k + 1) * 128],
                                                                          identity=ident[:]),
                              reads=[hb, ident], writes=[p], inc=(k == KC - 1))
                    fw.op('act', lambda a, j=j, p=p: a.copy(out=hT[:, :, j * 128:(j + 1) * 128],
                                                          in_=p[:].rearrange("p (k t) -> p k t", k=KC)),
                          reads=[p], writes=[hT])
                return hT

        def grp_rstd(src_ap, sq, ssum, rstd, n, Dh, rd):
            sqv = sq[:, 0:n * Dh].rearrange("p (n d) -> p n d", d=Dh)
            fw.op('dve', lambda v: v.tensor_tensor(out=sqv, in0=src_ap, in1=src_ap, op=ALU.mult),
                  reads=rd, writes=[sq])
            fw.op('dve', lambda v: v.tensor_reduce(out=ssum[:, 0:n], in_=sqv, axis=AX.X, op=ALU.add),
                  reads=[sq], writes=[ssum])
            fw.op('act', lambda a: a.activation(out=ssum[:, 0:n], in_=ssum[:, 0:n], func=AF.Sqrt, bias=EPS,
                                                scale=1.0 / Dh), reads=[ssum], writes=[ssum])
            fw.op('dve', lambda v: v.reciprocal(out=rstd[:, 0:n], in_=ssum[:, 0:n]), reads=[ssum], writes=[rstd])

        def rope(y, J, G, Dh, o, R, tab, cofs, tmp1, tmp2):
            q4 = R // 4
            yr = y[:, 0:J, :, o:o + R]
            C = tab[:, 0:J, cofs:cofs + R].unsqueeze(2).to_broadcast([128, J, G, R])
            t1 = tmp1[:, 0:J * G * R].rearrange("p (j g r) -> p j g r", j=J, g=G)
            t2 = tmp2[:, 0:J * G * R].rearrange("p (j g r) -> p j g r", j=J, g=G)
            fw.op('dve', lambda v: v.tensor_tensor(out=t1, in0=yr, in1=C, op=ALU.mult), reads=[y, tab], writes=[tmp1])
            for a in range(2):
                for hf in range(2):
                    dst = t2[:, :, :, a * 2 * q4 + hf * q4: a * 2 * q4 + (hf + 1) * q4]
                    srcv = y[:, 0:J, :, o + a * 2 * q4 + (1 - hf) * q4: o + a * 2 * q4 + (2 - hf) * q4]
                    sn = tab[:, 0:J, cofs + R + a * 2 * q4 + hf * q4: cofs + R + a * 2 * q4 + (hf + 1) * q4] \
                        .unsqueeze(2).to_broadcast([128, J, G, q4])
                    fw.op('dve', lambda v, dst=dst, srcv=srcv, sn=sn: v.tensor_tensor(out=dst, in0=srcv, in1=sn, op=ALU.mult),
                          reads=[y, tab], writes=[tmp2])
            fw.op('dve', lambda v: v.tensor_tensor(out=yr, in0=t1, in1=t2, op=ALU.add),
                  reads=[tmp1, tmp2], writes=[y])

        def phase_kv0():
            with ExitStack() as st:
                tp = slots(st, "tp", 2, [128, 1024], BF16, psum=True)
                pj = slots(st, "pj", 2, [128, 1024], F32, psum=True)
                fr = Front(st, 4, tp)
                stage = slots(st, "wst", 2, [128, 1024], F32)
                gck = sb(st, "gck", [128, 2], F32)
                fw.dma('sp', [(gck[:], ckv_gain[:, :])], writes=[gck])
                wkv = load_w_bf16(st, "wkv", lambda k: wkv0[k * 128:(k + 1) * 128, :], 544, stage)
                wuk = load_w_bf16(st, "wuk", lambda k: w_ukv[k * 128:(k + 1) * 128, :], 1024, stage, gain=gck, kchunks=2)
                gbc = sb(st, "gbc", [128, 320], F32)
                fw.dma('sp', [(gbc[:], gains0[0:1, :].partition_broadcast(128))], writes=[gbc])
                bc = {0: (load_bc(st, "sc1t", 1), load_bc(st, "sht", 0)), 1: (load_bc(st, "sc1c", 4), load_bc(st, "shc", 3))}
                tabs = slots(st, "tab", 2, [128, 4, 192], F32)
                stp = slots(st, "stp", 2, [128, 4, 544], F32)
                sq = sb(st, "sq", [128, 3072], F32)
                ssum = slots(st, "ssum", 4, [128, 32], F32)
                rstd = slots(st, "rstd", 4, [128, 32], F32)
                ckn = slots(st, "ckn", 2, [128, 4, 256], BF16)
                cTt = slots(st, "cTt", 2, [128, 2, 512], BF16)
                stK = slots(st, "stK", 2, [128, 4, 8, 96], F32)
                Kn = slots(st, "Kn", 1, [128, 4, 8, 96], BF16)
                Vst = slots(st, "Vst", 2, [128, 4, 8, 64], BF16)
                kTst = slots(st, "kTst", 2, [96, 8, 512], BF16)
                gkn = slots(st, "gkn", 1, [128, 4, 2, 64], F32)
                gkb = slots(st, "gkb", 2, [128, 4, 128], BF16)
                Vbst = slots(st, "Vbst", 2, [128, 4, 2, 64], BF16)
                kTbst = slots(st, "kTbst", 2, [128, 512], BF16)
                tmp1 = sq
                tmp2 = sb(st, "tmp2", [128, 1024], F32)

                units = [(ctxb, 0, 2, 1, None, 0)]
                for u in range(S // 512):
                    units.append((xkv, u * 512, 4, 0, u * 512, 2 + u * 4))
                for (src, r0, J, who, rp0, t0) in units:
                    sc1, sh = bc[who]
                    hT = fr.run(src, r0, J, sc1, sh)
                    chk('kv0_a')
                    tab = None
                    if rp0 is not None:
                        tab = tabs.next()
                        fw.dma('sp', [(tab[:, 0:J, :], ropek[rp0:rp0 + J * 128, :].rearrange("(j p) c -> p j c", p=128))],
                               writes=[tab])
                    s_ = stp.next()
                    for j in range(J):
                        p = pj.next()
                        for k in range(KC):
                            fw.op('pe', lambda t, k=k, p=p, j=j: t.matmul(p[:, 0:512], lhsT=hT[:, k, j * 128:(j + 1) * 128],
                                                                         rhs=wkv[:, k, 0:512], start=(k == 0),
                                                                         stop=(k == KC - 1)),
                                  reads=[hT, wkv], writes=[p], inc=False)
                        for k in range(KC):
                            fw.op('pe', lambda t, k=k, p=p, j=j: t.matmul(p[:, 512:544], lhsT=hT[:, k, j * 128:(j + 1) * 128],
                                                                         rhs=wkv[:, k, 512:544], start=(k == 0),
                                                                         stop=(k == KC - 1)),
                                  reads=[hT, wkv], writes=[p], inc=(k == KC - 1))
                        fw.op('act', lambda a, p=p, j=j: a.copy(out=s_[:, j, :], in_=p[:, 0:544]), reads=[p], writes=[s_])
                    sm = ssum.next()
                    rsd = rstd.next()
                    sqv = sq[:, 0:J * 256].rearrange("p (j c) -> p j c", j=J)
                    fw.op('dve', lambda v: v.tensor_tensor(out=sqv, in0=s_[:, 0:J, 0:256], in1=s_[:, 0:J, 0:256], op=ALU.mult),
                          reads=[s_], writes=[sq])
                    fw.op('dve', lambda v: v.tensor_reduce(out=sm[:, 0:J], in_=sqv, axis=AX.X, op=ALU.add),
                          reads=[sq], writes=[sm])
                    fw.op('act', lambda a: a.activation(out=sm[:, 0:J], in_=sm[:, 0:J], func=AF.Sqrt, bias=EPS,
                                                        scale=1.0 / 256), reads=[sm], writes=[sm])
                    fw.op('dve', lambda v: v.reciprocal(out=rsd[:, 0:J], in_=sm[:, 0:J]), reads=[sm], writes=[rsd])
                    cn = ckn.next()
                    fw.op('dve', lambda v: v.tensor_tensor(out=cn[:, 0:J, :], in0=s_[:, 0:J, 0:256],
                                                           in1=rsd[:, 0:J].unsqueeze(2).to_broadcast([128, J, 256]),
                                                           op=ALU.mult), reads=[s_, rsd], writes=[cn])
                    chk('kv0_b')
                    ct = cTt.next()
                    for j in range(J):
                        p = tp.next()
                        for k2 in range(2):
                            fw.op('pe', lambda t, k2=k2, p=p, j=j: t.transpose(out=p[:, k2 * 128:(k2 + 1) * 128],
                                                                              in_=cn[:, j, k2 * 128:(k2 + 1) * 128],
                                                                              identity=ident[:]),
                                  reads=[cn, ident], writes=[p], inc=(k2 == 1))
                        fw.op('act', lambda a, p=p, j=j: a.copy(out=ct[:, :, j * 128:(j + 1) * 128],
                                                              in_=p[:, 0:256].rearrange("p (k t) -> p k t", k=2)),
                              reads=[p], writes=[ct])
                    sk = stK.next()
                    vs = Vst.next()
                    for j in range(J):
                        p = pj.next()
                        for n in range(2):
                            for k2 in range(2):
                                fw.op('pe', lambda t, k2=k2, n=n, p=p, j=j: t.matmul(
                                    p[:, n * 512:(n + 1) * 512], lhsT=ct[:, k2, j * 128:(j + 1) * 128],
                                    rhs=wuk[:, k2, n * 512:(n + 1) * 512], start=(k2 == 0), stop=(k2 == 1)),
                                    reads=[ct, wuk], writes=[p], inc=(n == 1 and k2 == 1))
                        pv_ = p[:].rearrange("p (h c) -> p h c", h=8)
                        fw.op('act', lambda a, j=j, pv_=pv_: a.copy(out=sk[:, j, :, 0:64], in_=pv_[:, :, 0:64]),
                              reads=[p], writes=[sk])
                        fw.op('dve', lambda v, j=j, pv_=pv_: v.tensor_copy(out=vs[:, j, :, :], in_=pv_[:, :, 64:128]),
                              reads=[p], writes=[vs])
                    fw.op('pool', lambda g: g.tensor_copy(out=sk[:, 0:J, :, 64:96],
                                                          in_=s_[:, 0:J, 512:544].unsqueeze(2).to_broadcast([128, J, 8, 32])),
                          reads=[s_], writes=[sk])
                    chk('kv0_c')
                    sm = ssum.next()
                    rsd = rstd.next()
                    skv = sk[:, 0:J, :, :].rearrange("p j h c -> p (j h) c")
                    grp_rstd(skv, sq, sm, rsd, J * 8, 96, [sk])
                    fw.op('dve', lambda v: v.tensor_tensor(out=skv, in0=skv,
                                                           in1=rsd[:, 0:J * 8].unsqueeze(2).to_broadcast([128, J * 8, 96]),
                                                           op=ALU.mult), reads=[sk, rsd], writes=[sk])
                    fw.op('dve', lambda v: v.tensor_tensor(out=skv, in0=skv,
                                                           in1=gbc[:, 96:192].unsqueeze(1).to_broadcast([128, J * 8, 96]),
                                                           op=ALU.mult), reads=[sk, gbc], writes=[sk])
                    if tab is not None:
                        rope(sk, J, 8, 96, 64, 32, tab, 0, tmp1, tmp2)
                    kn = Kn.next()
                    fw.op('pool', lambda g: g.tensor_copy(out=kn[:, 0:J], in_=sk[:, 0:J]), reads=[sk], writes=[kn])
                    chk('kv0_d')
                    kt = kTst.next()
                    for j in range(J):
                        p = tp.next()
                        for h in range(8):
                            fw.op('pe', lambda t, h=h, p=p, j=j: t.transpose(out=p[0:96, h * 128:(h + 1) * 128],
                                                                            in_=kn[:, j, h, :], identity=ident[:]),
                                  reads=[kn, ident], writes=[p], inc=(h == 7))
                        fw.op('act', lambda a, p=p, j=j: a.copy(out=kt[:, :, j * 128:(j + 1) * 128],
                                                              in_=p[0:96, :].rearrange("p (h t) -> p h t", h=8)),
                              reads=[p], writes=[kt])
                    k0 = t0 * 128
                    fw.dma('pool', [(kTa[:, :, k0:k0 + J * 128].rearrange("h d t -> d h t"), kt[:, :, 0:J * 128])],
                           reads=[kt], accw=[kTa])
                    chk('kv0_e')
                    fw.dma('pool', [(vA[h, :, t0:t0 + J, :], vs[:, 0:J, h, :]) for h in range(8)], reads=[vs], accw=[vA])
                    chk('kv0_f')
                    gk = gkn.next()
                    fw.op('pool', lambda g: g.tensor_copy(out=gk[:, 0:J].rearrange("p j g c -> p j (g c)"),
                                                          in_=s_[:, 0:J, 256:384]), reads=[s_], writes=[gk])
                    vb = Vbst.next()
                    fw.op('pool', lambda g: g.tensor_copy(out=vb[:, 0:J].rearrange("p j g c -> p j (g c)"),
                                                          in_=s_[:, 0:J, 384:512]), reads=[s_], writes=[vb])
                    sm = ssum.next()
                    rsd = rstd.next()
                    gkv = gk[:, 0:J, :, :].rearrange("p j g c -> p (j g) c")
                    grp_rstd(gkv, sq, sm, rsd, J * 2, 64, [gk])
                    fw.op('dve', lambda v: v.tensor_tensor(out=gkv, in0=gkv,
                                                           in1=rsd[:, 0:J * 2].unsqueeze(2).to_broadcast([128, J * 2, 64]),
                                                           op=ALU.mult), reads=[gk, rsd], writes=[gk])
                    fw.op('dve', lambda v: v.tensor_tensor(out=gkv, in0=gkv,
                                                           in1=gbc[:, 256:320].unsqueeze(1).to_broadcast([128, J * 2, 64]),
                                                           op=ALU.mult), reads=[gk, gbc], writes=[gk])
                    if tab is not None:
                        rope(gk, J, 2, 64, 0, 64, tab, 64, tmp1, tmp2)
                    gb_ = gkb.next()
                    fw.op('pool', lambda g: g.tensor_copy(out=gb_[:, 0:J, :], in_=gk[:, 0:J].rearrange("p j g c -> p j (g c)")),
                          reads=[gk], writes=[gb_])
                    ktb = kTbst.next()
                    p = tp.next()
                    for j in range(J):
                        fw.op('pe', lambda t, p=p, j=j: t.transpose(out=p[:, j * 128:(j + 1) * 128], in_=gb_[:, j, :],
                                                                   identity=ident[:]),
                              reads=[gb_, ident], writes=[p], inc=(j == J - 1))
                    fw.op('act', lambda a, p=p: a.copy(out=ktb[:, 0:J * 128], in_=p[:, 0:J * 128]), reads=[p], writes=[ktb])
                    fw.dma('pool', [(kTb[g, :, k0:k0 + J * 128], ktb[g * 64:(g + 1) * 64, 0:J * 128]) for g in range(2)],
                           reads=[ktb], accw=[kTb])
                    fw.dma('pool', [(vB[g, :, t0:t0 + J, :], vb[:, 0:J, g, :]) for g in range(2)], reads=[vb], accw=[vB])
                fw.barrier()

        def phase_q0(src, units, who, rtab, qTa_d, qTb_d, gT_d):
            with ExitStack() as st:
                tp = slots(st, "tp", 2, [128, 1024], BF16, psum=True)
                pj = slots(st, "pj", 2, [128, 1024], F32, psum=True)
                fm = slots(st, "fm", 2, [128, 512], F32, psum=True)
                fr = Front(st, 2, tp)
                stage = slots(st, "wst", 2, [128, 1024], F32)
                gcq = sb(st, "gcq", [128, 2], F32)
                fw.dma('sp', [(gcq[:], cq_gain[:, :])], writes=[gcq])
                wq = load_w_bf16(st, "wq", lambda k: wq0[k * 128:(k + 1) * 128, :], 768, stage)
                wg = load_w_bf16(st, "wg", lambda k: wg0[k * 128:(k + 1) * 128, :], 1024, stage)
                wuq = load_w_bf16(st, "wuq", lambda k: w_uq[k * 128:(k + 1) * 128, :], 768, stage, gain=gcq, kchunks=2)
                gbc = sb(st, "gbc", [128, 320], F32)
                fw.dma('sp', [(gbc[:], gains0[0:1, :].partition_broadcast(128))], writes=[gbc])
                fw.op('dve', lambda v: v.tensor_scalar(out=gbc[:, 0:96], in0=gbc[:, 0:96], scalar1=96.0 ** -0.5,
                                                       scalar2=None, op0=ALU.mult), reads=[gbc], writes=[gbc])
                fw.op('dve', lambda v: v.tensor_scalar(out=gbc[:, 192:256], in0=gbc[:, 192:256], scalar1=64.0 ** -0.5,
                                                       scalar2=None, op0=ALU.mult), reads=[gbc], writes=[gbc])
                sc1 = load_bc(st, "sc1", who * 3 + 1)
                sh = load_bc(st, "sh", who * 3 + 0)
                tabs = slots(st, "tab", 2, [128, 2, 192], F32)
                stp = slots(st, "stp", 2, [128, 2, 768], F32)
                sq = sb(st, "sq", [128, 1536], F32)
                ssum = slots(st, "ssum", 4, [128, 32], F32)
                rstd = slots(st, "rstd", 4, [128, 32], F32)
                cqn = slots(st, "cqn", 2, [128, 2, 256], BF16)
                cTt = slots(st, "cTt", 2, [128, 2, 256], BF16)
                stQ = slots(st, "stQ", 2, [128, 2, 8, 96], F32)
                Qn = slots(st, "Qn", 2, [128, 2, 8, 96], BF16)
                qTst = slots(st, "qTst", 2, [96, 8, 256], BF16)
                gqn = slots(st, "gqn", 2, [128, 2, 8, 64], F32)
                gqb = slots(st, "gqb", 2, [128, 2, 512], BF16)
                qTbst = slots(st, "qTbst", 2, [128, 4, 256], BF16)
                gTst = slots(st, "gTst", 2, [128, 8, 256], BF16)
                tmp1 = sq
                tmp2 = sb(st, "tmp2", [128, 1024], F32)
                for (r0, J) in units:
                    hT = fr.run(src, r0, J, sc1, sh)
                    tab = None
                    if rtab is not None:
                        tab = tabs.next()
                        fw.dma('sp', [(tab[:, 0:J, :], rtab[r0:r0 + J * 128, :].rearrange("(j p) c -> p j c", p=128))],
                               writes=[tab])
                    s_ = stp.next()
                    for j in range(J):
                        p = pj.next()
                        for (c0, c1) in ((0, 512), (512, 768)):
                            for k in range(KC):
                                fw.op('pe', lambda t, k=k, p=p, j=j, c0=c0, c1=c1: t.matmul(
                                    p[:, c0:c1], lhsT=hT[:, k, j * 128:(j + 1) * 128], rhs=wq[:, k, c0:c1],
                                    start=(k == 0), stop=(k == KC - 1)),
                                    reads=[hT, wq], writes=[p], inc=(c0 == 512 and k == KC - 1))
                        fw.op('act', lambda a, p=p, j=j: a.copy(out=s_[:, j, :], in_=p[:, 0:768]), reads=[p], writes=[s_])
                    gt = gTst.next()
                    for c in range(8):
                        p = fm.next()
                        for k in range(KC):
                            fw.op('pe', lambda t, k=k, p=p, c=c: t.matmul(p[:, 0:J * 128], lhsT=wg[:, k, c * 128:(c + 1) * 128],
                                                                         rhs=hT[:, k, 0:J * 128], start=(k == 0),
                                                                         stop=(k == KC - 1)),
                                  reads=[hT, wg], writes=[p], inc=(k == KC - 1))
                        fw.op('act', lambda a, p=p, c=c: a.activation(out=gt[:, c, 0:J * 128], in_=p[:, 0:J * 128], func=AF.Silu),
                              reads=[p], writes=[gt])
                    fw.dma('pool', [(gT_d[:, :, r0:r0 + J * 128].rearrange("c p t -> p c t"), gt[:, :, 0:J * 128])],
                           reads=[gt], accw=[gT_d])
                    sm = ssum.next()
                    rsd = rstd.next()
                    sqv = sq[:, 0:J * 256].rearrange("p (j c) -> p j c", j=J)
                    fw.op('dve', lambda v: v.tensor_tensor(out=sqv, in0=s_[:, 0:J, 0:256], in1=s_[:, 0:J, 0:256], op=ALU.mult),
                          reads=[s_], writes=[sq])
                    fw.op('dve', lambda v: v.tensor_reduce(out=sm[:, 0:J], in_=sqv, axis=AX.X, op=ALU.add),
                          reads=[sq], writes=[sm])
                    fw.op('act', lambda a: a.activation(out=sm[:, 0:J], in_=sm[:, 0:J], func=AF.Sqrt, bias=EPS,
                                                        scale=1.0 / 256), reads=[sm], writes=[sm])
                    fw.op('dve', lambda v: v.reciprocal(out=rsd[:, 0:J], in_=sm[:, 0:J]), reads=[sm], writes=[rsd])
                    cn = cqn.next()
                    fw.op('dve', lambda v: v.tensor_tensor(out=cn[:, 0:J, :], in0=s_[:, 0:J, 0:256],
                                                           in1=rsd[:, 0:J].unsqueeze(2).to_broadcast([128, J, 256]),
                                                           op=ALU.mult), reads=[s_, rsd], writes=[cn])
                    ct = cTt.next()
                    for j in range(J):
                        p = tp.next()
                        for k2 in range(2):
                            fw.op('pe', lambda t, k2=k2, p=p, j=j: t.transpose(out=p[:, k2 * 128:(k2 + 1) * 128],
                                                                              in_=cn[:, j, k2 * 128:(k2 + 1) * 128],
                                                                              identity=ident[:]),
                                  reads=[cn, ident], writes=[p], inc=(k2 == 1))
                        fw.op('act', lambda a, p=p, j=j: a.copy(out=ct[:, :, j * 128:(j + 1) * 128],
                                                              in_=p[:, 0:256].rearrange("p (k t) -> p k t", k=2)),
                              reads=[p], writes=[ct])
                    sQ = stQ.next()
                    for j in range(J):
                        p = pj.next()
                        for n in range(2):
                            for k2 in range(2):
                                fw.op('pe', lambda t, k2=k2, n=n, p=p, j=j: t.matmul(
                                    p[:, n * 512:n * 512 + 384], lhsT=ct[:, k2, j * 128:(j + 1) * 128],
                                    rhs=wuq[:, k2, n * 384:(n + 1) * 384], start=(k2 == 0), stop=(k2 == 1)),
                                    reads=[ct, wuq], writes=[p], inc=(n == 1 and k2 == 1))
                        fw.op('act', lambda a, j=j, p=p: a.copy(
                            out=sQ[:, j, :, :].rearrange("p (a h) c -> p a (h c)", a=2),
                            in_=p[:].rearrange("p (a n) -> p a n", a=2)[:, :, 0:384]), reads=[p], writes=[sQ])
                    sm = ssum.next()
                    rsd = rstd.next()
                    sQv = sQ[:, 0:J, :, :].rearrange("p j h c -> p (j h) c")
                    grp_rstd(sQv, sq, sm, rsd, J * 8, 96, [sQ])
                    fw.op('dve', lambda v: v.tensor_tensor(out=sQv, in0=sQv,
                                                           in1=rsd[:, 0:J * 8].unsqueeze(2).to_broadcast([128, J * 8, 96]),
                                                           op=ALU.mult), reads=[sQ, rsd], writes=[sQ])
                    fw.op('dve', lambda v: v.tensor_tensor(out=sQv, in0=sQv,
                                                           in1=gbc[:, 0:96].unsqueeze(1).to_broadcast([128, J * 8, 96]),
                                                           op=ALU.mult), reads=[sQ, gbc], writes=[sQ])
                    if tab is not None:
                        rope(sQ, J, 8, 96, 64, 32, tab, 0, tmp1, tmp2)
                    qn = Qn.next()
                    fw.op('pool', lambda g: g.tensor_copy(out=qn[:, 0:J], in_=sQ[:, 0:J]), reads=[sQ], writes=[qn])
                    qt = qTst.next()
                    for j in range(J):
                        p = tp.next()
                        for h in range(8):
                            fw.op('pe', lambda t, h=h, p=p, j=j: t.transpose(out=p[0:96, h * 128:(h + 1) * 128],
                                                                            in_=qn[:, j, h, :], identity=ident[:]),
                                  reads=[qn, ident], writes=[p], inc=(h == 7))
                        fw.op('act', lambda a, p=p, j=j: a.copy(out=qt[:, :, j * 128:(j + 1) * 128],
                                                              in_=p[0:96, :].rearrange("p (h t) -> p h t", h=8)),
                              reads=[p], writes=[qt])
                    fw.dma('pool', [(qTa_d[:, :, r0:r0 + J * 128].rearrange("h d t -> d h t"), qt[:, :, 0:J * 128])],
                           reads=[qt], accw=[qTa_d])
                    gq = gqn.next()
                    fw.op('pool', lambda g: g.tensor_copy(out=gq[:, 0:J].rearrange("p j h c -> p j (h c)"),
                                                          in_=s_[:, 0:J, 256:768]), reads=[s_], writes=[gq])
                    sm = ssum.next()
                    rsd = rstd.next()
                    gqv = gq[:, 0:J, :, :].rearrange("p j h c -> p (j h) c")
                    grp_rstd(gqv, sq, sm, rsd, J * 8, 64, [gq])
                    fw.op('dve', lambda v: v.tensor_tensor(out=gqv, in0=gqv,
                                                           in1=rsd[:, 0:J * 8].unsqueeze(2).to_broadcast([128, J * 8, 64]),
                                                           op=ALU.mult), reads=[gq, rsd], writes=[gq])
                    fw.op('dve', lambda v: v.tensor_tensor(out=gqv, in0=gqv,
                                                           in1=gbc[:, 192:256].unsqueeze(1).to_broadcast([128, J * 8, 64]),
                                                           op=ALU.mult), reads=[gq, gbc], writes=[gq])
                    if tab is not None:
                        rope(gq, J, 8, 64, 0, 64, tab, 64, tmp1, tmp2)
                    gb_ = gqb.next()
                    fw.op('pool', lambda g: g.tensor_copy(out=gb_[:, 0:J, :], in_=gq[:, 0:J].rearrange("p j h c -> p j (h c)")),
                          reads=[gq], writes=[gb_])
                    qtb = qTbst.next()
                    for j in range(J):
                        p = tp.next()
                        for m in range(4):
                            fw.op('pe', lambda t, m=m, p=p, j=j: t.transpose(out=p[:, m * 128:(m + 1) * 128],
                                                                            in_=gb_[:, j, m * 128:(m + 1) * 128],
                                                                            identity=ident[:]),
                                  reads=[gb_, ident], writes=[p], inc=(m == 3))
                        fw.op('act', lambda a, p=p, j=j: a.copy(out=qtb[:, :, j * 128:(j + 1) * 128],
                                                              in_=p[:, 0:512].rearrange("p (m t) -> p m t", m=4)),
                              reads=[p], writes=[qtb])
                    fw.dma('pool', [(qTb_d[:, :, r0:r0 + J * 128].rearrange("(m two) d t -> (two d) m t", two=2),
                                     qtb[:, :, 0:J * 128])], reads=[qtb], accw=[qTb_d])
                fw.barrier()

        def phase_attn0(T, kt0, kt1, qTa_d, qTb_d, gT_d, mix_d):
            nkt = kt1 - kt0
            with ExitStack() as st:
                pss = slots(st, "pss", 2, [128, 3, 512], F32, psum=True)
                pso = slots(st, "pso", 2, [128, 512], F32, psum=True)
                kT = sb(st, "kT", [128, nkt * 128], BF16)
                Ve = sb(st, "Ve", [128, nkt, 128], BF16)
                Vo = sb(st, "Vo", [128, nkt, 128], BF16)
                qTs = slots(st, "qTs", 2, [128, T], BF16)
                gTs = slots(st, "gTs", 2, [128, T], BF16)
                mxs = slots(st, "mxs", 2, [128, T], BF16)
                pT = slots(st, "pT", 3, [128, 3, 512], BF16)
                Rs = slots(st, "Rs", 2, [128, 512], F32)
                t32 = slots(st, "t32", 2, [128, 512], F32)
                fw.op('pool', lambda g: g.memset(Ve[:, :, 64:128], 1.0), writes=[Ve])
                fw.op('pool', lambda g: g.memset(Vo[:, :, 0:64], 1.0), writes=[Vo])
                qsup = []
                q0 = 0
                while q0 < T:
                    wq_ = min(512, T - q0)
                    qsup.append((q0, wq_))
                    q0 += wq_
                for c in range(8):
                    gt = gTs.next()
                    fw.dma('sp', [(gt[:], gT_d[c, :, :])], reads=[gT_d], writes=[gt])
                    mx = mxs.next()
                    for half in range(2):
                        hh = 2 * c + half
                        if c < 4:
                            d = 96
                            ksrc, vsrc, qsrc = kTa[hh], vA[hh], qTa_d[hh]
                            newk = True
                            newv = True
                        else:
                            d = 64
                            qh = hh - 8
                            g_ = qh // 4
                            ksrc, vsrc, qsrc = kTb[g_], vB[g_], qTb_d[qh]
                            newk = (qh % 4 == 0)
                            newv = (qh % 4 < 2)
                        V = Ve if half == 0 else Vo
                        vcol = 0 if half == 0 else 64
                        if newk:
                            nsp = 4 if nkt >= 8 else 1
                            stp_ = (nkt * 128) // nsp
                            for i in range(nsp):
                                fw.dma('sp', [(kT[0:d, i * stp_:(i + 1) * stp_], ksrc[:, kt0 * 128 + i * stp_: kt0 * 128 + (i + 1) * stp_])],
                                       reads=[kTa, kTb], writes=[kT] if i == 0 else (), accw=() if i == 0 else [kT])
                        if newv:
                            fw.dma('sp', [(V[:, :, vcol:vcol + 64], vsrc[:, kt0:kt1, :])], reads=[vA, vB], writes=[V])
                        qT = qTs.next()
                        fw.dma('sp', [(qT[0:d, :], qsrc[:, :])], reads=[qTa_d, qTb_d], writes=[qT])
                        for (q0, wq_) in qsup:
                            po = pso.next()
                            pairs = [(i, min(3, nkt - i)) for i in range(0, nkt, 3)]

                            def qk(pi):
                                i0, n = pairs[pi]
                                p = pss.next()
                                for i in range(n):
                                    fw.op('pe', lambda t, i=i, p=p: t.matmul(
                                        p[:, i, 0:wq_], lhsT=kT[0:d, (i0 + i) * 128:(i0 + i + 1) * 128],
                                        rhs=qT[0:d, q0:q0 + wq_], start=True, stop=True),
                                        reads=[kT, qT], writes=[p], inc=(i == n - 1))
                                return p

                            pend = qk(0)
                            for pi in range(len(pairs)):
                                i0, n = pairs[pi]
                                p = pend
                                pt = pT.next()
                                fw.op('act', lambda a, p=p, pt=pt, n=n: a.activation(out=pt[:, 0:n, 0:wq_], in_=p[:, 0:n, 0:wq_],
                                                                                    func=AF.Exp), reads=[p], writes=[pt])
                                if pi + 1 < len(pairs):
                                    pend = qk(pi + 1)
                                for i in range(n):
                                    kt_ = i0 + i
                                    fw.op('pe', lambda t, i=i, pt=pt, kt_=kt_: t.matmul(
                                        po[:, 0:wq_], lhsT=V[:, kt_, :], rhs=pt[:, i, 0:wq_],
                                        start=(kt_ == 0), stop=(kt_ == nkt - 1)),
                                        reads=[V, pt], writes=[po], inc=(i == n - 1))
                            R = Rs.next()
                            tt = t32.next()
                            o0, s0 = (0, 64) if half == 0 else (64, 0)
                            fw.op('dve', lambda v, R=R: v.reciprocal(out=R[o0:o0 + 64, 0:wq_], in_=po[s0:s0 + 64, 0:wq_]),
                                  reads=[po], writes=[R])
                            fw.op('dve', lambda v, R=R, tt=tt: v.tensor_tensor(out=tt[o0:o0 + 64, 0:wq_], in0=po[o0:o0 + 64, 0:wq_],
                                                                               in1=R[o0:o0 + 64, 0:wq_], op=ALU.mult),
                                  reads=[po, R], writes=[tt])
                            fw.op('pool', lambda g, tt=tt: g.tensor_tensor(out=mx[o0:o0 + 64, q0:q0 + wq_],
                                                                          in0=tt[o0:o0 + 64, 0:wq_],
                                                                          in1=gt[o0:o0 + 64, q0:q0 + wq_], op=ALU.mult),
                                  reads=[tt, gt], writes=[mx])
                    fw.dma('pool', [(mix_d[c, :, :], mx[:])], reads=[mx], accw=[mix_d])
                fw.barrier()

        def phase_out(src, units, mix_d, wout_d, gate_idx, dst, dst_r0=None, st_outer=None):
            with ExitStack() as st:
                pj = slots(st, "pj", 2, [128, 1024], F32, psum=True)
                stage = slots(st, "wst", 2, [128, 1024], F32)
                wo = load_w_bf16(st, "wo", lambda k: wout_d[k * 128:(k + 1) * 128, :], 1024, stage)
                gbc_ = load_bc(st, "gatebc", gate_idx)
                mxs = slots(st, "mxs", 2, [128, 8, 512], BF16)
                xts = slots(st, "xts", 2, [128, 4, D], F32)
                t32 = slots(st, "t32", 2, [128, D], F32)
                xo = slots(st, "xo", 2, [128, 4, D], F32)
                for (r0, J) in units:
                    mx = mxs.next()
                    fw.dma('sp', [(mx[:, :, 0:J * 128], mix_d[:, :, r0:r0 + J * 128].rearrange("c p t -> p c t"))],
                           reads=[mix_d], writes=[mx])
                    xt = xts.next()
                    fw.dma('sp', [(xt[:, 0:J, :], src[r0:r0 + J * 128, :].rearrange("(j p) d -> p j d", p=128))],
                           reads=[src], writes=[xt])
                    xo_ = xo.next()
                    for j in range(J):
                        p = pj.next()
                        for n in range(2):
                            for k in range(KC):
                                fw.op('pe', lambda t, k=k, n=n, p=p, j=j: t.matmul(
                                    p[:, n * 512:(n + 1) * 512], lhsT=mx[:, k, j * 128:(j + 1) * 128],
                                    rhs=wo[:, k, n * 512:(n + 1) * 512], start=(k == 0), stop=(k == KC - 1)),
                                    reads=[mx, wo], writes=[p], inc=(n == 1 and k == KC - 1))
                        tt = t32.next()
                        fw.op('dve', lambda v, p=p, tt=tt: v.tensor_tensor(out=tt[:], in0=p[:], in1=gbc_[:], op=ALU.mult),
                              reads=[p, gbc_], writes=[tt])
                        fw.op('pool', lambda g, tt=tt, j=j: g.tensor_tensor(out=xo_[:, j, :], in0=tt[:], in1=xt[:, j, :], op=ALU.add),
                              reads=[tt, xt], writes=[xo_])
                    d0 = r0 if dst_r0 is None else r0 - dst_r0
                    fw.dma('pool', [(dst[d0:d0 + J * 128, :].rearrange("(j p) d -> p j d", p=128), xo_[:, 0:J, :])],
                           reads=[xo_], accw=[dst])
                fw.barrier()

        units_q = [(u * 512, 4) for u in range(T_q // 512)]
        if T_q % 512:
            units_q.append(((T_q // 512) * 512, (T_q % 512) // 128))
        units_q2 = [(u * 256, 2) for u in range(T_q // 256)]
        chk('setup')
        phase_kv0()
        chk('kv0')
        phase_q0(xq, units_q2, 0, ropeq, qTa, qTb, gT0)
        phase_q0(ctxb, [(0, 2)], 1, None, qTa_c, qTb_c, gT0_c)
        chk('q0')
        phase_attn0(T_q, 0, NKT, qTa, qTb, gT0, mixT0)
        phase_attn0(L, 0, 2, qTa_c, qTb_c, gT0_c, mixT0_c)
        chk('attn0')
        phase_out(xq, units_q, mixT0, wout0, 2, x1)
        phase_out(ctxb, [(0, 2)], mixT0_c, wout0, 5, xc1)
        chk('out0')

        with ExitStack() as st:
            KT = sb(st, "KT", [128, T_q], BF16)
            VX = sb(st, "VX", [128, NQT, 2, 128], BF16)
            KTc = sb(st, "KTc", [128, L], BF16)
            VXc = sb(st, "VXc", [128, 2, 2, 128], BF16)
            g1bc = sb(st, "g1bc", [128, 128], F32)
            fw.dma('sp', [(g1bc[:], gains1[0:1, :].partition_broadcast(128))], writes=[g1bc])
            fw.op('dve', lambda v: v.tensor_scalar(out=g1bc[:, 0:64], in0=g1bc[:, 0:64], scalar1=64.0 ** -0.5,
                                                   scalar2=None, op0=ALU.mult), reads=[g1bc], writes=[g1bc])
            fw.op('pool', lambda g: g.memset(VXc[:, :, :, 64:128], 1.0), writes=[VXc])

            with ExitStack() as s2:
                tp = slots(s2, "tp", 2, [128, 1024], BF16, psum=True)
                pj = slots(s2, "pj", 2, [128, 1024], F32, psum=True)
                fm = slots(s2, "fm", 2, [128, 512], F32, psum=True)
                fr = Front(s2, 2, tp)
                stage = slots(s2, "wst", 2, [128, 1024], F32)
                wt = load_w_bf16(s2, "wt", lambda k: wt1[k * 128:(k + 1) * 128, :], 768, stage)
                wf = sb(s2, "wf", [128, KC, 2560], BF16)
                for k in range(KC):
                    for (c0, c1) in ((0, 1024), (1024, 2048), (2048, 2560)):
                        s_ = stage.next()
                        fw.dma('sp', [(s_[:, 0:c1 - c0], wf1[k * 128:(k + 1) * 128, c0:c1])], writes=[s_])
                        fw.op('pool', lambda g, k=k, s_=s_, c0=c0, c1=c1: g.tensor_copy(out=wf[:, k, c0:c1], in_=s_[:, 0:c1 - c0]),
                              reads=[s_], writes=[wf])
                bcs = {0: (load_bc(s2, "sc1t", 7), load_bc(s2, "sht", 6)), 1: (load_bc(s2, "sc1c", 10), load_bc(s2, "shc", 9))}
                tabs = slots(s2, "tab", 2, [128, 2, 192], F32)
                stp = slots(s2, "stp", 2, [128, 2, 768], F32)
                ssum = slots(s2, "ssum", 4, [128, 40], F32)
                rstd = slots(s2, "rstd", 4, [128, 40], F32)
                qkn = slots(s2, "qkn", 2, [128, 2, 10, 64], F32)
                qkb = slots(s2, "qkb", 2, [128, 2, 640], BF16)
                tmp1 = sb(s2, "tmp1", [128, 1280], F32)
                tmp2 = sb(s2, "tmp2", [128, 1280], F32)
                qst = slots(s2, "qst", 2, [128, 4, 256], BF16)
                ust = slots(s2, "ust", 2, [128, 4, 256], BF16)
                gcs = slots(s2, "gcs", 2, [128, 256], BF16)
                gbst = slots(s2, "gbst", 2, [128, 4, 256], BF16)
                sgst = slots(s2, "sgst", 2, [128, 8, 256], BF16)

                units1 = [(xc1, 0, 2, 1, None)] + [(x1, r0, J, 0, r0) for (r0, J) in units_q2]
                for (src, r0, J, who, rp0) in units1:
                    sc1, sh = bcs[who]
                    hT = fr.run(src, r0, J, sc1, sh)
                    tab = None
                    if rp0 is not None:
                        tab = tabs.next()
                        fw.dma('sp', [(tab[:, 0:J, :], ropeq[rp0:rp0 + J * 128, :].rearrange("(j p) c -> p j c", p=128))],
                               writes=[tab])
                    s_ = stp.next()
                    for j in range(J):
                        p = pj.next()
                        for (c0, c1) in ((0, 512), (512, 768)):
                            for k in range(KC):
                                fw.op('pe', lambda t, k=k, p=p, j=j, c0=c0, c1=c1: t.matmul(
                                    p[:, c0:c1], lhsT=hT[:, k, j * 128:(j + 1) * 128], rhs=wt[:, k, c0:c1],
                                    start=(k == 0), stop=(k == KC - 1)),
                                    reads=[hT, wt], writes=[p], inc=(c0 == 512 and k == KC - 1))
                        fw.op('act', lambda a, p=p, j=j: a.copy(out=s_[:, j, :], in_=p[:, 0:768]), reads=[p], writes=[s_])
                    qk = qkn.next()
                    fw.op('pool', lambda g: g.tensor_copy(out=qk[:, 0:J].rearrange("p j h c -> p j (h c)"), in_=s_[:, 0:J, 0:640]),
                          reads=[s_], writes=[qk])
                    sm = ssum.next()
                    rsd = rstd.next()
                    qkv = qk[:, 0:J, :, :].rearrange("p j h c -> p (j h) c")
                    sq2 = tmp1[:, 0:J * 640].rearrange("p (n c) -> p n c", c=64)
                    fw.op('dve', lambda v: v.tensor_tensor(out=sq2, in0=qkv, in1=qkv, op=ALU.mult), reads=[qk], writes=[tmp1])
                    fw.op('dve', lambda v: v.tensor_reduce(out=sm[:, 0:J * 10], in_=sq2, axis=AX.X, op=ALU.add),
                          reads=[tmp1], writes=[sm])
                    fw.op('act', lambda a: a.activation(out=sm[:, 0:J * 10], in_=sm[:, 0:J * 10], func=AF.Sqrt, bias=EPS,
                                                        scale=1.0 / 64), reads=[sm], writes=[sm])
                    fw.op('dve', lambda v: v.reciprocal(out=rsd[:, 0:J * 10], in_=sm[:, 0:J * 10]), reads=[sm], writes=[rsd])
                    fw.op('dve', lambda v: v.tensor_tensor(out=qkv, in0=qkv,
                                                           in1=rsd[:, 0:J * 10].unsqueeze(2).to_broadcast([128, J * 10, 64]),
                                                           op=ALU.mult), reads=[qk, rsd], writes=[qk])
                    fw.op('dve', lambda v: v.tensor_tensor(out=qk[:, 0:J, 0:8, :], in0=qk[:, 0:J, 0:8, :],
                                                           in1=g1bc[:, 0:64].unsqueeze(1).unsqueeze(1).to_broadcast([128, J, 8, 64]),
                                                           op=ALU.mult), reads=[qk, g1bc], writes=[qk])
                    fw.op('dve', lambda v: v.tensor_tensor(out=qk[:, 0:J, 8:10, :], in0=qk[:, 0:J, 8:10, :],
                                                           in1=g1bc[:, 64:128].unsqueeze(1).unsqueeze(1).to_broadcast([128, J, 2, 64]),
                                                           op=ALU.mult), reads=[qk, g1bc], writes=[qk])
                    if tab is not None:
                        rope(qk, J, 10, 64, 0, 64, tab, 64, tmp1, tmp2)
                    qb = qkb.next()
                    fw.op('pool', lambda g: g.tensor_copy(out=qb[:, 0:J, :], in_=qk[:, 0:J].rearrange("p j h c -> p j (h c)")),
                          reads=[qk], writes=[qb])
                    if who == 0:
                        tl0 = r0 // 128
                        for j in range(J):
                            tl = tl0 + j
                            fw.op('pool', lambda g, j=j, tl=tl: g.tensor_copy(
                                out=VX[:, tl, :, 0:64], in_=s_[:, j, 640:768].rearrange("p (g c) -> p g c", g=2)),
                                reads=[s_], writes=[VX])
                            fw.op('pool', lambda g, tl=tl: g.memset(VX[:, tl, :, 64:128], 1.0), reads=[], writes=[VX])
                            if tl == 0 or tl == NQT - 1:
                                vc = 0 if tl == 0 else 1
                                fw.op('dve', lambda v, tl=tl, vc=vc: v.tensor_scalar(
                                    out=VX[:, tl, :, :], in0=VX[:, tl, :, :], scalar1=validt[:, vc:vc + 1], scalar2=None,
                                    op0=ALU.mult), reads=[VX, validt], writes=[VX])
                    else:
                        for j in range(J):
                            fw.op('pool', lambda g, j=j: g.tensor_copy(
                                out=VXc[:, j, :, 0:64], in_=s_[:, j, 640:768].rearrange("p (g c) -> p g c", g=2)),
                                reads=[s_], writes=[VXc])
                    qs = qst.next() if who == 0 else None
                    for j in range(J):
                        p = tp.next()
                        for m in range(4):
                            fw.op('pe', lambda t, m=m, p=p, j=j: t.transpose(
                                out=p[0:64, m * 128:(m + 1) * 128], in_=qb[:, j, m * 64:(m + 1) * 64], identity=ident[:]),
                                reads=[qb, ident], writes=[p], inc=False)
                            fw.op('pe', lambda t, m=m, p=p, j=j: t.transpose(
                                out=p[64:128, m * 128:(m + 1) * 128], in_=qb[:, j, (m + 4) * 64:(m + 5) * 64], identity=ident[:]),
                                reads=[qb, ident], writes=[p], inc=False)
                        fw.op('pe', lambda t, p=p, j=j: t.transpose(out=p[:, 512:640], in_=qb[:, j, 512:640], identity=ident[:]),
                              reads=[qb, ident], writes=[p], inc=True)
                        if who == 0:
                            c0 = r0 + j * 128
                            fw.op('act', lambda a, p=p, j=j: a.copy(out=qs[:, :, j * 128:(j + 1) * 128],
                                                                   in_=p[:, 0:512].rearrange("p (m t) -> p m t", m=4)),
                                  reads=[p], writes=[qs])
                            fw.op('act', lambda a, p=p, c0=c0: a.copy(out=KT[:, c0:c0 + 128], in_=p[:, 512:640]),
                                  reads=[p], writes=[KT])
                        else:
                            fw.op('act', lambda a, p=p, j=j: a.copy(out=KTc[:, j * 128:(j + 1) * 128], in_=p[:, 512:640]),
                                  reads=[p], writes=[KTc])
                    if who == 1:
                        continue
                    fw.dma('pool', [(qT1[:, :, r0:r0 + J * 128].rearrange("m p t -> p m t"), qs[:, :, 0:J * 128])],
                           reads=[qs], accw=[qT1])
                    us = ust.next()
                    gb_ = gbst.next()
                    sg_ = sgst.next()
                    W_ = J * 128
                    for c in range(4):
                        p = fm.next()
                        for k in range(KC):
                            fw.op('pe', lambda t, k=k, p=p, c=c: t.matmul(p[:, 0:W_], lhsT=wf[:, k, c * 128:(c + 1) * 128],
                                                                         rhs=hT[:, k, 0:W_], start=(k == 0), stop=(k == KC - 1)),
                                  reads=[hT, wf], writes=[p], inc=(k == KC - 1))
                        fw.op('act', lambda a, p=p, c=c: a.copy(out=gb_[:, c, 0:W_], in_=p[:, 0:W_]), reads=[p], writes=[gb_])
                        p = fm.next()
                        for k in range(KC):
                            fw.op('pe', lambda t, k=k, p=p, c=c: t.matmul(p[:, 0:W_], lhsT=wf[:, k, 512 + c * 128:512 + (c + 1) * 128],
                                                                         rhs=hT[:, k, 0:W_], start=(k == 0), stop=(k == KC - 1)),
                                  reads=[hT, wf], writes=[p], inc=(k == KC - 1))
                        gc_ = gcs.next()
                        fw.op('act', lambda a, p=p, gc_=gc_: a.copy(out=gc_[:, 0:W_], in_=p[:, 0:W_]), reads=[p], writes=[gc_])
                        p = fm.next()
                        for k in range(KC):
                            fw.op('pe', lambda t, k=k, p=p, c=c: t.matmul(p[:, 0:W_], lhsT=wf[:, k, 1024 + c * 128:1024 + (c + 1) * 128],
                                                                         rhs=hT[:, k, 0:W_], start=(k == 0), stop=(k == KC - 1)),
                                  reads=[hT, wf], writes=[p], inc=(k == KC - 1))
                        fw.op('dve', lambda v, p=p, gc_=gc_, c=c: v.tensor_tensor(out=us[:, c, 0:W_], in0=p[:, 0:W_],
                                                                                 in1=gc_[:, 0:W_], op=ALU.mult),
                              reads=[p, gc_], writes=[us])
                    for c in range(8):
                        p = fm.next()
                        for k in range(KC):
                            fw.op('pe', lambda t, k=k, p=p, c=c: t.matmul(p[:, 0:W_], lhsT=wf[:, k, 1536 + c * 128:1536 + (c + 1) * 128],
                                                                         rhs=hT[:, k, 0:W_], start=(k == 0), stop=(k == KC - 1)),
                                  reads=[hT, wf], writes=[p], inc=(k == KC - 1))
                        fw.op('act', lambda a, p=p, c=c: a.activation(out=sg_[:, c, 0:W_], in_=p[:, 0:W_], func=AF.Silu),
                              reads=[p], writes=[sg_])
                    if r0 == 0:
                        fw.op('dve', lambda v: v.tensor_scalar(out=us[:, :, 0:128], in0=us[:, :, 0:128], scalar1=validt[:, 0:1],
                                                               scalar2=None, op0=ALU.mult), reads=[us, validt], writes=[us])
                    if r0 + W_ == T_q:
                        fw.op('dve', lambda v: v.tensor_scalar(out=us[:, :, W_ - 128:W_], in0=us[:, :, W_ - 128:W_],
                                                               scalar1=validt[:, 1:2], scalar2=None, op0=ALU.mult),
                              reads=[us, validt], writes=[us])
                    fw.dma('pool', [(uT[:, :, r0:r0 + W_].rearrange("c p t -> p c t"), us[:, :, 0:W_])], reads=[us], accw=[uT])
                    fw.dma('pool', [(gbT[:, :, r0:r0 + W_].rearrange("c p t -> p c t"), gb_[:, :, 0:W_])], reads=[gb_], accw=[gbT])
                    fw.dma('pool', [(sgT[:, :, r0:r0 + W_].rearrange("c p t -> p c t"), sg_[:, :, 0:W_])], reads=[sg_], accw=[sgT])
                fw.barrier()

            chk('d1')
            with ExitStack() as s3:
                pss = slots(s3, "pss", 2, [128, 2, 512], F32, psum=True)
                pso = slots(s3, "pso", 2, [128, 512], F32, psum=True)
                pj = slots(s3, "pj", 1, [128, 1024], F32, psum=True)
                stage = slots(s3, "wst", 2, [128, 1024], F32)
                wo = load_w_bf16(s3, "wo", lambda k: wout1[k * 128:(k + 1) * 128, :], 1024, stage)
                gate1 = load_bc(s3, "gate1", 8)
                cw = sb(s3, "cw", [128, 4, 3], F32)
                fw.dma('sp', [(cw[:], convw[:, :, :])], writes=[cw])
                esk = sb(s3, "esk", [128, 8], F32)
                fw.dma('sp', [(esk[:], sink[0:1, :].partition_broadcast(128))], writes=[esk])
                fw.op('act', lambda a: a.activation(out=esk[:], in_=esk[:], func=AF.Exp), reads=[esk], writes=[esk])
                esf = sb(s3, "esf", [128, 2, 4, 128], F32)
                fw.op('pool', lambda g: g.memset(esf[:], 0.0), writes=[esf])
                for g_ in range(2):
                    fw.op('dve', lambda v, g_=g_: v.tensor_tensor(
                        out=esf[:, g_, :, :], in0=esf[:, g_, :, :],
                        in1=esk[:, g_ * 4:(g_ + 1) * 4].unsqueeze(2).to_broadcast([128, 4, 128]), op=ALU.add),
                        reads=[esf, esk], writes=[esf])
                mprev = sb(s3, "mprev", [128, 128], BF16)
                mnext = sb(s3, "mnext", [128, 128], BF16)
                fw.op('pool', lambda g: g.memset(mprev[:], 1.0), writes=[mprev])
                fw.op('pool', lambda g: g.memset(mnext[:], 1.0), writes=[mnext])
                fw.op('pool', lambda g: g.affine_select(out=mprev[:], in_=mprev[:], pattern=[[-1, 128]], compare_op=ALU.is_ge,
                                                        fill=0.0, base=0, channel_multiplier=1), reads=[mprev], writes=[mprev])
                fw.op('pool', lambda g: g.affine_select(out=mnext[:], in_=mnext[:], pattern=[[1, 128]], compare_op=ALU.is_ge,
                                                        fill=0.0, base=0, channel_multiplier=-1), reads=[mnext], writes=[mnext])
                pT = slots(s3, "pT", 3, [128, 2, 512], BF16)
                Rs = slots(s3, "Rs", 2, [128, 512], F32)
                gbs = slots(s3, "gbs", 2, [128, 4, 512], BF16)
                sgs = slots(s3, "sgs", 2, [128, 8, 512], BF16)
                mxu = slots(s3, "mxu", 2, [128, 8, 512], BF16)
                acc = slots(s3, "acc", 2, [128, 512], F32)
                xts = slots(s3, "xts", 2, [128, 4, D], F32)
                t32 = slots(s3, "t32", 2, [128, D], F32)
                xo = slots(s3, "xo", 2, [128, 4, D], F32)
                qsu = slots(s3, "qsu", 2, [128, 4, 512], BF16)
                usu = slots(s3, "usu", 2, [128, 4, 514], BF16)
                for u in range(T_own // 512):
                    r0 = 128 + u * 512
                    gb_ = gbs.next()
                    sg_ = sgs.next()
                    fw.dma('sp', [(gb_[:], gbT[:, :, r0:r0 + 512].rearrange("c p t -> p c t"))], reads=[gbT], writes=[gb_])
                    fw.dma('sp', [(sg_[:], sgT[:, :, r0:r0 + 512].rearrange("c p t -> p c t"))], reads=[sgT], writes=[sg_])
                    xt = xts.next()
                    fw.dma('sp', [(xt[:], x1[r0:r0 + 512, :].rearrange("(j p) d -> p j d", p=128))], reads=[x1], writes=[xt])
                    mx = mxu.next()
                    qs = qsu.next()
                    us = usu.next()
                    fw.dma('sp', [(qs[:], qT1[:, :, r0:r0 + 512].rearrange("m p t -> p m t"))], reads=[qT1], writes=[qs])
                    fw.dma('sp', [(us[:], uT[:, :, r0 - 1:r0 + 513].rearrange("c p t -> p c t"))], reads=[uT], writes=[us])
                    for c in range(4):
                        a_ = acc.next()
                        fw.op('dve', lambda v, c=c, a_=a_: v.tensor_scalar(out=a_[:], in0=us[:, c, 0:512], scalar1=cw[:, c, 0:1],
                                                                          scalar2=None, op0=ALU.mult), reads=[us, cw], writes=[a_])
                        fw.op('dve', lambda v, c=c, a_=a_: v.scalar_tensor_tensor(out=a_[:], in0=us[:, c, 1:513],
                                                                                 scalar=cw[:, c, 1:2], in1=a_[:], op0=ALU.mult,
                                                                                 op1=ALU.add), reads=[us, cw, a_], writes=[a_])
                        fw.op('dve', lambda v, c=c, a_=a_: v.scalar_tensor_tensor(out=a_[:], in0=us[:, c, 2:514],
                                                                                 scalar=cw[:, c, 2:3], in1=a_[:], op0=ALU.mult,
                                                                                 op1=ALU.add), reads=[us, cw, a_], writes=[a_])
                        fw.op('pool', lambda g, c=c, a_=a_: g.tensor_tensor(out=a_[:], in0=a_[:], in1=gb_[:, c, :], op=ALU.mult),
                              reads=[a_, gb_], writes=[a_])
                        fw.op('pool', lambda g, c=c, a_=a_: g.tensor_tensor(out=mx[:, 4 + c, :], in0=a_[:], in1=sg_[:, 4 + c, :], op=ALU.mult),
                              reads=[a_, sg_], writes=[mx])
                    for jb in range(4):
                        tl = r0 // 128 + jb
                        c0 = tl * 128
                        for g_ in range(2):
                            pr = slice(g_ * 64, (g_ + 1) * 64)
                            po = pso.next()
                            batches = [[('c', 0), ('c', 1)], [('l', tl - 1), ('l', tl)], [('l', tl + 1)]]
                            nmm = 5
                            cnt = 0
                            for bt in batches:
                                p = pss.next()
                                for i, (kind, ti) in enumerate(bt):
                                    lhs = KTc[pr, ti * 128:(ti + 1) * 128] if kind == 'c' else KT[pr, ti * 128:(ti + 1) * 128]
                                    fw.op('pe', lambda t, i=i, p=p, lhs=lhs: t.matmul(
                                        p[:, i, :], lhsT=lhs, rhs=qs[pr, :, jb * 128:(jb + 1) * 128], start=True, stop=True),
                                        reads=[KTc, KT, qs], writes=[p], inc=(i == len(bt) - 1))
                                pt = pT.next()
                                n = len(bt)
                                fw.op('act', lambda a, p=p, pt=pt, n=n: a.activation(out=pt[:, 0:n, :], in_=p[:, 0:n, :], func=AF.Exp),
                                      reads=[p], writes=[pt])
                                for i, (kind, ti) in enumerate(bt):
                                    if kind == 'l' and ti != tl:
                                        msk = mprev if ti < tl else mnext
                                        fw.op('dve', lambda v, i=i, pt=pt, msk=msk: v.tensor_tensor(
                                            out=pt[:, i, :].rearrange("p (m t) -> p m t", m=4),
                                            in0=pt[:, i, :].rearrange("p (m t) -> p m t", m=4),
                                            in1=msk[:].unsqueeze(1).to_broadcast([128, 4, 128]), op=ALU.mult),
                                            reads=[pt, msk], writes=[pt])
                                for i, (kind, ti) in enumerate(bt):
                                    lhs = VXc[:, ti, g_, :] if kind == 'c' else VX[:, ti, g_, :]
                                    fw.op('pe', lambda t, i=i, pt=pt, lhs=lhs, cnt=cnt: t.matmul(
                                        po[:], lhsT=lhs, rhs=pt[:, i, :], start=(cnt == 0), stop=(cnt == nmm - 1)),
                                        reads=[VXc, VX, pt], writes=[po], inc=(i == len(bt) - 1))
                                    cnt += 1
                            R = Rs.next()
                            fw.op('dve', lambda v, R=R, po=po, g_=g_: v.tensor_tensor(
                                out=R[0:64, :], in0=po[64:128, :], in1=esf[64:128, g_, :, :].rearrange("p m t -> p (m t)"),
                                op=ALU.add), reads=[po, esf], writes=[R])
                            fw.op('dve', lambda v, R=R: v.reciprocal(out=R[0:64, :], in_=R[0:64, :]), reads=[R], writes=[R])
                            fw.op('dve', lambda v, R=R, po=po: v.tensor_tensor(out=R[0:64, :], in0=po[0:64, :], in1=R[0:64, :], op=ALU.mult),
                                  reads=[po, R], writes=[R])
                            Rv = R[0:64, :].rearrange("p (c two t) -> p c two t", c=2, two=2)
                            for hf in range(2):
                                fw.op('pool', lambda g, hf=hf, Rv=Rv, g_=g_, jb=jb: g.tensor_copy(
                                    out=mx[hf * 64:(hf + 1) * 64, 2 * g_:2 * g_ + 2, jb * 128:(jb + 1) * 128],
                                    in_=Rv[:, :, hf, :]), reads=[R], writes=[mx])
                    fw.op('dve', lambda v: v.tensor_tensor(out=mx[:, 0:4, :], in0=mx[:, 0:4, :], in1=sg_[:, 0:4, :], op=ALU.mult),
                          reads=[mx, sg_], writes=[mx])
                    xo_ = xo.next()
                    for j in range(4):
                        p = pj.next()
                        for n in range(2):
                            for k in range(KC):
                                fw.op('pe', lambda t, k=k, n=n, p=p, j=j: t.matmul(
                                    p[:, n * 512:(n + 1) * 512], lhsT=mx[:, k, j * 128:(j + 1) * 128],
                                    rhs=wo[:, k, n * 512:(n + 1) * 512], start=(k == 0), stop=(k == KC - 1)),
                                    reads=[mx, wo], writes=[p], inc=(n == 1 and k == KC - 1))
                        tt = t32.next()
                        fw.op('dve', lambda v, p=p, tt=tt: v.tensor_tensor(out=tt[:], in0=p[:], in1=gate1[:], op=ALU.mult),
                              reads=[p, gate1], writes=[tt])
                        fw.op('pool', lambda g, tt=tt, j=j: g.tensor_tensor(out=xo_[:, j, :], in0=tt[:], in1=xt[:, j, :], op=ALU.add),
                              reads=[tt, xt], writes=[xo_])
                    fw.dma('pool', [(out[u * 512:(u + 1) * 512, :].rearrange("(j p) d -> p j d", p=128), xo_[:])],
                           reads=[xo_], accw=[out])
                fw.barrier()
        fw.barrier()
        build.nins = dict(fw.nins)
    except _Stop:
        pass
    return nc


def _rope_tables(pos_row, pos_col):
    f32 = np.float32

    def cs(pos, half):
        inv = (f32(10000.0) ** (-(np.arange(half, dtype=f32)) / f32(half))).astype(f32)
        ang = (pos.astype(f32)[:, None] * inv[None, :]).astype(f32)
        return np.cos(ang).astype(f32), np.sin(ang).astype(f32)

    cr8, sr8 = cs(pos_row, 8)
    cc8, sc8 = cs(pos_col, 8)
    cr16, sr16 = cs(pos_row, 16)
    cc16, sc16 = cs(pos_col, 16)
    C32 = np.concatenate([cr8, cr8, cc8, cc8], 1)
    S32 = np.concatenate([-sr8, sr8, -sc8, sc8], 1)
    C64 = np.concatenate([cr16, cr16, cc16, cc16], 1)
    S64 = np.concatenate([-sr16, sr16, -sc16, sc16], 1)
    return np.ascontiguousarray(np.concatenate([C32, S32, C64, S64], 1).astype(f32))


_NC_CACHE = {}


def run(inputs, S, stop=None, dbg=None, dbg_out=()):
    f32 = np.float32
    x = np.asarray(inputs['x'], f32)
    B = x.shape[0]
    assert x.shape[1] == S and B == 2
    T_own = S // 4
    T_q = T_own + 256
    if (S, stop) not in _NC_CACHE:
        _NC_CACHE[(S, stop)] = build(S, stop, dbg_out)
    nc = _NC_CACHE[(S, stop)]
    c = np.asarray(inputs['c'], f32)
    ctx = np.asarray(inputs['ctx'], f32)
    c_ctx = np.asarray(inputs['c_ctx'], f32)
    w_in0 = np.asarray(inputs['ab_w_in'], f32)[0]
    w_in1 = np.asarray(inputs['cd_w_in'], f32)[0]
    wkv0 = np.ascontiguousarray(np.concatenate([w_in0[:, 256:512], w_in0[:, 1056:1184], w_in0[:, 1184:1312], w_in0[:, 512:544]], 1))
    wq0 = np.ascontiguousarray(np.concatenate([w_in0[:, 0:256], w_in0[:, 544:1056]], 1))
    wg0 = np.ascontiguousarray(w_in0[:, 1312:2336])
    wt1 = np.ascontiguousarray(w_in1[:, 0:768])
    wf1 = np.ascontiguousarray(w_in1[:, 768:3328])
    gains0 = np.concatenate([np.asarray(inputs['mla_q_gain'], f32)[0], np.asarray(inputs['mla_k_gain'], f32)[0],
                             np.asarray(inputs['gqa_q_gain'], f32)[0], np.asarray(inputs['gqa_k_gain'], f32)[0]])[None, :]
    gains1 = np.concatenate([np.asarray(inputs['win_q_gain'], f32)[0], np.asarray(inputs['win_k_gain'], f32)[0]])[None, :]
    convw = np.ascontiguousarray(np.asarray(inputs['conv_w'], f32)[0].reshape(3, 4, 128).transpose(2, 1, 0))
    pos = np.arange(S)
    ropek = _rope_tables(pos // GRID_W, pos % GRID_W)
    shared = {
        "mod_w": np.ascontiguousarray(np.asarray(inputs['mod_w'], f32)),
        "mod_b": np.ascontiguousarray(np.asarray(inputs['mod_b'], f32)),
        "wkv0": wkv0, "wq0": wq0, "wg0": wg0,
        "wout0": np.ascontiguousarray(np.asarray(inputs['ab_w_out'], f32)[0]),
        "w_uq": np.ascontiguousarray(np.asarray(inputs['mla_w_uq'], f32)[0]),
        "w_ukv": np.ascontiguousarray(np.asarray(inputs['mla_w_ukv'], f32)[0]),
        "cq_gain": np.ascontiguousarray(np.asarray(inputs['mla_cq_gain'], f32)[0].reshape(2, 128).T),
        "ckv_gain": np.ascontiguousarray(np.asarray(inputs['mla_ckv_gain'], f32)[0].reshape(2, 128).T),
        "gains0": np.ascontiguousarray(gains0),
        "wt1": wt1, "wf1": wf1,
        "wout1": np.ascontiguousarray(np.asarray(inputs['cd_w_out'], f32)[0]),
        "gains1": np.ascontiguousarray(gains1),
        "sink": np.ascontiguousarray(np.asarray(inputs['win_sink'], f32)[0][None, :]),
        "convw": convw,
        "ropek": ropek,
    }
    in_maps = []
    for core in range(NCORES):
        b, qc = core // 4, core % 4
        start = qc * T_own
        lo, hi = start - 128, start + T_own + 128
        xq = np.zeros((T_q, D), f32)
        a, e = max(lo, 0), min(hi, S)
        xq[a - lo:e - lo] = x[b, a:e]
        pq = np.clip(np.arange(lo, hi), 0, S - 1)
        vcol = np.array([1.0 if lo >= 0 else 0.0, 1.0 if hi <= S else 0.0], f32)
        cvec = np.stack([c[b], c_ctx], 1).reshape(KC, 128, 2).transpose(1, 0, 2)
        m = dict(shared)
        m.update({
            "xq": xq, "xkv": np.ascontiguousarray(x[b]), "ctxb": np.ascontiguousarray(ctx[b]),
            "cT": np.ascontiguousarray(cvec.astype(f32)),
            "ropeq": _rope_tables(pq // GRID_W, pq % GRID_W),
            "valid": np.ascontiguousarray(np.broadcast_to(vcol[None, :], (128, 2)).astype(f32)),
        })
        in_maps.append(m)
    res = run_bass_kernel_spmd(nc, in_maps, core_ids=list(range(NCORES)))
    if dbg is not None:
        dbg.append(res)
    outp = np.zeros((B, S, D), f32)
    for core in range(NCORES):
        b, qc = core // 4, core % 4
        outp[b, qc * T_own:(qc + 1) * T_own] = res.results[core]["out"]
    return outp


def kernel(**inputs):
    return run(inputs, 16384)
```

```python
import numpy as np
from contextlib import ExitStack
import concourse.bass as bass
import concourse.mybir as mybir
from concourse.bass_utils import run_bass_kernel_spmd

F32 = mybir.dt.float32
BF16 = mybir.dt.bfloat16
ALU = mybir.AluOpType
AF = mybir.ActivationFunctionType
AX = mybir.AxisListType

D = 1024
KC = 8
L = 256
EPS = 1e-6
GRID_W = 64
NCORES = 8


class Tn:
    def __init__(self, h, const=False, psum=False):
        self.h = h
        self.w = {}
        self.r = {}
        self.const = const
        self.psum = psum

    def __getitem__(self, k):
        return self.h[k]


class FW:
    def __init__(self, nc, es):
        self.nc = nc
        self.E = {'sp': nc.sync, 'act': nc.scalar, 'pool': nc.gpsimd, 'dve': nc.vector, 'pe': nc.tensor}
        self.sem = {}
        self.tot = {}
        self.seen = {e: {} for e in self.E}
        for e in self.E:
            self.sem[e] = es.enter_context(nc.semaphore('s_' + e))
            self.tot[e] = 0
        self.ring = {}
        self.rpos = {}
        for e, n in (('sp', 30), ('pool', 24), ('act', 8)):
            keys = []
            for i in range(n):
                k = 'd_%s%d' % (e, i)
                self.sem[k] = es.enter_context(nc.semaphore(k))
                self.tot[k] = 0
                keys.append(k)
            self.ring[e] = keys
            self.rpos[e] = 0
        self.nins = {e: 0 for e in self.E}

    def _wait(self, e, deps):
        for k, v in deps.items():
            if v <= 0:
                continue
            if k == e and e == 'pe':
                continue
            if self.seen[e].get(k, 0) >= v:
                continue
            assert v <= self.tot[k], "wait on unclosed group %s %d>%d (eng %s)" % (k, v, self.tot[k], e)
            self.E[e].wait_ge(self.sem[k], v)
            self.seen[e][k] = v

    @staticmethod
    def _deps(e, reads, writes, accw):
        d = {}
        for b in reads:
            for k, v in b.w.items():
                if d.get(k, 0) < v:
                    d[k] = v
            if b.psum:
                for k, v in b.r.items():
                    if k != e and d.get(k, 0) < v:
                        d[k] = v
        for b in writes:
            for k, v in b.w.items():
                if d.get(k, 0) < v:
                    d[k] = v
            for k, v in b.r.items():
                if d.get(k, 0) < v:
                    d[k] = v
        for b in accw:
            for k, v in b.r.items():
                if d.get(k, 0) < v:
                    d[k] = v
        return d

    def op(self, e, fn, reads=(), writes=(), inc=True):
        self._wait(e, self._deps(e, reads, writes, ()))
        ins = fn(self.E[e])
        self.nins[e] += 1
        val = self.tot[e] + 1
        if inc:
            ins.then_inc(self.sem[e], 1)
            self.tot[e] = val
        for b in reads:
            if not b.const and b.r.get(e, 0) < val:
                b.r[e] = val
        for b in writes:
            b.w = {e: val}
            b.r = {}
        return ins

    def dma(self, e, pairs, reads=(), writes=(), accw=()):
        k = self.ring[e][self.rpos[e]]
        self.rpos[e] = (self.rpos[e] + 1) % len(self.ring[e])
        deps = self._deps(e, reads, writes, accw)
        if deps.get(k, 0) < self.tot[k]:
            deps[k] = self.tot[k]
        self._wait(e, deps)
        val = self.tot[k] + 16 * len(pairs)
        for (o, i) in pairs:
            self.E[e].dma_start(out=o, in_=i).then_inc(self.sem[k], 16)
            self.nins[e] += 1
        self.tot[k] = val
        for b in reads:
            if not b.const and b.r.get(k, 0) < val:
                b.r[k] = val
        for b in writes:
            b.w = {k: val}
            b.r = {}
        for b in accw:
            if b.w.get(k, 0) < val:
                b.w[k] = val

    def barrier(self, engines=None):
        for e in (engines or self.E):
            self._wait(e, dict(self.tot))


class Slots:
    def __init__(self, items):
        self.items = items
        self.i = 0

    def next(self):
        t = self.items[self.i]
        self.i = (self.i + 1) % len(self.items)
        return t


class _Stop(Exception):
    pass


def build(S, stop=None, dbg_out=()):
    T_own = S // 4
    T_q = T_own + 256
    NQT = T_q // 128
    NK = L + S
    NKT = NK // 128

    nc = bass.Bass("TRN2", target_bir_lowering=False)

    def din(name, shape, dt=F32):
        return Tn(nc.dram_tensor(name, list(shape), dt, kind="ExternalInput").ap(), const=True)

    def dscr(name, shape, dt):
        if name in dbg_out:
            return Tn(nc.dram_tensor(name, list(shape), dt, kind="ExternalOutput").ap())
        return Tn(nc.dram_tensor(name, list(shape), dt).ap())

    xq = din("xq", [T_q, D])
    xkv = din("xkv", [S, D])
    ctxb = din("ctxb", [L, D])
    cT = din("cT", [128, KC, 2])
    mod_w = din("mod_w", [2, D, 3 * D])
    mod_b = din("mod_b", [2, 3 * D])
    wkv0 = din("wkv0", [D, 544])
    wq0 = din("wq0", [D, 768])
    wg0 = din("wg0", [D, 1024])
    wout0 = din("wout0", [D, D])
    w_uq = din("w_uq", [256, 768])
    w_ukv = din("w_ukv", [256, 1024])
    cq_gain = din("cq_gain", [128, 2])
    ckv_gain = din("ckv_gain", [128, 2])
    gains0 = din("gains0", [1, 96 + 96 + 64 + 64])
    wt1 = din("wt1", [D, 768])
    wf1 = din("wf1", [D, 2560])
    wout1 = din("wout1", [D, D])
    gains1 = din("gains1", [1, 128])
    sink = din("sink", [1, 8])
    convw = din("convw", [128, 4, 3])
    ropeq = din("ropeq", [T_q, 192])
    ropek = din("ropek", [S, 192])
    valid = din("valid", [128, 2])
    out = Tn(nc.dram_tensor("out", [T_own, D], F32, kind="ExternalOutput").ap())

    modbc = dscr("modbc", [12, 128, D], F32)
    kTa = dscr("kTa", [8, 96, NK], BF16)
    vA = dscr("vA", [8, 128, NKT, 64], BF16)
    kTb = dscr("kTb", [2, 64, NK], BF16)
    vB = dscr("vB", [2, 128, NKT, 64], BF16)
    qTa = dscr("qTa", [8, 96, T_q], BF16)
    qTb = dscr("qTb", [8, 64, T_q], BF16)
    gT0 = dscr("gT0", [8, 128, T_q], BF16)
    mixT0 = dscr("mixT0", [8, 128, T_q], BF16)
    x1 = dscr("x1", [T_q, D], F32)
    qTa_c = dscr("qTa_c", [8, 96, L], BF16)
    qTb_c = dscr("qTb_c", [8, 64, L], BF16)
    gT0_c = dscr("gT0_c", [8, 128, L], BF16)
    mixT0_c = dscr("mixT0_c", [8, 128, L], BF16)
    xc1 = dscr("xc1", [L, D], F32)
    gbT = dscr("gbT", [4, 128, T_q], BF16)
    sgT = dscr("sgT", [8, 128, T_q], BF16)
    qT1 = dscr("qT1", [4, 128, T_q], BF16)
    uT = dscr("uT", [4, 128, T_q], BF16)

    es = ExitStack()
    try:
      with es:
        fw = FW(nc, es)

        def chk(name):
            if stop == name:
                fw.barrier()
                build.nins = dict(fw.nins)
                raise _Stop()

        uid = [0]

        def sb(stk, name, shape, dt, const=False):
            uid[0] += 1
            return Tn(stk.enter_context(nc.sbuf_tensor("%s_%d" % (name, uid[0]), list(shape), dt)), const=const)

        def ps(stk, name, shape, dt=F32):
            uid[0] += 1
            return Tn(stk.enter_context(nc.psum_tensor("%s_%d" % (name, uid[0]), list(shape), dt)), psum=True)

        def slots(stk, name, n, shape, dt, psum=False):
            return Slots([(ps if psum else sb)(stk, "%s%d" % (name, i), shape, dt) for i in range(n)])

        ident = sb(es, "ident", [128, 128], BF16)
        fw.op('pool', lambda g: g.memset(ident[:], 0.0), writes=[ident])
        fw.op('pool', lambda g: g.affine_select(out=ident[:], in_=ident[:], pattern=[[-1, 128]],
                                                compare_op=ALU.not_equal, fill=1.0, base=0,
                                                channel_multiplier=1), reads=[ident], writes=[ident])
        ident.const = True
        validt = sb(es, "validt", [128, 2], F32)
        fw.dma('sp', [(validt[:], valid[:, :])], writes=[validt])
        validt.const = True

        with ExitStack() as st:
            cTt = sb(st, "cTt", [128, KC, 2], F32)
            scT = sb(st, "scT", [128, KC, 2], F32)
            mws = slots(st, "mws", 2, [128, KC, 512], F32)
            modsb = sb(st, "modsb", [2, 2, 3 * D], F32)
            modbias = sb(st, "modbias", [2, 2, 3 * D], F32)
            sel = sb(st, "sel", [2, 2, 128], F32)
            selw = sb(st, "selw", [2, 128], F32)
            pm = slots(st, "pm", 2, [2, 512], F32, psum=True)
            pb = slots(st, "pb", 2, [128, 1024], F32, psum=True)
            bcs = slots(st, "bcs", 2, [128, D], F32)

            fw.dma('sp', [(cTt[:], cT[:, :, :])], writes=[cTt])
            fw.op('act', lambda a: a.activation(out=scT[:], in_=cTt[:], func=AF.Silu), reads=[cTt], writes=[scT])
            for i in range(2):
                fw.dma('sp', [(modbias[0:1, i, :], mod_b[i:i + 1, :]), (modbias[1:2, i, :], mod_b[i:i + 1, :])],
                       accw=[modbias])
            fw.op('pool', lambda g: g.memset(sel[:], 0.0), writes=[sel])
            for who in range(2):
                fw.op('pool', lambda g, who=who: g.affine_select(
                    out=sel[:, who, :], in_=sel[:, who, :], pattern=[[0, 128]], compare_op=ALU.not_equal,
                    fill=1.0, base=-who, channel_multiplier=1), reads=[sel], writes=[sel])
            for i in range(2):
                for n in range(6):
                    mw = mws.next()
                    fw.dma('sp', [(mw[:], mod_w[i, :, n * 512:(n + 1) * 512].rearrange("(k p) n -> p k n", p=128))],
                           writes=[mw])
                    p = pm.next()
                    for k in range(KC):
                        fw.op('pe', lambda t, k=k, p=p, mw=mw: t.matmul(p[:], lhsT=scT[:, k, :], rhs=mw[:, k, :],
                                                                       start=(k == 0), stop=(k == KC - 1)),
                              reads=[scT, mw], writes=[p], inc=(k == KC - 1))
                    fw.op('dve', lambda v, p=p, i=i, n=n: v.tensor_tensor(
                        out=modsb[:, i, n * 512:(n + 1) * 512], in0=p[:], in1=modbias[:, i, n * 512:(n + 1) * 512],
                        op=ALU.add), reads=[p, modbias], writes=[modsb])
            for i in range(2):
                for who in range(2):
                    for which in range(3):
                        p = pb.next()
                        for n in range(2):
                            fw.op('pe', lambda t, p=p, n=n, i=i, who=who, which=which: t.matmul(
                                p[:, n * 512:(n + 1) * 512], lhsT=sel[:, who, :],
                                rhs=modsb[:, i, which * D + n * 512: which * D + (n + 1) * 512],
                                start=True, stop=True), reads=[sel, modsb], writes=[p], inc=(n == 1))
                        bc = bcs.next()
                        if which == 1:
                            fw.op('dve', lambda v, p=p, bc=bc: v.tensor_scalar(out=bc[:], in0=p[:], scalar1=1.0,
                                                                              scalar2=None, op0=ALU.add),
                                  reads=[p], writes=[bc])
                        else:
                            fw.op('dve', lambda v, p=p, bc=bc: v.tensor_copy(out=bc[:], in_=p[:]), reads=[p], writes=[bc])
                        fw.dma('pool', [(modbc[i * 6 + who * 3 + which, :, :], bc[:])], reads=[bc], accw=[modbc])
            fw.barrier()

        def load_w_bf16(stk, name, src_ap_fn, ncols, stage, gain=None, kchunks=KC):
            w = sb(stk, name, [128, kchunks, ncols], BF16)
            for k in range(kchunks):
                s_ = stage.next()
                fw.dma('sp', [(s_[:, 0:ncols], src_ap_fn(k))], writes=[s_])
                if gain is None:
                    fw.op('pool', lambda g, k=k, s_=s_: g.tensor_copy(out=w[:, k, :], in_=s_[:, 0:ncols]),
                          reads=[s_], writes=[w])
                else:
                    fw.op('dve', lambda v, k=k, s_=s_: v.tensor_scalar(out=w[:, k, :], in0=s_[:, 0:ncols],
                                                                      scalar1=gain[:, k:k + 1], scalar2=None,
                                                                      op0=ALU.mult), reads=[s_, gain], writes=[w])
            return w

        def load_bc(stk, name, idx):
            t = sb(stk, name, [128, D], F32)
            fw.dma('sp', [(t[:], modbc[idx, :, :])], reads=[modbc], writes=[t])
            return t

        def run_pipelined(gen_fn, units):
            live = []
            it = iter(units)
            while True:
                for g in list(live):
                    try:
                        next(g)
                    except StopIteration:
                        live.remove(g)
                u = next(it, None)
                if u is not None:
                    g = gen_fn(u)
                    try:
                        next(g)
                        live.append(g)
                    except StopIteration:
                        pass
                elif not live:
                    break

        class Front:
            def __init__(self, stk, Jmax, tp):
                self.xt = slots(stk, "f_xt", 2, [128, Jmax, D], F32)
                self.junk = sb(stk, "f_junk", [128, D], BF16)
                self.ss = slots(stk, "f_ss", 2, [128, Jmax], F32)
                self.rs = slots(stk, "f_rs", 2, [128, Jmax], F32)
                self.t32 = slots(stk, "f_t32", 1, [128, D], F32)
                self.hb = slots(stk, "f_hb", 2, [128, D], BF16)
                self.hT = slots(stk, "f_hT", 2, [128, KC, Jmax * 128], BF16)
                self.tp = tp

            def run(self, src, r0, J, sc1, sh):
                xt = self.xt.next()
                fw.dma('sp', [(xt[:, 0:J, :], src[r0:r0 + J * 128, :].rearrange("(j p) d -> p j d", p=128))],
                       reads=[src], writes=[xt])
                ss = self.ss.next()
                rs = self.rs.next()
                for j in range(J):
                    fw.op('act', lambda a, j=j: a.activation(out=self.junk[:], in_=xt[:, j, :], func=AF.Square,
                                                            accum_out=ss[:, j:j + 1]),
                          reads=[xt], writes=[self.junk, ss])
                fw.op('act', lambda a: a.activation(out=ss[:, 0:J], in_=ss[:, 0:J], func=AF.Sqrt, bias=EPS,
                                                    scale=1.0 / D), reads=[ss], writes=[ss])
                fw.op('dve', lambda v: v.reciprocal(out=rs[:, 0:J], in_=ss[:, 0:J]), reads=[ss], writes=[rs])
                hT = self.hT.next()
                for j in range(J):
                    t32 = self.t32.next()
                    hb = self.hb.next()
                    fw.op('dve', lambda v, j=j, t32=t32: v.scalar_tensor_tensor(
                        out=t32[:], in0=xt[:, j, :], scalar=rs[:, j:j + 1], in1=sc1[:], op0=ALU.mult, op1=ALU.mult),
                        reads=[xt, rs, sc1], writes=[t32])
                    fw.op('pool', lambda g, t32=t32, hb=hb: g.tensor_tensor(out=hb[:], in0=t32[:], in1=sh[:], op=ALU.add),
                          reads=[t32, sh], writes=[hb])
                    p = self.tp.next()
                    for k in range(KC):
                        fw.op('pe', lambda t, k=k, p=p, hb=hb: t.transpose(out=p[:, k * 128:(k + 1) * 128],
                                                                          in_=hb[:, k * 128:(k + 1) * 128],
                                                                          identity=ident[:]),
                              reads=[hb, ident], writes=[p], inc=(k == KC - 1))
                    fw.op('act', lambda a, j=j, p=p: a.copy(out=hT[:, :, j * 128:(j + 1) * 128],
                                                          in_=p[:].rearrange("p (k t) -> p k t", k=KC)),
                          reads=[p], writes=[hT])
                return hT

        def grp_rstd(src_ap, sq, ssum, rstd, n, Dh, rd):
            sqv = sq[:, 0:n * Dh].rearrange("p (n d) -> p n d", d=Dh)
            fw.op('dve', lambda v: v.tensor_tensor(out=sqv, in0=src_ap, in1=src_ap, op=ALU.mult),
                  reads=rd, writes=[sq])
            fw.op('dve', lambda v: v.tensor_reduce(out=ssum[:, 0:n], in_=sqv, axis=AX.X, op=ALU.add),
                  reads=[sq], writes=[ssum])
            fw.op('act', lambda a: a.activation(out=ssum[:, 0:n], in_=ssum[:, 0:n], func=AF.Sqrt, bias=EPS,
                                                scale=1.0 / Dh), reads=[ssum], writes=[ssum])
            fw.op('dve', lambda v: v.reciprocal(out=rstd[:, 0:n], in_=ssum[:, 0:n]), reads=[ssum], writes=[rstd])

        def rope(y, J, G, Dh, o, R, tab, cofs, tmp1, tmp2):
            q4 = R // 4
            yr = y[:, 0:J, :, o:o + R]
            C = tab[:, 0:J, cofs:cofs + R].unsqueeze(2).to_broadcast([128, J, G, R])
            t1 = tmp1[:, 0:J * G * R].rearrange("p (j g r) -> p j g r", j=J, g=G)
            t2 = tmp2[:, 0:J * G * R].rearrange("p (j g r) -> p j g r", j=J, g=G)
            fw.op('dve', lambda v: v.tensor_tensor(out=t1, in0=yr, in1=C, op=ALU.mult), reads=[y, tab], writes=[tmp1])
            for a in range(2):
                for hf in range(2):
                    dst = t2[:, :, :, a * 2 * q4 + hf * q4: a * 2 * q4 + (hf + 1) * q4]
                    srcv = y[:, 0:J, :, o + a * 2 * q4 + (1 - hf) * q4: o + a * 2 * q4 + (2 - hf) * q4]
                    sn = tab[:, 0:J, cofs + R + a * 2 * q4 + hf * q4: cofs + R + a * 2 * q4 + (hf + 1) * q4] \
                        .unsqueeze(2).to_broadcast([128, J, G, q4])
                    fw.op('dve', lambda v, dst=dst, srcv=srcv, sn=sn: v.tensor_tensor(out=dst, in0=srcv, in1=sn, op=ALU.mult),
                          reads=[y, tab], writes=[tmp2])
            fw.op('dve', lambda v: v.tensor_tensor(out=yr, in0=t1, in1=t2, op=ALU.add),
                  reads=[tmp1, tmp2], writes=[y])

        def phase_kv0():
            with ExitStack() as st:
                tp = slots(st, "tp", 2, [128, 1024], BF16, psum=True)
                pj = slots(st, "pj", 2, [128, 1024], F32, psum=True)
                fr = Front(st, 4, tp)
                stage = slots(st, "wst", 2, [128, 1024], F32)
                gck = sb(st, "gck", [128, 2], F32)
                fw.dma('sp', [(gck[:], ckv_gain[:, :])], writes=[gck])
                wkv = load_w_bf16(st, "wkv", lambda k: wkv0[k * 128:(k + 1) * 128, :], 544, stage)
                wuk = load_w_bf16(st, "wuk", lambda k: w_ukv[k * 128:(k + 1) * 128, :], 1024, stage, gain=gck, kchunks=2)
                gbc = sb(st, "gbc", [128, 320], F32)
                fw.dma('sp', [(gbc[:], gains0[0:1, :].partition_broadcast(128))], writes=[gbc])
                bc = {0: (load_bc(st, "sc1t", 1), load_bc(st, "sht", 0)), 1: (load_bc(st, "sc1c", 4), load_bc(st, "shc", 3))}
                tabs = slots(st, "tab", 2, [128, 4, 192], F32)
                stp = slots(st, "stp", 2, [128, 4, 544], F32)
                sq = sb(st, "sq", [128, 3072], F32)
                ssum = slots(st, "ssum", 4, [128, 32], F32)
                rstd = slots(st, "rstd", 4, [128, 32], F32)
                ckn = slots(st, "ckn", 2, [128, 4, 256], BF16)
                cTt = slots(st, "cTt", 2, [128, 2, 512], BF16)
                stK = slots(st, "stK", 2, [128, 4, 8, 96], F32)
                Kn = slots(st, "Kn", 1, [128, 4, 8, 96], BF16)
                Vst = slots(st, "Vst", 2, [128, 8, 4, 64], BF16)
                kTst = slots(st, "kTst", 2, [96, 8, 512], BF16)
                gkn = slots(st, "gkn", 1, [128, 4, 2, 64], F32)
                gkb = slots(st, "gkb", 2, [128, 4, 128], BF16)
                Vbst = slots(st, "Vbst", 2, [128, 2, 4, 64], BF16)
                kTbst = slots(st, "kTbst", 2, [128, 512], BF16)
                tmp1 = sq
                tmp2 = sb(st, "tmp2", [128, 1024], F32)

                units = [(ctxb, 0, 2, 1, None, 0)]
                for u in range(S // 512):
                    units.append((xkv, u * 512, 4, 0, u * 512, 2 + u * 4))
                def body(unit):
                    (src, r0, J, who, rp0, t0) = unit
                    sc1, sh = bc[who]
                    hT = fr.run(src, r0, J, sc1, sh)
                    tab = None
                    if rp0 is not None:
                        tab = tabs.next()
                        fw.dma('sp', [(tab[:, 0:J, :], ropek[rp0:rp0 + J * 128, :].rearrange("(j p) c -> p j c", p=128))],
                               writes=[tab])
                    yield
                    s_ = stp.next()
                    for j in range(J):
                        p = pj.next()
                        for k in range(KC):
                            fw.op('pe', lambda t, k=k, p=p, j=j: t.matmul(p[:, 0:512], lhsT=hT[:, k, j * 128:(j + 1) * 128],
                                                                         rhs=wkv[:, k, 0:512], start=(k == 0),
                                                                         stop=(k == KC - 1)),
                                  reads=[hT, wkv], writes=[p], inc=False)
                        for k in range(KC):
                            fw.op('pe', lambda t, k=k, p=p, j=j: t.matmul(p[:, 512:544], lhsT=hT[:, k, j * 128:(j + 1) * 128],
                                                                         rhs=wkv[:, k, 512:544], start=(k == 0),
                                                                         stop=(k == KC - 1)),
                                  reads=[hT, wkv], writes=[p], inc=(k == KC - 1))
                        fw.op('act', lambda a, p=p, j=j: a.copy(out=s_[:, j, :], in_=p[:, 0:544]), reads=[p], writes=[s_])
                    sm = ssum.next()
                    rsd = rstd.next()
                    sqv = sq[:, 0:J * 256].rearrange("p (j c) -> p j c", j=J)
                    fw.op('dve', lambda v: v.tensor_tensor(out=sqv, in0=s_[:, 0:J, 0:256], in1=s_[:, 0:J, 0:256], op=ALU.mult),
                          reads=[s_], writes=[sq])
                    fw.op('dve', lambda v: v.tensor_reduce(out=sm[:, 0:J], in_=sqv, axis=AX.X, op=ALU.add),
                          reads=[sq], writes=[sm])
                    fw.op('act', lambda a: a.activation(out=sm[:, 0:J], in_=sm[:, 0:J], func=AF.Sqrt, bias=EPS,
                                                        scale=1.0 / 256), reads=[sm], writes=[sm])
                    fw.op('dve', lambda v: v.reciprocal(out=rsd[:, 0:J], in_=sm[:, 0:J]), reads=[sm], writes=[rsd])
                    cn = ckn.next()
                    fw.op('dve', lambda v: v.tensor_tensor(out=cn[:, 0:J, :], in0=s_[:, 0:J, 0:256],
                                                           in1=rsd[:, 0:J].unsqueeze(2).to_broadcast([128, J, 256]),
                                                           op=ALU.mult), reads=[s_, rsd], writes=[cn])
                    ct = cTt.next()
                    for j in range(J):
                        p = tp.next()
                        for k2 in range(2):
                            fw.op('pe', lambda t, k2=k2, p=p, j=j: t.transpose(out=p[:, k2 * 128:(k2 + 1) * 128],
                                                                              in_=cn[:, j, k2 * 128:(k2 + 1) * 128],
                                                                              identity=ident[:]),
                                  reads=[cn, ident], writes=[p], inc=(k2 == 1))
                        fw.op('act', lambda a, p=p, j=j: a.copy(out=ct[:, :, j * 128:(j + 1) * 128],
                                                              in_=p[:, 0:256].rearrange("p (k t) -> p k t", k=2)),
                              reads=[p], writes=[ct])
                    sk = stK.next()
                    vs = Vst.next()
                    for j in range(J):
                        p = pj.next()
                        for n in range(2):
                            for k2 in range(2):
                                fw.op('pe', lambda t, k2=k2, n=n, p=p, j=j: t.matmul(
                                    p[:, n * 512:(n + 1) * 512], lhsT=ct[:, k2, j * 128:(j + 1) * 128],
                                    rhs=wuk[:, k2, n * 512:(n + 1) * 512], start=(k2 == 0), stop=(k2 == 1)),
                                    reads=[ct, wuk], writes=[p], inc=(n == 1 and k2 == 1))
                        pv_ = p[:].rearrange("p (h c) -> p h c", h=8)
                        fw.op('act', lambda a, j=j, pv_=pv_: a.copy(out=sk[:, j, :, 0:64], in_=pv_[:, :, 0:64]),
                              reads=[p], writes=[sk])
                        fw.op('dve', lambda v, j=j, pv_=pv_: v.tensor_copy(out=vs[:, :, j, :], in_=pv_[:, :, 64:128]),
                              reads=[p], writes=[vs])
                    fw.op('pool', lambda g: g.tensor_copy(out=sk[:, 0:J, :, 64:96],
                                                          in_=s_[:, 0:J, 512:544].unsqueeze(2).to_broadcast([128, J, 8, 32])),
                          reads=[s_], writes=[sk])
                    yield
                    sm = ssum.next()
                    rsd = rstd.next()
                    skv = sk[:, 0:J, :, :].rearrange("p j h c -> p (j h) c")
                    grp_rstd(skv, sq, sm, rsd, J * 8, 96, [sk])
                    fw.op('dve', lambda v: v.tensor_tensor(out=skv, in0=skv,
                                                           in1=rsd[:, 0:J * 8].unsqueeze(2).to_broadcast([128, J * 8, 96]),
                                                           op=ALU.mult), reads=[sk, rsd], writes=[sk])
                    fw.op('dve', lambda v: v.tensor_tensor(out=skv, in0=skv,
                                                           in1=gbc[:, 96:192].unsqueeze(1).to_broadcast([128, J * 8, 96]),
                                                           op=ALU.mult), reads=[sk, gbc], writes=[sk])
                    if tab is not None:
                        rope(sk, J, 8, 96, 64, 32, tab, 0, tmp1, tmp2)
                    kn = Kn.next()
                    fw.op('pool', lambda g: g.tensor_copy(out=kn[:, 0:J], in_=sk[:, 0:J]), reads=[sk], writes=[kn])
                    kt = kTst.next()
                    for j in range(J):
                        p = tp.next()
                        for h in range(8):
                            fw.op('pe', lambda t, h=h, p=p, j=j: t.transpose(out=p[0:96, h * 128:(h + 1) * 128],
                                                                            in_=kn[:, j, h, :], identity=ident[:]),
                                  reads=[kn, ident], writes=[p], inc=(h == 7))
                        fw.op('act', lambda a, p=p, j=j: a.copy(out=kt[:, :, j * 128:(j + 1) * 128],
                                                              in_=p[0:96, :].rearrange("p (h t) -> p h t", h=8)),
                              reads=[p], writes=[kt])
                    k0 = t0 * 128
                    fw.dma('pool', [(kTa[:, :, k0:k0 + J * 128].rearrange("h d t -> d h t"), kt[:, :, 0:J * 128])],
                           reads=[kt], accw=[kTa])
                    fw.dma('pool', [(vA[:, :, t0:t0 + J, :].rearrange("h p j d -> p h (j d)"), vs[:, :, 0:J, :].rearrange("p h j d -> p h (j d)"))], reads=[vs], accw=[vA])
                    gk = gkn.next()
                    fw.op('pool', lambda g: g.tensor_copy(out=gk[:, 0:J].rearrange("p j g c -> p j (g c)"),
                                                          in_=s_[:, 0:J, 256:384]), reads=[s_], writes=[gk])
                    vb = Vbst.next()
                    fw.op('pool', lambda g: g.tensor_copy(out=vb[:, :, 0:J, :].rearrange("p g j c -> p j g c"),
                                                          in_=s_[:, 0:J, 384:512].rearrange("p j (g c) -> p j g c", g=2)),
                          reads=[s_], writes=[vb])
                    sm = ssum.next()
                    rsd = rstd.next()
                    gkv = gk[:, 0:J, :, :].rearrange("p j g c -> p (j g) c")
                    grp_rstd(gkv, sq, sm, rsd, J * 2, 64, [gk])
                    fw.op('dve', lambda v: v.tensor_tensor(out=gkv, in0=gkv,
                                                           in1=rsd[:, 0:J * 2].unsqueeze(2).to_broadcast([128, J * 2, 64]),
                                                           op=ALU.mult), reads=[gk, rsd], writes=[gk])
                    fw.op('dve', lambda v: v.tensor_tensor(out=gkv, in0=gkv,
                                                           in1=gbc[:, 256:320].unsqueeze(1).to_broadcast([128, J * 2, 64]),
                                                           op=ALU.mult), reads=[gk, gbc], writes=[gk])
                    if tab is not None:
                        rope(gk, J, 2, 64, 0, 64, tab, 64, tmp1, tmp2)
                    gb_ = gkb.next()
                    fw.op('pool', lambda g: g.tensor_copy(out=gb_[:, 0:J, :], in_=gk[:, 0:J].rearrange("p j g c -> p j (g c)")),
                          reads=[gk], writes=[gb_])
                    ktb = kTbst.next()
                    p = tp.next()
                    for j in range(J):
                        fw.op('pe', lambda t, p=p, j=j: t.transpose(out=p[:, j * 128:(j + 1) * 128], in_=gb_[:, j, :],
                                                                   identity=ident[:]),
                              reads=[gb_, ident], writes=[p], inc=(j == J - 1))
                    fw.op('act', lambda a, p=p: a.copy(out=ktb[:, 0:J * 128], in_=p[:, 0:J * 128]), reads=[p], writes=[ktb])
                    fw.dma('pool', [(kTb[g, :, k0:k0 + J * 128], ktb[g * 64:(g + 1) * 64, 0:J * 128]) for g in range(2)],
                           reads=[ktb], accw=[kTb])
                    fw.dma('pool', [(vB[:, :, t0:t0 + J, :].rearrange("g p j d -> p g (j d)"), vb[:, :, 0:J, :].rearrange("p g j d -> p g (j d)"))], reads=[vb], accw=[vB])
                run_pipelined(body, units)
                fw.barrier()

        def phase_q0(src, units, who, rtab, qTa_d, qTb_d, gT_d):
            with ExitStack() as st:
                tp = slots(st, "tp", 2, [128, 1024], BF16, psum=True)
                pj = slots(st, "pj", 2, [128, 1024], F32, psum=True)
                fm = slots(st, "fm", 2, [128, 512], F32, psum=True)
                fr = Front(st, 2, tp)
                stage = slots(st, "wst", 2, [128, 1024], F32)
                gcq = sb(st, "gcq", [128, 2], F32)
                fw.dma('sp', [(gcq[:], cq_gain[:, :])], writes=[gcq])
                wq = load_w_bf16(st, "wq", lambda k: wq0[k * 128:(k + 1) * 128, :], 768, stage)
                wg = load_w_bf16(st, "wg", lambda k: wg0[k * 128:(k + 1) * 128, :], 1024, stage)
                wuq = load_w_bf16(st, "wuq", lambda k: w_uq[k * 128:(k + 1) * 128, :], 768, stage, gain=gcq, kchunks=2)
                gbc = sb(st, "gbc", [128, 320], F32)
                fw.dma('sp', [(gbc[:], gains0[0:1, :].partition_broadcast(128))], writes=[gbc])
                fw.op('dve', lambda v: v.tensor_scalar(out=gbc[:, 0:96], in0=gbc[:, 0:96], scalar1=96.0 ** -0.5,
                                                       scalar2=None, op0=ALU.mult), reads=[gbc], writes=[gbc])
                fw.op('dve', lambda v: v.tensor_scalar(out=gbc[:, 192:256], in0=gbc[:, 192:256], scalar1=64.0 ** -0.5,
                                                       scalar2=None, op0=ALU.mult), reads=[gbc], writes=[gbc])
                sc1 = load_bc(st, "sc1", who * 3 + 1)
                sh = load_bc(st, "sh", who * 3 + 0)
                tabs = slots(st, "tab", 2, [128, 2, 192], F32)
                stp = slots(st, "stp", 2, [128, 2, 768], F32)
                sq = sb(st, "sq", [128, 1536], F32)
                ssum = slots(st, "ssum", 4, [128, 32], F32)
                rstd = slots(st, "rstd", 4, [128, 32], F32)
                cqn = slots(st, "cqn", 2, [128, 2, 256], BF16)
                cTt = slots(st, "cTt", 2, [128, 2, 256], BF16)
                stQ = slots(st, "stQ", 2, [128, 2, 8, 96], F32)
                Qn = slots(st, "Qn", 2, [128, 2, 8, 96], BF16)
                qTst = slots(st, "qTst", 2, [96, 8, 256], BF16)
                gqn = slots(st, "gqn", 2, [128, 2, 8, 64], F32)
                gqb = slots(st, "gqb", 2, [128, 2, 512], BF16)
                qTbst = slots(st, "qTbst", 2, [128, 4, 256], BF16)
                gTst = slots(st, "gTst", 2, [128, 8, 256], BF16)
                tmp1 = sq
                tmp2 = sb(st, "tmp2", [128, 1024], F32)
                def body(unit):
                    (r0, J) = unit
                    hT = fr.run(src, r0, J, sc1, sh)
                    tab = None
                    if rtab is not None:
                        tab = tabs.next()
                        fw.dma('sp', [(tab[:, 0:J, :], rtab[r0:r0 + J * 128, :].rearrange("(j p) c -> p j c", p=128))],
                               writes=[tab])
                    yield
                    s_ = stp.next()
                    for j in range(J):
                        p = pj.next()
                        for (c0, c1) in ((0, 512), (512, 768)):
                            for k in range(KC):
                                fw.op('pe', lambda t, k=k, p=p, j=j, c0=c0, c1=c1: t.matmul(
                                    p[:, c0:c1], lhsT=hT[:, k, j * 128:(j + 1) * 128], rhs=wq[:, k, c0:c1],
                                    start=(k == 0), stop=(k == KC - 1)),
                                    reads=[hT, wq], writes=[p], inc=(c0 == 512 and k == KC - 1))
                        fw.op('act', lambda a, p=p, j=j: a.copy(out=s_[:, j, :], in_=p[:, 0:768]), reads=[p], writes=[s_])
                    gt = gTst.next()
                    for c in range(8):
                        p = fm.next()
                        for k in range(KC):
                            fw.op('pe', lambda t, k=k, p=p, c=c: t.matmul(p[:, 0:J * 128], lhsT=wg[:, k, c * 128:(c + 1) * 128],
                                                                         rhs=hT[:, k, 0:J * 128], start=(k == 0),
                                                                         stop=(k == KC - 1)),
                                  reads=[hT, wg], writes=[p], inc=(k == KC - 1))
                        fw.op('act', lambda a, p=p, c=c: a.activation(out=gt[:, c, 0:J * 128], in_=p[:, 0:J * 128], func=AF.Silu),
                              reads=[p], writes=[gt])
                    fw.dma('pool', [(gT_d[:, :, r0:r0 + J * 128].rearrange("c p t -> p c t"), gt[:, :, 0:J * 128])],
                           reads=[gt], accw=[gT_d])
                    sm = ssum.next()
                    rsd = rstd.next()
                    sqv = sq[:, 0:J * 256].rearrange("p (j c) -> p j c", j=J)
                    fw.op('dve', lambda v: v.tensor_tensor(out=sqv, in0=s_[:, 0:J, 0:256], in1=s_[:, 0:J, 0:256], op=ALU.mult),
                          reads=[s_], writes=[sq])
                    fw.op('dve', lambda v: v.tensor_reduce(out=sm[:, 0:J], in_=sqv, axis=AX.X, op=ALU.add),
                          reads=[sq], writes=[sm])
                    fw.op('act', lambda a: a.activation(out=sm[:, 0:J], in_=sm[:, 0:J], func=AF.Sqrt, bias=EPS,
                                                        scale=1.0 / 256), reads=[sm], writes=[sm])
                    fw.op('dve', lambda v: v.reciprocal(out=rsd[:, 0:J], in_=sm[:, 0:J]), reads=[sm], writes=[rsd])
                    cn = cqn.next()
                    fw.op('dve', lambda v: v.tensor_tensor(out=cn[:, 0:J, :], in0=s_[:, 0:J, 0:256],
                                                           in1=rsd[:, 0:J].unsqueeze(2).to_broadcast([128, J, 256]),
                                                           op=ALU.mult), reads=[s_, rsd], writes=[cn])
                    ct = cTt.next()
                    for j in range(J):
                        p = tp.next()
                        for k2 in range(2):
                            fw.op('pe', lambda t, k2=k2, p=p, j=j: t.transpose(out=p[:, k2 * 128:(k2 + 1) * 128],
                                                                              in_=cn[:, j, k2 * 128:(k2 + 1) * 128],
                                                                              identity=ident[:]),
                                  reads=[cn, ident], writes=[p], inc=(k2 == 1))
                        fw.op('act', lambda a, p=p, j=j: a.copy(out=ct[:, :, j * 128:(j + 1) * 128],
                                                              in_=p[:, 0:256].rearrange("p (k t) -> p k t", k=2)),
                              reads=[p], writes=[ct])
                    sQ = stQ.next()
                    for j in range(J):
                        p = pj.next()
                        for n in range(2):
                            for k2 in range(2):
                                fw.op('pe', lambda t, k2=k2, n=n, p=p, j=j: t.matmul(
                                    p[:, n * 512:n * 512 + 384], lhsT=ct[:, k2, j * 128:(j + 1) * 128],
                                    rhs=wuq[:, k2, n * 384:(n + 1) * 384], start=(k2 == 0), stop=(k2 == 1)),
                                    reads=[ct, wuq], writes=[p], inc=(n == 1 and k2 == 1))
                        fw.op('act', lambda a, j=j, p=p: a.copy(
                            out=sQ[:, j, :, :].rearrange("p (a h) c -> p a (h c)", a=2),
                            in_=p[:].rearrange("p (a n) -> p a n", a=2)[:, :, 0:384]), reads=[p], writes=[sQ])
                    yield
                    sm = ssum.next()
                    rsd = rstd.next()
                    sQv = sQ[:, 0:J, :, :].rearrange("p j h c -> p (j h) c")
                    grp_rstd(sQv, sq, sm, rsd, J * 8, 96, [sQ])
                    fw.op('dve', lambda v: v.tensor_tensor(out=sQv, in0=sQv,
                                                           in1=rsd[:, 0:J * 8].unsqueeze(2).to_broadcast([128, J * 8, 96]),
                                                           op=ALU.mult), reads=[sQ, rsd], writes=[sQ])
                    fw.op('dve', lambda v: v.tensor_tensor(out=sQv, in0=sQv,
                                                           in1=gbc[:, 0:96].unsqueeze(1).to_broadcast([128, J * 8, 96]),
                                                           op=ALU.mult), reads=[sQ, gbc], writes=[sQ])
                    if tab is not None:
                        rope(sQ, J, 8, 96, 64, 32, tab, 0, tmp1, tmp2)
                    qn = Qn.next()
                    fw.op('pool', lambda g: g.tensor_copy(out=qn[:, 0:J], in_=sQ[:, 0:J]), reads=[sQ], writes=[qn])
                    qt = qTst.next()
                    for j in range(J):
                        p = tp.next()
                        for h in range(8):
                            fw.op('pe', lambda t, h=h, p=p, j=j: t.transpose(out=p[0:96, h * 128:(h + 1) * 128],
                                                                            in_=qn[:, j, h, :], identity=ident[:]),
                                  reads=[qn, ident], writes=[p], inc=(h == 7))
                        fw.op('act', lambda a, p=p, j=j: a.copy(out=qt[:, :, j * 128:(j + 1) * 128],
                                                              in_=p[0:96, :].rearrange("p (h t) -> p h t", h=8)),
                              reads=[p], writes=[qt])
                    fw.dma('pool', [(qTa_d[:, :, r0:r0 + J * 128].rearrange("h d t -> d h t"), qt[:, :, 0:J * 128])],
                           reads=[qt], accw=[qTa_d])
                    gq = gqn.next()
                    fw.op('pool', lambda g: g.tensor_copy(out=gq[:, 0:J].rearrange("p j h c -> p j (h c)"),
                                                          in_=s_[:, 0:J, 256:768]), reads=[s_], writes=[gq])
                    sm = ssum.next()
                    rsd = rstd.next()
                    gqv = gq[:, 0:J, :, :].rearrange("p j h c -> p (j h) c")
                    grp_rstd(gqv, sq, sm, rsd, J * 8, 64, [gq])
                    fw.op('dve', lambda v: v.tensor_tensor(out=gqv, in0=gqv,
                                                           in1=rsd[:, 0:J * 8].unsqueeze(2).to_broadcast([128, J * 8, 64]),
                                                           op=ALU.mult), reads=[gq, rsd], writes=[gq])
                    fw.op('dve', lambda v: v.tensor_tensor(out=gqv, in0=gqv,
                                                           in1=gbc[:, 192:256].unsqueeze(1).to_broadcast([128, J * 8, 64]),
                                                           op=ALU.mult), reads=[gq, gbc], writes=[gq])
                    if tab is not None:
                        rope(gq, J, 8, 64, 0, 64, tab, 64, tmp1, tmp2)
                    gb_ = gqb.next()
                    fw.op('pool', lambda g: g.tensor_copy(out=gb_[:, 0:J, :], in_=gq[:, 0:J].rearrange("p j h c -> p j (h c)")),
                          reads=[gq], writes=[gb_])
                    qtb = qTbst.next()
                    for j in range(J):
                        p = tp.next()
                        for m in range(4):
                            fw.op('pe', lambda t, m=m, p=p, j=j: t.transpose(out=p[:, m * 128:(m + 1) * 128],
                                                                            in_=gb_[:, j, m * 128:(m + 1) * 128],
                                                                            identity=ident[:]),
                                  reads=[gb_, ident], writes=[p], inc=(m == 3))
                        fw.op('act', lambda a, p=p, j=j: a.copy(out=qtb[:, :, j * 128:(j + 1) * 128],
                                                              in_=p[:, 0:512].rearrange("p (m t) -> p m t", m=4)),
                              reads=[p], writes=[qtb])
                    fw.dma('pool', [(qTb_d[:, :, r0:r0 + J * 128].rearrange("(m two) d t -> (two d) m t", two=2),
                                     qtb[:, :, 0:J * 128])], reads=[qtb], accw=[qTb_d])
                run_pipelined(body, units)
                fw.barrier()

        def phase_attn0(T, kt0, kt1, qTa_d, qTb_d, gT_d, mix_d):
            nkt = kt1 - kt0
            with ExitStack() as st:
                pss = slots(st, "pss", 2, [128, 3, 512], F32, psum=True)
                pso = slots(st, "pso", 2, [128, 512], F32, psum=True)
                kT = sb(st, "kT", [128, nkt * 128], BF16)
                Ve = sb(st, "Ve", [128, nkt, 128], BF16)
                Vo = sb(st, "Vo", [128, nkt, 128], BF16)
                qTs = slots(st, "qTs", 2, [128, T], BF16)
                gTs = slots(st, "gTs", 2, [128, T], BF16)
                mxs = slots(st, "mxs", 2, [128, T], BF16)
                pT = slots(st, "pT", 3, [128, 3, 512], BF16)
                Rs = slots(st, "Rs", 2, [128, 512], F32)
                t32 = slots(st, "t32", 2, [128, 512], F32)
                fw.op('pool', lambda g: g.memset(Ve[:, :, 64:128], 1.0), writes=[Ve])
                fw.op('pool', lambda g: g.memset(Vo[:, :, 0:64], 1.0), writes=[Vo])
                qsup = []
                q0 = 0
                while q0 < T:
                    wq_ = min(512, T - q0)
                    qsup.append((q0, wq_))
                    q0 += wq_
                for c in range(8):
                    gt = gTs.next()
                    fw.dma('sp', [(gt[:], gT_d[c, :, :])], reads=[gT_d], writes=[gt])
                    mx = mxs.next()
                    for half in range(2):
                        hh = 2 * c + half
                        if c < 4:
                            d = 96
                            ksrc, vsrc, qsrc = kTa[hh], vA[hh], qTa_d[hh]
                            newk = True
                            newv = True
                        else:
                            d = 64
                            qh = hh - 8
                            g_ = qh // 4
                            ksrc, vsrc, qsrc = kTb[g_], vB[g_], qTb_d[qh]
                            newk = (qh % 4 == 0)
                            newv = (qh % 4 < 2)
                        V = Ve if half == 0 else Vo
                        vcol = 0 if half == 0 else 64
                        if newk:
                            nsp = 4 if nkt >= 8 else 1
                            stp_ = (nkt * 128) // nsp
                            for i in range(nsp):
                                fw.dma('sp', [(kT[0:d, i * stp_:(i + 1) * stp_], ksrc[:, kt0 * 128 + i * stp_: kt0 * 128 + (i + 1) * stp_])],
                                       reads=[kTa, kTb], writes=[kT] if i == 0 else (), accw=() if i == 0 else [kT])
                        if newv:
                            fw.dma('sp', [(V[:, :, vcol:vcol + 64], vsrc[:, kt0:kt1, :])], reads=[vA, vB], writes=[V])
                        qT = qTs.next()
                        fw.dma('sp', [(qT[0:d, :], qsrc[:, :])], reads=[qTa_d, qTb_d], writes=[qT])
                        for (q0, wq_) in qsup:
                            po = pso.next()
                            pairs = [(i, min(3, nkt - i)) for i in range(0, nkt, 3)]

                            def qk(pi):
                                i0, n = pairs[pi]
                                p = pss.next()
                                for i in range(n):
                                    fw.op('pe', lambda t, i=i, p=p: t.matmul(
                                        p[:, i, 0:wq_], lhsT=kT[0:d, (i0 + i) * 128:(i0 + i + 1) * 128],
                                        rhs=qT[0:d, q0:q0 + wq_], start=True, stop=True),
                                        reads=[kT, qT], writes=[p], inc=(i == n - 1))
                                return p

                            pend = qk(0)
                            for pi in range(len(pairs)):
                                i0, n = pairs[pi]
                                p = pend
                                pt = pT.next()
                                fw.op('act', lambda a, p=p, pt=pt, n=n: a.activation(out=pt[:, 0:n, 0:wq_], in_=p[:, 0:n, 0:wq_],
                                                                                    func=AF.Exp), reads=[p], writes=[pt])
                                if pi + 1 < len(pairs):
                                    pend = qk(pi + 1)
                                for i in range(n):
                                    kt_ = i0 + i
                                    fw.op('pe', lambda t, i=i, pt=pt, kt_=kt_: t.matmul(
                                        po[:, 0:wq_], lhsT=V[:, kt_, :], rhs=pt[:, i, 0:wq_],
                                        start=(kt_ == 0), stop=(kt_ == nkt - 1)),
                                        reads=[V, pt], writes=[po], inc=(i == n - 1))
                            R = Rs.next()
                            tt = t32.next()
                            o0, s0 = (0, 64) if half == 0 else (64, 0)
                            fw.op('dve', lambda v, R=R: v.reciprocal(out=R[o0:o0 + 64, 0:wq_], in_=po[s0:s0 + 64, 0:wq_]),
                                  reads=[po], writes=[R])
                            fw.op('dve', lambda v, R=R, tt=tt: v.tensor_tensor(out=tt[o0:o0 + 64, 0:wq_], in0=po[o0:o0 + 64, 0:wq_],
                                                                               in1=R[o0:o0 + 64, 0:wq_], op=ALU.mult),
                                  reads=[po, R], writes=[tt])
                            fw.op('pool', lambda g, tt=tt: g.tensor_tensor(out=mx[o0:o0 + 64, q0:q0 + wq_],
                                                                          in0=tt[o0:o0 + 64, 0:wq_],
                                                                          in1=gt[o0:o0 + 64, q0:q0 + wq_], op=ALU.mult),
                                  reads=[tt, gt], writes=[mx])
                    fw.dma('pool', [(mix_d[c, :, :], mx[:])], reads=[mx], accw=[mix_d])
                fw.barrier()

        def phase_out(src, units, mix_d, wout_d, gate_idx, dst, dst_r0=None, st_outer=None):
            with ExitStack() as st:
                pj = slots(st, "pj", 2, [128, 1024], F32, psum=True)
                stage = slots(st, "wst", 2, [128, 1024], F32)
                wo = load_w_bf16(st, "wo", lambda k: wout_d[k * 128:(k + 1) * 128, :], 1024, stage)
                gbc_ = load_bc(st, "gatebc", gate_idx)
                mxs = slots(st, "mxs", 2, [128, 8, 512], BF16)
                xts = slots(st, "xts", 2, [128, 4, D], F32)
                t32 = slots(st, "t32", 2, [128, D], F32)
                xo = slots(st, "xo", 2, [128, 4, D], F32)
                for (r0, J) in units:
                    mx = mxs.next()
                    fw.dma('sp', [(mx[:, :, 0:J * 128], mix_d[:, :, r0:r0 + J * 128].rearrange("c p t -> p c t"))],
                           reads=[mix_d], writes=[mx])
                    xt = xts.next()
                    fw.dma('sp', [(xt[:, 0:J, :], src[r0:r0 + J * 128, :].rearrange("(j p) d -> p j d", p=128))],
                           reads=[src], writes=[xt])
                    xo_ = xo.next()
                    for j in range(J):
                        p = pj.next()
                        for n in range(2):
                            for k in range(KC):
                                fw.op('pe', lambda t, k=k, n=n, p=p, j=j: t.matmul(
                                    p[:, n * 512:(n + 1) * 512], lhsT=mx[:, k, j * 128:(j + 1) * 128],
                                    rhs=wo[:, k, n * 512:(n + 1) * 512], start=(k == 0), stop=(k == KC - 1)),
                                    reads=[mx, wo], writes=[p], inc=(n == 1 and k == KC - 1))
                        tt = t32.next()
                        fw.op('dve', lambda v, p=p, tt=tt: v.tensor_tensor(out=tt[:], in0=p[:], in1=gbc_[:], op=ALU.mult),
                              reads=[p, gbc_], writes=[tt])
                        fw.op('pool', lambda g, tt=tt, j=j: g.tensor_tensor(out=xo_[:, j, :], in0=tt[:], in1=xt[:, j, :], op=ALU.add),
                              reads=[tt, xt], writes=[xo_])
                    d0 = r0 if dst_r0 is None else r0 - dst_r0
                    fw.dma('pool', [(dst[d0:d0 + J * 128, :].rearrange("(j p) d -> p j d", p=128), xo_[:, 0:J, :])],
                           reads=[xo_], accw=[dst])
                fw.barrier()

        units_q = [(u * 512, 4) for u in range(T_q // 512)]
        if T_q % 512:
            units_q.append(((T_q // 512) * 512, (T_q % 512) // 128))
        units_q2 = [(u * 256, 2) for u in range(T_q // 256)]
        chk('setup')
        phase_kv0()
        chk('kv0')
        phase_q0(xq, units_q2, 0, ropeq, qTa, qTb, gT0)
        phase_q0(ctxb, [(0, 2)], 1, None, qTa_c, qTb_c, gT0_c)
        chk('q0')
        phase_attn0(T_q, 0, NKT, qTa, qTb, gT0, mixT0)
        phase_attn0(L, 0, 2, qTa_c, qTb_c, gT0_c, mixT0_c)
        chk('attn0')
        phase_out(xq, units_q, mixT0, wout0, 2, x1)
        phase_out(ctxb, [(0, 2)], mixT0_c, wout0, 5, xc1)
        chk('out0')

        with ExitStack() as st:
            KT = sb(st, "KT", [128, T_q], BF16)
            VX = sb(st, "VX", [128, NQT, 2, 128], BF16)
            KTc = sb(st, "KTc", [128, L], BF16)
            VXc = sb(st, "VXc", [128, 2, 2, 128], BF16)
            g1bc = sb(st, "g1bc", [128, 128], F32)
            fw.dma('sp', [(g1bc[:], gains1[0:1, :].partition_broadcast(128))], writes=[g1bc])
            fw.op('dve', lambda v: v.tensor_scalar(out=g1bc[:, 0:64], in0=g1bc[:, 0:64], scalar1=64.0 ** -0.5,
                                                   scalar2=None, op0=ALU.mult), reads=[g1bc], writes=[g1bc])
            fw.op('pool', lambda g: g.memset(VXc[:, :, :, 64:128], 1.0), writes=[VXc])

            with ExitStack() as s2:
                tp = slots(s2, "tp", 2, [128, 1024], BF16, psum=True)
                pj = slots(s2, "pj", 2, [128, 1024], F32, psum=True)
                fm = slots(s2, "fm", 2, [128, 512], F32, psum=True)
                fr = Front(s2, 2, tp)
                stage = slots(s2, "wst", 2, [128, 1024], F32)
                wt = load_w_bf16(s2, "wt", lambda k: wt1[k * 128:(k + 1) * 128, :], 768, stage)
                wf = sb(s2, "wf", [128, KC, 2560], BF16)
                for k in range(KC):
                    for (c0, c1) in ((0, 1024), (1024, 2048), (2048, 2560)):
                        s_ = stage.next()
                        fw.dma('sp', [(s_[:, 0:c1 - c0], wf1[k * 128:(k + 1) * 128, c0:c1])], writes=[s_])
                        fw.op('pool', lambda g, k=k, s_=s_, c0=c0, c1=c1: g.tensor_copy(out=wf[:, k, c0:c1], in_=s_[:, 0:c1 - c0]),
                              reads=[s_], writes=[wf])
                bcs = {0: (load_bc(s2, "sc1t", 7), load_bc(s2, "sht", 6)), 1: (load_bc(s2, "sc1c", 10), load_bc(s2, "shc", 9))}
                tabs = slots(s2, "tab", 2, [128, 2, 192], F32)
                stp = slots(s2, "stp", 2, [128, 2, 768], F32)
                ssum = slots(s2, "ssum", 4, [128, 40], F32)
                rstd = slots(s2, "rstd", 4, [128, 40], F32)
                qkn = slots(s2, "qkn", 2, [128, 2, 10, 64], F32)
                qkb = slots(s2, "qkb", 2, [128, 2, 640], BF16)
                tmp1 = sb(s2, "tmp1", [128, 1280], F32)
                tmp2 = sb(s2, "tmp2", [128, 1280], F32)
                qst = slots(s2, "qst", 2, [128, 4, 256], BF16)
                ust = slots(s2, "ust", 2, [128, 4, 256], BF16)
                gcs = slots(s2, "gcs", 2, [128, 256], BF16)
                gbst = slots(s2, "gbst", 2, [128, 4, 256], BF16)
                sgst = slots(s2, "sgst", 2, [128, 8, 256], BF16)

                units1 = [(xc1, 0, 2, 1, None)] + [(x1, r0, J, 0, r0) for (r0, J) in units_q2]
                def body(unit):
                    (src, r0, J, who, rp0) = unit
                    sc1, sh = bcs[who]
                    hT = fr.run(src, r0, J, sc1, sh)
                    tab = None
                    if rp0 is not None:
                        tab = tabs.next()
                        fw.dma('sp', [(tab[:, 0:J, :], ropeq[rp0:rp0 + J * 128, :].rearrange("(j p) c -> p j c", p=128))],
                               writes=[tab])
                    yield
                    s_ = stp.next()
                    for j in range(J):
                        p = pj.next()
                        for (c0, c1) in ((0, 512), (512, 768)):
                            for k in range(KC):
                                fw.op('pe', lambda t, k=k, p=p, j=j, c0=c0, c1=c1: t.matmul(
                                    p[:, c0:c1], lhsT=hT[:, k, j * 128:(j + 1) * 128], rhs=wt[:, k, c0:c1],
                                    start=(k == 0), stop=(k == KC - 1)),
                                    reads=[hT, wt], writes=[p], inc=(c0 == 512 and k == KC - 1))
                        fw.op('act', lambda a, p=p, j=j: a.copy(out=s_[:, j, :], in_=p[:, 0:768]), reads=[p], writes=[s_])
                    qk = qkn.next()
                    fw.op('pool', lambda g: g.tensor_copy(out=qk[:, 0:J].rearrange("p j h c -> p j (h c)"), in_=s_[:, 0:J, 0:640]),
                          reads=[s_], writes=[qk])
                    sm = ssum.next()
                    rsd = rstd.next()
                    qkv = qk[:, 0:J, :, :].rearrange("p j h c -> p (j h) c")
                    sq2 = tmp1[:, 0:J * 640].rearrange("p (n c) -> p n c", c=64)
                    fw.op('dve', lambda v: v.tensor_tensor(out=sq2, in0=qkv, in1=qkv, op=ALU.mult), reads=[qk], writes=[tmp1])
                    fw.op('dve', lambda v: v.tensor_reduce(out=sm[:, 0:J * 10], in_=sq2, axis=AX.X, op=ALU.add),
                          reads=[tmp1], writes=[sm])
                    fw.op('act', lambda a: a.activation(out=sm[:, 0:J * 10], in_=sm[:, 0:J * 10], func=AF.Sqrt, bias=EPS,
                                                        scale=1.0 / 64), reads=[sm], writes=[sm])
                    fw.op('dve', lambda v: v.reciprocal(out=rsd[:, 0:J * 10], in_=sm[:, 0:J * 10]), reads=[sm], writes=[rsd])
                    fw.op('dve', lambda v: v.tensor_tensor(out=qkv, in0=qkv,
                                                           in1=rsd[:, 0:J * 10].unsqueeze(2).to_broadcast([128, J * 10, 64]),
                                                           op=ALU.mult), reads=[qk, rsd], writes=[qk])
                    fw.op('dve', lambda v: v.tensor_tensor(out=qk[:, 0:J, 0:8, :], in0=qk[:, 0:J, 0:8, :],
                                                           in1=g1bc[:, 0:64].unsqueeze(1).unsqueeze(1).to_broadcast([128, J, 8, 64]),
                                                           op=ALU.mult), reads=[qk, g1bc], writes=[qk])
                    fw.op('dve', lambda v: v.tensor_tensor(out=qk[:, 0:J, 8:10, :], in0=qk[:, 0:J, 8:10, :],
                                                           in1=g1bc[:, 64:128].unsqueeze(1).unsqueeze(1).to_broadcast([128, J, 2, 64]),
                                                           op=ALU.mult), reads=[qk, g1bc], writes=[qk])
                    if tab is not None:
                        rope(qk, J, 10, 64, 0, 64, tab, 64, tmp1, tmp2)
                    qb = qkb.next()
                    fw.op('pool', lambda g: g.tensor_copy(out=qb[:, 0:J, :], in_=qk[:, 0:J].rearrange("p j h c -> p j (h c)")),
                          reads=[qk], writes=[qb])
                    if who == 0:
                        tl0 = r0 // 128
                        for j in range(J):
                            tl = tl0 + j
                            fw.op('pool', lambda g, j=j, tl=tl: g.tensor_copy(
                                out=VX[:, tl, :, 0:64], in_=s_[:, j, 640:768].rearrange("p (g c) -> p g c", g=2)),
                                reads=[s_], writes=[VX])
                            fw.op('pool', lambda g, tl=tl: g.memset(VX[:, tl, :, 64:128], 1.0), reads=[], writes=[VX])
                            if tl == 0 or tl == NQT - 1:
                                vc = 0 if tl == 0 else 1
                                fw.op('dve', lambda v, tl=tl, vc=vc: v.tensor_scalar(
                                    out=VX[:, tl, :, :], in0=VX[:, tl, :, :], scalar1=validt[:, vc:vc + 1], scalar2=None,
                                    op0=ALU.mult), reads=[VX, validt], writes=[VX])
                    else:
                        for j in range(J):
                            fw.op('pool', lambda g, j=j: g.tensor_copy(
                                out=VXc[:, j, :, 0:64], in_=s_[:, j, 640:768].rearrange("p (g c) -> p g c", g=2)),
                                reads=[s_], writes=[VXc])
                    yield
                    qs = qst.next() if who == 0 else None
                    for j in range(J):
                        p = tp.next()
                        for m in range(4):
                            fw.op('pe', lambda t, m=m, p=p, j=j: t.transpose(
                                out=p[0:64, m * 128:(m + 1) * 128], in_=qb[:, j, m * 64:(m + 1) * 64], identity=ident[:]),
                                reads=[qb, ident], writes=[p], inc=False)
                            fw.op('pe', lambda t, m=m, p=p, j=j: t.transpose(
                                out=p[64:128, m * 128:(m + 1) * 128], in_=qb[:, j, (m + 4) * 64:(m + 5) * 64], identity=ident[:]),
                                reads=[qb, ident], writes=[p], inc=False)
                        fw.op('pe', lambda t, p=p, j=j: t.transpose(out=p[:, 512:640], in_=qb[:, j, 512:640], identity=ident[:]),
                              reads=[qb, ident], writes=[p], inc=True)
                        if who == 0:
                            c0 = r0 + j * 128
                            fw.op('act', lambda a, p=p, j=j: a.copy(out=qs[:, :, j * 128:(j + 1) * 128],
                                                                   in_=p[:, 0:512].rearrange("p (m t) -> p m t", m=4)),
                                  reads=[p], writes=[qs])
                            fw.op('act', lambda a, p=p, c0=c0: a.copy(out=KT[:, c0:c0 + 128], in_=p[:, 512:640]),
                                  reads=[p], writes=[KT])
                        else:
                            fw.op('act', lambda a, p=p, j=j: a.copy(out=KTc[:, j * 128:(j + 1) * 128], in_=p[:, 512:640]),
                                  reads=[p], writes=[KTc])
                    if who == 1:
                        return
                    fw.dma('pool', [(qT1[:, :, r0:r0 + J * 128].rearrange("m p t -> p m t"), qs[:, :, 0:J * 128])],
                           reads=[qs], accw=[qT1])
                    us = ust.next()
                    gb_ = gbst.next()
                    sg_ = sgst.next()
                    W_ = J * 128
                    for c in range(4):
                        p = fm.next()
                        for k in range(KC):
                            fw.op('pe', lambda t, k=k, p=p, c=c: t.matmul(p[:, 0:W_], lhsT=wf[:, k, c * 128:(c + 1) * 128],
                                                                         rhs=hT[:, k, 0:W_], start=(k == 0), stop=(k == KC - 1)),
                                  reads=[hT, wf], writes=[p], inc=(k == KC - 1))
                        fw.op('act', lambda a, p=p, c=c: a.copy(out=gb_[:, c, 0:W_], in_=p[:, 0:W_]), reads=[p], writes=[gb_])
                        p = fm.next()
                        for k in range(KC):
                            fw.op('pe', lambda t, k=k, p=p, c=c: t.matmul(p[:, 0:W_], lhsT=wf[:, k, 512 + c * 128:512 + (c + 1) * 128],
                                                                         rhs=hT[:, k, 0:W_], start=(k == 0), stop=(k == KC - 1)),
                                  reads=[hT, wf], writes=[p], inc=(k == KC - 1))
                        gc_ = gcs.next()
                        fw.op('act', lambda a, p=p, gc_=gc_: a.copy(out=gc_[:, 0:W_], in_=p[:, 0:W_]), reads=[p], writes=[gc_])
                        p = fm.next()
                        for k in range(KC):
                            fw.op('pe', lambda t, k=k, p=p, c=c: t.matmul(p[:, 0:W_], lhsT=wf[:, k, 1024 + c * 128:1024 + (c + 1) * 128],
                                                                         rhs=hT[:, k, 0:W_], start=(k == 0), stop=(k == KC - 1)),
                                  reads=[hT, wf], writes=[p], inc=(k == KC - 1))
                        fw.op('dve', lambda v, p=p, gc_=gc_, c=c: v.tensor_tensor(out=us[:, c, 0:W_], in0=p[:, 0:W_],
                                                                                 in1=gc_[:, 0:W_], op=ALU.mult),
                              reads=[p, gc_], writes=[us])
                    for c in range(8):
                        p = fm.next()
                        for k in range(KC):
                            fw.op('pe', lambda t, k=k, p=p, c=c: t.matmul(p[:, 0:W_], lhsT=wf[:, k, 1536 + c * 128:1536 + (c + 1) * 128],
                                                                         rhs=hT[:, k, 0:W_], start=(k == 0), stop=(k == KC - 1)),
                                  reads=[hT, wf], writes=[p], inc=(k == KC - 1))
                        fw.op('act', lambda a, p=p, c=c: a.activation(out=sg_[:, c, 0:W_], in_=p[:, 0:W_], func=AF.Silu),
                              reads=[p], writes=[sg_])
                    if r0 == 0:
                        fw.op('dve', lambda v: v.tensor_scalar(out=us[:, :, 0:128], in0=us[:, :, 0:128], scalar1=validt[:, 0:1],
                                                               scalar2=None, op0=ALU.mult), reads=[us, validt], writes=[us])
                    if r0 + W_ == T_q:
                        fw.op('dve', lambda v: v.tensor_scalar(out=us[:, :, W_ - 128:W_], in0=us[:, :, W_ - 128:W_],
                                                               scalar1=validt[:, 1:2], scalar2=None, op0=ALU.mult),
                              reads=[us, validt], writes=[us])
                    fw.dma('pool', [(uT[:, :, r0:r0 + W_].rearrange("c p t -> p c t"), us[:, :, 0:W_])], reads=[us], accw=[uT])
                    fw.dma('pool', [(gbT[:, :, r0:r0 + W_].rearrange("c p t -> p c t"), gb_[:, :, 0:W_])], reads=[gb_], accw=[gbT])
                    fw.dma('pool', [(sgT[:, :, r0:r0 + W_].rearrange("c p t -> p c t"), sg_[:, :, 0:W_])], reads=[sg_], accw=[sgT])
                run_pipelined(body, units1)
                fw.barrier()

            chk('d1')
            with ExitStack() as s3:
                pss = slots(s3, "pss", 2, [128, 2, 512], F32, psum=True)
                pso = slots(s3, "pso", 2, [128, 512], F32, psum=True)
                pj = slots(s3, "pj", 1, [128, 1024], F32, psum=True)
                stage = slots(s3, "wst", 2, [128, 1024], F32)
                wo = load_w_bf16(s3, "wo", lambda k: wout1[k * 128:(k + 1) * 128, :], 1024, stage)
                gate1 = load_bc(s3, "gate1", 8)
                cw = sb(s3, "cw", [128, 4, 3], F32)
                fw.dma('sp', [(cw[:], convw[:, :, :])], writes=[cw])
                esk = sb(s3, "esk", [128, 8], F32)
                fw.dma('sp', [(esk[:], sink[0:1, :].partition_broadcast(128))], writes=[esk])
                fw.op('act', lambda a: a.activation(out=esk[:], in_=esk[:], func=AF.Exp), reads=[esk], writes=[esk])
                esf = sb(s3, "esf", [128, 2, 4, 128], F32)
                fw.op('pool', lambda g: g.memset(esf[:], 0.0), writes=[esf])
                for g_ in range(2):
                    fw.op('dve', lambda v, g_=g_: v.tensor_tensor(
                        out=esf[:, g_, :, :], in0=esf[:, g_, :, :],
                        in1=esk[:, g_ * 4:(g_ + 1) * 4].unsqueeze(2).to_broadcast([128, 4, 128]), op=ALU.add),
                        reads=[esf, esk], writes=[esf])
                mprev = sb(s3, "mprev", [128, 128], BF16)
                mnext = sb(s3, "mnext", [128, 128], BF16)
                fw.op('pool', lambda g: g.memset(mprev[:], 1.0), writes=[mprev])
                fw.op('pool', lambda g: g.memset(mnext[:], 1.0), writes=[mnext])
                fw.op('pool', lambda g: g.affine_select(out=mprev[:], in_=mprev[:], pattern=[[-1, 128]], compare_op=ALU.is_ge,
                                                        fill=0.0, base=0, channel_multiplier=1), reads=[mprev], writes=[mprev])
                fw.op('pool', lambda g: g.affine_select(out=mnext[:], in_=mnext[:], pattern=[[1, 128]], compare_op=ALU.is_ge,
                                                        fill=0.0, base=0, channel_multiplier=-1), reads=[mnext], writes=[mnext])
                pT = slots(s3, "pT", 3, [128, 2, 512], BF16)
                Rs = slots(s3, "Rs", 2, [128, 512], F32)
                gbs = slots(s3, "gbs", 2, [128, 4, 512], BF16)
                sgs = slots(s3, "sgs", 2, [128, 8, 512], BF16)
                mxu = slots(s3, "mxu", 2, [128, 8, 512], BF16)
                acc = slots(s3, "acc", 2, [128, 512], F32)
                xts = slots(s3, "xts", 2, [128, 4, D], F32)
                t32 = slots(s3, "t32", 2, [128, D], F32)
                xo = slots(s3, "xo", 2, [128, 4, D], F32)
                qsu = slots(s3, "qsu", 2, [128, 4, 512], BF16)
                usu = slots(s3, "usu", 2, [128, 4, 514], BF16)
                for u in range(T_own // 512):
                    r0 = 128 + u * 512
                    gb_ = gbs.next()
                    sg_ = sgs.next()
                    fw.dma('sp', [(gb_[:], gbT[:, :, r0:r0 + 512].rearrange("c p t -> p c t"))], reads=[gbT], writes=[gb_])
                    fw.dma('sp', [(sg_[:], sgT[:, :, r0:r0 + 512].rearrange("c p t -> p c t"))], reads=[sgT], writes=[sg_])
                    xt = xts.next()
                    fw.dma('sp', [(xt[:], x1[r0:r0 + 512, :].rearrange("(j p) d -> p j d", p=128))], reads=[x1], writes=[xt])
                    mx = mxu.next()
                    qs = qsu.next()
                    us = usu.next()
                    fw.dma('sp', [(qs[:], qT1[:, :, r0:r0 + 512].rearrange("m p t -> p m t"))], reads=[qT1], writes=[qs])
                    fw.dma('sp', [(us[:], uT[:, :, r0 - 1:r0 + 513].rearrange("c p t -> p c t"))], reads=[uT], writes=[us])
                    for c in range(4):
                        a_ = acc.next()
                        fw.op('dve', lambda v, c=c, a_=a_: v.tensor_scalar(out=a_[:], in0=us[:, c, 0:512], scalar1=cw[:, c, 0:1],
                                                                          scalar2=None, op0=ALU.mult), reads=[us, cw], writes=[a_])
                        fw.op('dve', lambda v, c=c, a_=a_: v.scalar_tensor_tensor(out=a_[:], in0=us[:, c, 1:513],
                                                                                 scalar=cw[:, c, 1:2], in1=a_[:], op0=ALU.mult,
                                                                                 op1=ALU.add), reads=[us, cw, a_], writes=[a_])
                        fw.op('dve', lambda v, c=c, a_=a_: v.scalar_tensor_tensor(out=a_[:], in0=us[:, c, 2:514],
                                                                                 scalar=cw[:, c, 2:3], in1=a_[:], op0=ALU.mult,
                                                                                 op1=ALU.add), reads=[us, cw, a_], writes=[a_])
                        fw.op('pool', lambda g, c=c, a_=a_: g.tensor_tensor(out=a_[:], in0=a_[:], in1=gb_[:, c, :], op=ALU.mult),
                              reads=[a_, gb_], writes=[a_])
                        fw.op('pool', lambda g, c=c, a_=a_: g.tensor_tensor(out=mx[:, 4 + c, :], in0=a_[:], in1=sg_[:, 4 + c, :], op=ALU.mult),
                              reads=[a_, sg_], writes=[mx])
                    for jb in range(4):
                        tl = r0 // 128 + jb
                        c0 = tl * 128
                        for g_ in range(2):
                            pr = slice(g_ * 64, (g_ + 1) * 64)
                            po = pso.next()
                            batches = [[('c', 0), ('c', 1)], [('l', tl - 1), ('l', tl)], [('l', tl + 1)]]
                            nmm = 5
                            cnt = 0
                            for bt in batches:
                                p = pss.next()
                                for i, (kind, ti) in enumerate(bt):
                                    lhs = KTc[pr, ti * 128:(ti + 1) * 128] if kind == 'c' else KT[pr, ti * 128:(ti + 1) * 128]
                                    fw.op('pe', lambda t, i=i, p=p, lhs=lhs: t.matmul(
                                        p[:, i, :], lhsT=lhs, rhs=qs[pr, :, jb * 128:(jb + 1) * 128], start=True, stop=True),
                                        reads=[KTc, KT, qs], writes=[p], inc=(i == len(bt) - 1))
                                pt = pT.next()
                                n = len(bt)
                                fw.op('act', lambda a, p=p, pt=pt, n=n: a.activation(out=pt[:, 0:n, :], in_=p[:, 0:n, :], func=AF.Exp),
                                      reads=[p], writes=[pt])
                                for i, (kind, ti) in enumerate(bt):
                                    if kind == 'l' and ti != tl:
                                        msk = mprev if ti < tl else mnext
                                        fw.op('dve', lambda v, i=i, pt=pt, msk=msk: v.tensor_tensor(
                                            out=pt[:, i, :].rearrange("p (m t) -> p m t", m=4),
                                            in0=pt[:, i, :].rearrange("p (m t) -> p m t", m=4),
                                            in1=msk[:].unsqueeze(1).to_broadcast([128, 4, 128]), op=ALU.mult),
                                            reads=[pt, msk], writes=[pt])
                                for i, (kind, ti) in enumerate(bt):
                                    lhs = VXc[:, ti, g_, :] if kind == 'c' else VX[:, ti, g_, :]
                                    fw.op('pe', lambda t, i=i, pt=pt, lhs=lhs, cnt=cnt: t.matmul(
                                        po[:], lhsT=lhs, rhs=pt[:, i, :], start=(cnt == 0), stop=(cnt == nmm - 1)),
                                        reads=[VXc, VX, pt], writes=[po], inc=(i == len(bt) - 1))
                                    cnt += 1
                            R = Rs.next()
                            fw.op('dve', lambda v, R=R, po=po, g_=g_: v.tensor_tensor(
                                out=R[0:64, :], in0=po[64:128, :], in1=esf[64:128, g_, :, :].rearrange("p m t -> p (m t)"),
                                op=ALU.add), reads=[po, esf], writes=[R])
                            fw.op('dve', lambda v, R=R: v.reciprocal(out=R[0:64, :], in_=R[0:64, :]), reads=[R], writes=[R])
                            fw.op('dve', lambda v, R=R, po=po: v.tensor_tensor(out=R[0:64, :], in0=po[0:64, :], in1=R[0:64, :], op=ALU.mult),
                                  reads=[po, R], writes=[R])
                            Rv = R[0:64, :].rearrange("p (c two t) -> p c two t", c=2, two=2)
                            for hf in range(2):
                                fw.op('pool', lambda g, hf=hf, Rv=Rv, g_=g_, jb=jb: g.tensor_copy(
                                    out=mx[hf * 64:(hf + 1) * 64, 2 * g_:2 * g_ + 2, jb * 128:(jb + 1) * 128],
                                    in_=Rv[:, :, hf, :]), reads=[R], writes=[mx])
                    fw.op('dve', lambda v: v.tensor_tensor(out=mx[:, 0:4, :], in0=mx[:, 0:4, :], in1=sg_[:, 0:4, :], op=ALU.mult),
                          reads=[mx, sg_], writes=[mx])
                    xo_ = xo.next()
                    for j in range(4):
                        p = pj.next()
                        for n in range(2):
                            for k in range(KC):
                                fw.op('pe', lambda t, k=k, n=n, p=p, j=j: t.matmul(
                                    p[:, n * 512:(n + 1) * 512], lhsT=mx[:, k, j * 128:(j + 1) * 128],
                                    rhs=wo[:, k, n * 512:(n + 1) * 512], start=(k == 0), stop=(k == KC - 1)),
                                    reads=[mx, wo], writes=[p], inc=(n == 1 and k == KC - 1))
                        tt = t32.next()
                        fw.op('dve', lambda v, p=p, tt=tt: v.tensor_tensor(out=tt[:], in0=p[:], in1=gate1[:], op=ALU.mult),
                              reads=[p, gate1], writes=[tt])
                        fw.op('pool', lambda g, tt=tt, j=j: g.tensor_tensor(out=xo_[:, j, :], in0=tt[:], in1=xt[:, j, :], op=ALU.add),
                              reads=[tt, xt], writes=[xo_])
                    fw.dma('pool', [(out[u * 512:(u + 1) * 512, :].rearrange("(j p) d -> p j d", p=128), xo_[:])],
                           reads=[xo_], accw=[out])
                fw.barrier()
        fw.barrier()
        build.nins = dict(fw.nins)
    except _Stop:
        pass
    return nc


def _rope_tables(pos_row, pos_col):
    f32 = np.float32

    def cs(pos, half):
        inv = (f32(10000.0) ** (-(np.arange(half, dtype=f32)) / f32(half))).astype(f32)
        ang = (pos.astype(f32)[:, None] * inv[None, :]).astype(f32)
        return np.cos(ang).astype(f32), np.sin(ang).astype(f32)

    cr8, sr8 = cs(pos_row, 8)
    cc8, sc8 = cs(pos_col, 8)
    cr16, sr16 = cs(pos_row, 16)
    cc16, sc16 = cs(pos_col, 16)
    C32 = np.concatenate([cr8, cr8, cc8, cc8], 1)
    S32 = np.concatenate([-sr8, sr8, -sc8, sc8], 1)
    C64 = np.concatenate([cr16, cr16, cc16, cc16], 1)
    S64 = np.concatenate([-sr16, sr16, -sc16, sc16], 1)
    return np.ascontiguousarray(np.concatenate([C32, S32, C64, S64], 1).astype(f32))


_NC_CACHE = {}


def run(inputs, S, stop=None, dbg=None, dbg_out=()):
    f32 = np.float32
    x = np.asarray(inputs['x'], f32)
    B = x.shape[0]
    assert x.shape[1] == S and B == 2
    T_own = S // 4
    T_q = T_own + 256
    if (S, stop) not in _NC_CACHE:
        _NC_CACHE[(S, stop)] = build(S, stop, dbg_out)
    nc = _NC_CACHE[(S, stop)]
    c = np.asarray(inputs['c'], f32)
    ctx = np.asarray(inputs['ctx'], f32)
    c_ctx = np.asarray(inputs['c_ctx'], f32)
    w_in0 = np.asarray(inputs['ab_w_in'], f32)[0]
    w_in1 = np.asarray(inputs['cd_w_in'], f32)[0]
    wkv0 = np.ascontiguousarray(np.concatenate([w_in0[:, 256:512], w_in0[:, 1056:1184], w_in0[:, 1184:1312], w_in0[:, 512:544]], 1))
    wq0 = np.ascontiguousarray(np.concatenate([w_in0[:, 0:256], w_in0[:, 544:1056]], 1))
    wg0 = np.ascontiguousarray(w_in0[:, 1312:2336])
    wt1 = np.ascontiguousarray(w_in1[:, 0:768])
    wf1 = np.ascontiguousarray(w_in1[:, 768:3328])
    gains0 = np.concatenate([np.asarray(inputs['mla_q_gain'], f32)[0], np.asarray(inputs['mla_k_gain'], f32)[0],
                             np.asarray(inputs['gqa_q_gain'], f32)[0], np.asarray(inputs['gqa_k_gain'], f32)[0]])[None, :]
    gains1 = np.concatenate([np.asarray(inputs['win_q_gain'], f32)[0], np.asarray(inputs['win_k_gain'], f32)[0]])[None, :]
    convw = np.ascontiguousarray(np.asarray(inputs['conv_w'], f32)[0].reshape(3, 4, 128).transpose(2, 1, 0))
    pos = np.arange(S)
    ropek = _rope_tables(pos // GRID_W, pos % GRID_W)
    shared = {
        "mod_w": np.ascontiguousarray(np.asarray(inputs['mod_w'], f32)),
        "mod_b": np.ascontiguousarray(np.asarray(inputs['mod_b'], f32)),
        "wkv0": wkv0, "wq0": wq0, "wg0": wg0,
        "wout0": np.ascontiguousarray(np.asarray(inputs['ab_w_out'], f32)[0]),
        "w_uq": np.ascontiguousarray(np.asarray(inputs['mla_w_uq'], f32)[0]),
        "w_ukv": np.ascontiguousarray(np.asarray(inputs['mla_w_ukv'], f32)[0]),
        "cq_gain": np.ascontiguousarray(np.asarray(inputs['mla_cq_gain'], f32)[0].reshape(2, 128).T),
        "ckv_gain": np.ascontiguousarray(np.asarray(inputs['mla_ckv_gain'], f32)[0].reshape(2, 128).T),
        "gains0": np.ascontiguousarray(gains0),
        "wt1": wt1, "wf1": wf1,
        "wout1": np.ascontiguousarray(np.asarray(inputs['cd_w_out'], f32)[0]),
        "gains1": np.ascontiguousarray(gains1),
        "sink": np.ascontiguousarray(np.asarray(inputs['win_sink'], f32)[0][None, :]),
        "convw": convw,
        "ropek": ropek,
    }
    in_maps = []
    for core in range(NCORES):
        b, qc = core // 4, core % 4
        start = qc * T_own
        lo, hi = start - 128, start + T_own + 128
        xq = np.zeros((T_q, D), f32)
        a, e = max(lo, 0), min(hi, S)
        xq[a - lo:e - lo] = x[b, a:e]
        pq = np.clip(np.arange(lo, hi), 0, S - 1)
        vcol = np.array([1.0 if lo >= 0 else 0.0, 1.0 if hi <= S else 0.0], f32)
        cvec = np.stack([c[b], c_ctx], 1).reshape(KC, 128, 2).transpose(1, 0, 2)
        m = dict(shared)
        m.update({
            "xq": xq, "xkv": np.ascontiguousarray(x[b]), "ctxb": np.ascontiguousarray(ctx[b]),
            "cT": np.ascontiguousarray(cvec.astype(f32)),
            "ropeq": _rope_tables(pq // GRID_W, pq % GRID_W),
            "valid": np.ascontiguousarray(np.broadcast_to(vcol[None, :], (128, 2)).astype(f32)),
        })
        in_maps.append(m)
    res = run_bass_kernel_spmd(nc, in_maps, core_ids=list(range(NCORES)))
    if dbg is not None:
        dbg.append(res)
    outp = np.zeros((B, S, D), f32)
    for core in range(NCORES):
        b, qc = core // 4, core % 4
        outp[b, qc * T_own:(qc + 1) * T_own] = res.results[core]["out"]
    return outp


def kernel(**inputs):
    return run(inputs, 16384)
```

```python
import numpy as np
from contextlib import ExitStack
import concourse.bass as bass
import concourse.mybir as mybir
from concourse.bass_utils import run_bass_kernel_spmd

F32 = mybir.dt.float32
BF16 = mybir.dt.bfloat16
ALU = mybir.AluOpType
AF = mybir.ActivationFunctionType
AX = mybir.AxisListType

D = 1024
KC = 8
L = 256
EPS = 1e-6
GRID_W = 64
NCORES = 8
import os
STQ = os.environ.get('STQ', 'pool')


class Tn:
    def __init__(self, h, const=False, psum=False):
        self.h = h
        self.w = {}
        self.r = {}
        self.const = const
        self.psum = psum

    def __getitem__(self, k):
        return self.h[k]


class FW:
    def __init__(self, nc, es):
        self.nc = nc
        self.E = {'sp': nc.sync, 'act': nc.scalar, 'pool': nc.gpsimd, 'dve': nc.vector, 'pe': nc.tensor}
        self.sem = {}
        self.tot = {}
        self.seen = {e: {} for e in self.E}
        for e in self.E:
            self.sem[e] = es.enter_context(nc.semaphore('s_' + e))
            self.tot[e] = 0
        self.ring = {}
        self.rpos = {}
        for e, n in (('sp', 30), ('pool', 24), ('act', 8)):
            keys = []
            for i in range(n):
                k = 'd_%s%d' % (e, i)
                self.sem[k] = es.enter_context(nc.semaphore(k))
                self.tot[k] = 0
                keys.append(k)
            self.ring[e] = keys
            self.rpos[e] = 0
        self.nins = {e: 0 for e in self.E}

    def _wait(self, e, deps):
        for k, v in deps.items():
            if v <= 0:
                continue
            if k == e and e == 'pe':
                continue
            if self.seen[e].get(k, 0) >= v:
                continue
            assert v <= self.tot[k], "wait on unclosed group %s %d>%d (eng %s)" % (k, v, self.tot[k], e)
            self.E[e].wait_ge(self.sem[k], v)
            self.seen[e][k] = v

    @staticmethod
    def _deps(e, reads, writes, accw):
        d = {}
        for b in reads:
            for k, v in b.w.items():
                if d.get(k, 0) < v:
                    d[k] = v
            if b.psum:
                for k, v in b.r.items():
                    if k != e and d.get(k, 0) < v:
                        d[k] = v
        for b in writes:
            for k, v in b.w.items():
                if d.get(k, 0) < v:
                    d[k] = v
            for k, v in b.r.items():
                if d.get(k, 0) < v:
                    d[k] = v
        for b in accw:
            for k, v in b.r.items():
                if d.get(k, 0) < v:
                    d[k] = v
        return d

    def op(self, e, fn, reads=(), writes=(), inc=True):
        self._wait(e, self._deps(e, reads, writes, ()))
        ins = fn(self.E[e])
        self.nins[e] += 1
        val = self.tot[e] + 1
        if inc:
            ins.then_inc(self.sem[e], 1)
            self.tot[e] = val
        for b in reads:
            if not b.const and b.r.get(e, 0) < val:
                b.r[e] = val
        for b in writes:
            b.w = {e: val}
            b.r = {}
        return ins

    def dma(self, e, pairs, reads=(), writes=(), accw=()):
        k = self.ring[e][self.rpos[e]]
        self.rpos[e] = (self.rpos[e] + 1) % len(self.ring[e])
        deps = self._deps(e, reads, writes, accw)
        if deps.get(k, 0) < self.tot[k]:
            deps[k] = self.tot[k]
        self._wait(e, deps)
        val = self.tot[k] + 16 * len(pairs)
        for (o, i) in pairs:
            self.E[e].dma_start(out=o, in_=i).then_inc(self.sem[k], 16)
            self.nins[e] += 1
        self.tot[k] = val
        for b in reads:
            if not b.const and b.r.get(k, 0) < val:
                b.r[k] = val
        for b in writes:
            b.w = {k: val}
            b.r = {}
        for b in accw:
            if b.w.get(k, 0) < val:
                b.w[k] = val

    def barrier(self, engines=None):
        for e in (engines or self.E):
            self._wait(e, dict(self.tot))


class Slots:
    def __init__(self, items):
        self.items = items
        self.i = 0

    def next(self):
        t = self.items[self.i]
        self.i = (self.i + 1) % len(self.items)
        return t


class _Stop(Exception):
    pass


def build(S, stop=None, dbg_out=()):
    T_own = S // 4
    T_q = T_own + 256
    NQT = T_q // 128
    NK = L + S
    NKT = NK // 128

    nc = bass.Bass("TRN2", target_bir_lowering=False)

    def din(name, shape, dt=F32):
        return Tn(nc.dram_tensor(name, list(shape), dt, kind="ExternalInput").ap(), const=True)

    def dscr(name, shape, dt):
        if name in dbg_out:
            return Tn(nc.dram_tensor(name, list(shape), dt, kind="ExternalOutput").ap())
        return Tn(nc.dram_tensor(name, list(shape), dt).ap())

    xq = din("xq", [T_q, D])
    xkv = din("xkv", [S, D])
    ctxb = din("ctxb", [L, D])
    cT = din("cT", [128, KC, 2])
    mod_w = din("mod_w", [2, D, 3 * D])
    mod_b = din("mod_b", [2, 3 * D])
    wkv0 = din("wkv0", [D, 544])
    wq0 = din("wq0", [D, 768])
    wg0 = din("wg0", [D, 1024])
    wout0 = din("wout0", [D, D])
    w_uq = din("w_uq", [256, 768])
    w_ukv = din("w_ukv", [256, 1024])
    cq_gain = din("cq_gain", [128, 2])
    ckv_gain = din("ckv_gain", [128, 2])
    gains0 = din("gains0", [1, 96 + 96 + 64 + 64])
    wt1 = din("wt1", [D, 768])
    wf1 = din("wf1", [D, 2560])
    wout1 = din("wout1", [D, D])
    gains1 = din("gains1", [1, 128])
    sink = din("sink", [1, 8])
    convw = din("convw", [128, 4, 3])
    ropeq = din("ropeq", [T_q, 192])
    ropek = din("ropek", [S, 192])
    valid = din("valid", [128, 2])
    out = Tn(nc.dram_tensor("out", [T_own, D], F32, kind="ExternalOutput").ap())

    modbc = dscr("modbc", [12, 128, D], F32)
    kTa = dscr("kTa", [8, 96, NK], BF16)
    vA = dscr("vA", [8, 128, NKT, 64], BF16)
    kTb = dscr("kTb", [2, 64, NK], BF16)
    vB = dscr("vB", [2, 128, NKT, 64], BF16)
    qTa = dscr("qTa", [8, 96, T_q], BF16)
    qTb = dscr("qTb", [8, 64, T_q], BF16)
    gT0 = dscr("gT0", [8, 128, T_q], BF16)
    mixT0 = dscr("mixT0", [8, 128, T_q], BF16)
    x1 = dscr("x1", [T_q, D], F32)
    qTa_c = dscr("qTa_c", [8, 96, L], BF16)
    qTb_c = dscr("qTb_c", [8, 64, L], BF16)
    gT0_c = dscr("gT0_c", [8, 128, L], BF16)
    mixT0_c = dscr("mixT0_c", [8, 128, L], BF16)
    xc1 = dscr("xc1", [L, D], F32)
    gbT = dscr("gbT", [4, 128, T_q], BF16)
    sgT = dscr("sgT", [8, 128, T_q], BF16)
    qT1 = dscr("qT1", [4, 128, T_q], BF16)
    uT = dscr("uT", [4, 128, T_q], BF16)

    es = ExitStack()
    try:
      with es:
        fw = FW(nc, es)

        def chk(name):
            if stop == name:
                fw.barrier()
                build.nins = dict(fw.nins)
                raise _Stop()

        uid = [0]

        def sb(stk, name, shape, dt, const=False):
            uid[0] += 1
            return Tn(stk.enter_context(nc.sbuf_tensor("%s_%d" % (name, uid[0]), list(shape), dt)), const=const)

        def ps(stk, name, shape, dt=F32):
            uid[0] += 1
            return Tn(stk.enter_context(nc.psum_tensor("%s_%d" % (name, uid[0]), list(shape), dt)), psum=True)

        def slots(stk, name, n, shape, dt, psum=False):
            return Slots([(ps if psum else sb)(stk, "%s%d" % (name, i), shape, dt) for i in range(n)])

        ident = sb(es, "ident", [128, 128], BF16)
        fw.op('pool', lambda g: g.memset(ident[:], 0.0), writes=[ident])
        fw.op('pool', lambda g: g.affine_select(out=ident[:], in_=ident[:], pattern=[[-1, 128]],
                                                compare_op=ALU.not_equal, fill=1.0, base=0,
                                                channel_multiplier=1), reads=[ident], writes=[ident])
        ident.const = True
        validt = sb(es, "validt", [128, 2], F32)
        fw.dma('sp', [(validt[:], valid[:, :])], writes=[validt])
        validt.const = True

        with ExitStack() as st:
            cTt = sb(st, "cTt", [128, KC, 2], F32)
            scT = sb(st, "scT", [128, KC, 2], F32)
            mws = slots(st, "mws", 2, [128, KC, 512], F32)
            modsb = sb(st, "modsb", [2, 2, 3 * D], F32)
            modbias = sb(st, "modbias", [2, 2, 3 * D], F32)
            sel = sb(st, "sel", [2, 2, 128], F32)
            selw = sb(st, "selw", [2, 128], F32)
            pm = slots(st, "pm", 2, [2, 512], F32, psum=True)
            pb = slots(st, "pb", 2, [128, 1024], F32, psum=True)
            bcs = slots(st, "bcs", 2, [128, D], F32)

            fw.dma('sp', [(cTt[:], cT[:, :, :])], writes=[cTt])
            fw.op('act', lambda a: a.activation(out=scT[:], in_=cTt[:], func=AF.Silu), reads=[cTt], writes=[scT])
            for i in range(2):
                fw.dma('sp', [(modbias[0:1, i, :], mod_b[i:i + 1, :]), (modbias[1:2, i, :], mod_b[i:i + 1, :])],
                       accw=[modbias])
            fw.op('pool', lambda g: g.memset(sel[:], 0.0), writes=[sel])
            for who in range(2):
                fw.op('pool', lambda g, who=who: g.affine_select(
                    out=sel[:, who, :], in_=sel[:, who, :], pattern=[[0, 128]], compare_op=ALU.not_equal,
                    fill=1.0, base=-who, channel_multiplier=1), reads=[sel], writes=[sel])
            for i in range(2):
                for n in range(6):
                    mw = mws.next()
                    fw.dma('sp', [(mw[:], mod_w[i, :, n * 512:(n + 1) * 512].rearrange("(k p) n -> p k n", p=128))],
                           writes=[mw])
                    p = pm.next()
                    for k in range(KC):
                        fw.op('pe', lambda t, k=k, p=p, mw=mw: t.matmul(p[:], lhsT=scT[:, k, :], rhs=mw[:, k, :],
                                                                       start=(k == 0), stop=(k == KC - 1)),
                              reads=[scT, mw], writes=[p], inc=(k == KC - 1))
                    fw.op('dve', lambda v, p=p, i=i, n=n: v.tensor_tensor(
                        out=modsb[:, i, n * 512:(n + 1) * 512], in0=p[:], in1=modbias[:, i, n * 512:(n + 1) * 512],
                        op=ALU.add), reads=[p, modbias], writes=[modsb])
            for i in range(2):
                for who in range(2):
                    for which in range(3):
                        p = pb.next()
                        for n in range(2):
                            fw.op('pe', lambda t, p=p, n=n, i=i, who=who, which=which: t.matmul(
                                p[:, n * 512:(n + 1) * 512], lhsT=sel[:, who, :],
                                rhs=modsb[:, i, which * D + n * 512: which * D + (n + 1) * 512],
                                start=True, stop=True), reads=[sel, modsb], writes=[p], inc=(n == 1))
                        bc = bcs.next()
                        if which == 1:
                            fw.op('dve', lambda v, p=p, bc=bc: v.tensor_scalar(out=bc[:], in0=p[:], scalar1=1.0,
                                                                              scalar2=None, op0=ALU.add),
                                  reads=[p], writes=[bc])
                        else:
                            fw.op('dve', lambda v, p=p, bc=bc: v.tensor_copy(out=bc[:], in_=p[:]), reads=[p], writes=[bc])
                        fw.dma(STQ, [(modbc[i * 6 + who * 3 + which, :, :], bc[:])], reads=[bc], accw=[modbc])
            fw.barrier()

        def load_w_bf16(stk, name, src_ap_fn, ncols, stage, gain=None, kchunks=KC):
            w = sb(stk, name, [128, kchunks, ncols], BF16)
            for k in range(kchunks):
                s_ = stage.next()
                fw.dma('sp', [(s_[:, 0:ncols], src_ap_fn(k))], writes=[s_])
                if gain is None:
                    fw.op('pool', lambda g, k=k, s_=s_: g.tensor_copy(out=w[:, k, :], in_=s_[:, 0:ncols]),
                          reads=[s_], writes=[w])
                else:
                    fw.op('dve', lambda v, k=k, s_=s_: v.tensor_scalar(out=w[:, k, :], in0=s_[:, 0:ncols],
                                                                      scalar1=gain[:, k:k + 1], scalar2=None,
                                                                      op0=ALU.mult), reads=[s_, gain], writes=[w])
            return w

        def load_bc(stk, name, idx):
            t = sb(stk, name, [128, D], F32)
            fw.dma('sp', [(t[:], modbc[idx, :, :])], reads=[modbc], writes=[t])
            return t

        def run_pipelined(gen_fn, units):
            live = []
            it = iter(units)
            while True:
                for g in list(live):
                    try:
                        next(g)
                    except StopIteration:
                        live.remove(g)
                u = next(it, None)
                if u is not None:
                    g = gen_fn(u)
                    try:
                        next(g)
                        live.append(g)
                    except StopIteration:
                        pass
                elif not live:
                    break

        class Front:
            def __init__(self, stk, Jmax, tp):
                self.xt = slots(stk, "f_xt", 2, [128, Jmax, D], F32)
                self.junk = sb(stk, "f_junk", [128, D], BF16)
                self.ss = slots(stk, "f_ss", 2, [128, Jmax], F32)
                self.rs = slots(stk, "f_rs", 2, [128, Jmax], F32)
                self.t32 = slots(stk, "f_t32", 1, [128, D], F32)
                self.hb = slots(stk, "f_hb", 2, [128, D], BF16)
                self.hT = slots(stk, "f_hT", 2, [128, KC, Jmax * 128], BF16)
                self.tp = tp

            def run(self, src, r0, J, sc1, sh):
                xt = self.xt.next()
                fw.dma('sp', [(xt[:, 0:J, :], src[r0:r0 + J * 128, :].rearrange("(j p) d -> p j d", p=128))],
                       reads=[src], writes=[xt])
                ss = self.ss.next()
                rs = self.rs.next()
                for j in range(J):
                    fw.op('act', lambda a, j=j: a.activation(out=self.junk[:], in_=xt[:, j, :], func=AF.Square,
                                                            accum_out=ss[:, j:j + 1]),
                          reads=[xt], writes=[self.junk, ss])
                fw.op('act', lambda a: a.activation(out=ss[:, 0:J], in_=ss[:, 0:J], func=AF.Sqrt, bias=EPS,
                                                    scale=1.0 / D), reads=[ss], writes=[ss])
                fw.op('dve', lambda v: v.reciprocal(out=rs[:, 0:J], in_=ss[:, 0:J]), reads=[ss], writes=[rs])
                hT = self.hT.next()
                for j in range(J):
                    t32 = self.t32.next()
                    hb = self.hb.next()
                    fw.op('dve', lambda v, j=j, t32=t32: v.scalar_tensor_tensor(
                        out=t32[:], in0=xt[:, j, :], scalar=rs[:, j:j + 1], in1=sc1[:], op0=ALU.mult, op1=ALU.mult),
                        reads=[xt, rs, sc1], writes=[t32])
                    fw.op('pool', lambda g, t32=t32, hb=hb: g.tensor_tensor(out=hb[:], in0=t32[:], in1=sh[:], op=ALU.add),
                          reads=[t32, sh], writes=[hb])
                    p = self.tp.next()
                    for k in range(KC):
                        fw.op('pe', lambda t, k=k, p=p, hb=hb: t.transpose(out=p[:, k * 128:(k + 1) * 128],
                                                                          in_=hb[:, k * 128:(k + 1) * 128],
                                                                          identity=ident[:]),
                              reads=[hb, ident], writes=[p], inc=(k == KC - 1))
                    fw.op('act', lambda a, j=j, p=p: a.copy(out=hT[:, :, j * 128:(j + 1) * 128],
                                                          in_=p[:].rearrange("p (k t) -> p k t", k=KC)),
                          reads=[p], writes=[hT])
                return hT

        def grp_rstd(src_ap, sq, ssum, rstd, n, Dh, rd):
            sqv = sq[:, 0:n * Dh].rearrange("p (n d) -> p n d", d=Dh)
            fw.op('dve', lambda v: v.tensor_tensor(out=sqv, in0=src_ap, in1=src_ap, op=ALU.mult),
                  reads=rd, writes=[sq])
            fw.op('dve', lambda v: v.tensor_reduce(out=ssum[:, 0:n], in_=sqv, axis=AX.X, op=ALU.add),
                  reads=[sq], writes=[ssum])
            fw.op('act', lambda a: a.activation(out=ssum[:, 0:n], in_=ssum[:, 0:n], func=AF.Sqrt, bias=EPS,
                                                scale=1.0 / Dh), reads=[ssum], writes=[ssum])
            fw.op('dve', lambda v: v.reciprocal(out=rstd[:, 0:n], in_=ssum[:, 0:n]), reads=[ssum], writes=[rstd])

        def rope(y, J, G, Dh, o, R, tab, cofs, tmp1, tmp2):
            q4 = R // 4
            yr = y[:, 0:J, :, o:o + R]
            C = tab[:, 0:J, cofs:cofs + R].unsqueeze(2).to_broadcast([128, J, G, R])
            t1 = tmp1[:, 0:J * G * R].rearrange("p (j g r) -> p j g r", j=J, g=G)
            t2 = tmp2[:, 0:J * G * R].rearrange("p (j g r) -> p j g r", j=J, g=G)
            fw.op('dve', lambda v: v.tensor_tensor(out=t1, in0=yr, in1=C, op=ALU.mult), reads=[y, tab], writes=[tmp1])
            for a in range(2):
                for hf in range(2):
                    dst = t2[:, :, :, a * 2 * q4 + hf * q4: a * 2 * q4 + (hf + 1) * q4]
                    srcv = y[:, 0:J, :, o + a * 2 * q4 + (1 - hf) * q4: o + a * 2 * q4 + (2 - hf) * q4]
                    sn = tab[:, 0:J, cofs + R + a * 2 * q4 + hf * q4: cofs + R + a * 2 * q4 + (hf + 1) * q4] \
                        .unsqueeze(2).to_broadcast([128, J, G, q4])
                    fw.op('dve', lambda v, dst=dst, srcv=srcv, sn=sn: v.tensor_tensor(out=dst, in0=srcv, in1=sn, op=ALU.mult),
                          reads=[y, tab], writes=[tmp2])
            fw.op('dve', lambda v: v.tensor_tensor(out=yr, in0=t1, in1=t2, op=ALU.add),
                  reads=[tmp1, tmp2], writes=[y])

        def phase_kv0():
            with ExitStack() as st:
                tp = slots(st, "tp", 2, [128, 1024], BF16, psum=True)
                pj = slots(st, "pj", 2, [128, 1024], F32, psum=True)
                fr = Front(st, 4, tp)
                stage = slots(st, "wst", 2, [128, 1024], F32)
                gck = sb(st, "gck", [128, 2], F32)
                fw.dma('sp', [(gck[:], ckv_gain[:, :])], writes=[gck])
                wkv = load_w_bf16(st, "wkv", lambda k: wkv0[k * 128:(k + 1) * 128, :], 544, stage)
                wuk = load_w_bf16(st, "wuk", lambda k: w_ukv[k * 128:(k + 1) * 128, :], 1024, stage, gain=gck, kchunks=2)
                gbc = sb(st, "gbc", [128, 320], F32)
                fw.dma('sp', [(gbc[:], gains0[0:1, :].partition_broadcast(128))], writes=[gbc])
                bc = {0: (load_bc(st, "sc1t", 1), load_bc(st, "sht", 0)), 1: (load_bc(st, "sc1c", 4), load_bc(st, "shc", 3))}
                tabs = slots(st, "tab", 2, [128, 4, 192], F32)
                stp = slots(st, "stp", 2, [128, 4, 544], F32)
                sq = sb(st, "sq", [128, 3072], F32)
                ssum = slots(st, "ssum", 4, [128, 32], F32)
                rstd = slots(st, "rstd", 4, [128, 32], F32)
                ckn = slots(st, "ckn", 2, [128, 4, 256], BF16)
                cTt = slots(st, "cTt", 2, [128, 2, 512], BF16)
                stK = slots(st, "stK", 2, [128, 4, 8, 96], F32)
                Kn = slots(st, "Kn", 1, [128, 4, 8, 96], BF16)
                Vst = slots(st, "Vst", 2, [128, 8, 4, 64], BF16)
                kTst = slots(st, "kTst", 2, [96, 8, 512], BF16)
                gkn = slots(st, "gkn", 1, [128, 4, 2, 64], F32)
                gkb = slots(st, "gkb", 2, [128, 4, 128], BF16)
                Vbst = slots(st, "Vbst", 2, [128, 2, 4, 64], BF16)
                kTbst = slots(st, "kTbst", 2, [128, 512], BF16)
                tmp1 = sq
                tmp2 = sb(st, "tmp2", [128, 1024], F32)

                units = [(ctxb, 0, 2, 1, None, 0)]
                for u in range(S // 512):
                    units.append((xkv, u * 512, 4, 0, u * 512, 2 + u * 4))
                def body(unit):
                    (src, r0, J, who, rp0, t0) = unit
                    sc1, sh = bc[who]
                    hT = fr.run(src, r0, J, sc1, sh)
                    tab = None
                    if rp0 is not None:
                        tab = tabs.next()
                        fw.dma('sp', [(tab[:, 0:J, :], ropek[rp0:rp0 + J * 128, :].rearrange("(j p) c -> p j c", p=128))],
                               writes=[tab])
                    yield
                    s_ = stp.next()
                    for j in range(J):
                        p = pj.next()
                        for k in range(KC):
                            fw.op('pe', lambda t, k=k, p=p, j=j: t.matmul(p[:, 0:512], lhsT=hT[:, k, j * 128:(j + 1) * 128],
                                                                         rhs=wkv[:, k, 0:512], start=(k == 0),
                                                                         stop=(k == KC - 1)),
                                  reads=[hT, wkv], writes=[p], inc=False)
                        for k in range(KC):
                            fw.op('pe', lambda t, k=k, p=p, j=j: t.matmul(p[:, 512:544], lhsT=hT[:, k, j * 128:(j + 1) * 128],
                                                                         rhs=wkv[:, k, 512:544], start=(k == 0),
                                                                         stop=(k == KC - 1)),
                                  reads=[hT, wkv], writes=[p], inc=(k == KC - 1))
                        fw.op('act', lambda a, p=p, j=j: a.copy(out=s_[:, j, :], in_=p[:, 0:544]), reads=[p], writes=[s_])
                    sm = ssum.next()
                    rsd = rstd.next()
                    sqv = sq[:, 0:J * 256].rearrange("p (j c) -> p j c", j=J)
                    fw.op('dve', lambda v: v.tensor_tensor(out=sqv, in0=s_[:, 0:J, 0:256], in1=s_[:, 0:J, 0:256], op=ALU.mult),
                          reads=[s_], writes=[sq])
                    fw.op('dve', lambda v: v.tensor_reduce(out=sm[:, 0:J], in_=sqv, axis=AX.X, op=ALU.add),
                          reads=[sq], writes=[sm])
                    fw.op('act', lambda a: a.activation(out=sm[:, 0:J], in_=sm[:, 0:J], func=AF.Sqrt, bias=EPS,
                                                        scale=1.0 / 256), reads=[sm], writes=[sm])
                    fw.op('dve', lambda v: v.reciprocal(out=rsd[:, 0:J], in_=sm[:, 0:J]), reads=[sm], writes=[rsd])
                    cn = ckn.next()
                    fw.op('dve', lambda v: v.tensor_tensor(out=cn[:, 0:J, :], in0=s_[:, 0:J, 0:256],
                                                           in1=rsd[:, 0:J].unsqueeze(2).to_broadcast([128, J, 256]),
                                                           op=ALU.mult), reads=[s_, rsd], writes=[cn])
                    ct = cTt.next()
                    for j in range(J):
                        p = tp.next()
                        for k2 in range(2):
                            fw.op('pe', lambda t, k2=k2, p=p, j=j: t.transpose(out=p[:, k2 * 128:(k2 + 1) * 128],
                                                                              in_=cn[:, j, k2 * 128:(k2 + 1) * 128],
                                                                              identity=ident[:]),
                                  reads=[cn, ident], writes=[p], inc=(k2 == 1))
                        fw.op('act', lambda a, p=p, j=j: a.copy(out=ct[:, :, j * 128:(j + 1) * 128],
                                                              in_=p[:, 0:256].rearrange("p (k t) -> p k t", k=2)),
                              reads=[p], writes=[ct])
                    sk = stK.next()
                    vs = Vst.next()
                    for j in range(J):
                        p = pj.next()
                        for n in range(2):
                            for k2 in range(2):
                                fw.op('pe', lambda t, k2=k2, n=n, p=p, j=j: t.matmul(
                                    p[:, n * 512:(n + 1) * 512], lhsT=ct[:, k2, j * 128:(j + 1) * 128],
                                    rhs=wuk[:, k2, n * 512:(n + 1) * 512], start=(k2 == 0), stop=(k2 == 1)),
                                    reads=[ct, wuk], writes=[p], inc=(n == 1 and k2 == 1))
                        pv_ = p[:].rearrange("p (h c) -> p h c", h=8)
                        fw.op('act', lambda a, j=j, pv_=pv_: a.copy(out=sk[:, j, :, 0:64], in_=pv_[:, :, 0:64]),
                              reads=[p], writes=[sk])
                        fw.op('dve', lambda v, j=j, pv_=pv_: v.tensor_copy(out=vs[:, :, j, :], in_=pv_[:, :, 64:128]),
                              reads=[p], writes=[vs])
                    fw.op('pool', lambda g: g.tensor_copy(out=sk[:, 0:J, :, 64:96],
                                                          in_=s_[:, 0:J, 512:544].unsqueeze(2).to_broadcast([128, J, 8, 32])),
                          reads=[s_], writes=[sk])
                    yield
                    sm = ssum.next()
                    rsd = rstd.next()
                    skv = sk[:, 0:J, :, :].rearrange("p j h c -> p (j h) c")
                    grp_rstd(skv, sq, sm, rsd, J * 8, 96, [sk])
                    fw.op('dve', lambda v: v.tensor_tensor(out=skv, in0=skv,
                                                           in1=rsd[:, 0:J * 8].unsqueeze(2).to_broadcast([128, J * 8, 96]),
                                                           op=ALU.mult), reads=[sk, rsd], writes=[sk])
                    fw.op('dve', lambda v: v.tensor_tensor(out=skv, in0=skv,
                                                           in1=gbc[:, 96:192].unsqueeze(1).to_broadcast([128, J * 8, 96]),
                                                           op=ALU.mult), reads=[sk, gbc], writes=[sk])
                    if tab is not None:
                        rope(sk, J, 8, 96, 64, 32, tab, 0, tmp1, tmp2)
                    kn = Kn.next()
                    fw.op('pool', lambda g: g.tensor_copy(out=kn[:, 0:J], in_=sk[:, 0:J]), reads=[sk], writes=[kn])
                    kt = kTst.next()
                    for j in range(J):
                        p = tp.next()
                        for h in range(8):
                            fw.op('pe', lambda t, h=h, p=p, j=j: t.transpose(out=p[0:96, h * 128:(h + 1) * 128],
                                                                            in_=kn[:, j, h, :], identity=ident[:]),
                                  reads=[kn, ident], writes=[p], inc=(h == 7))
                        fw.op('act', lambda a, p=p, j=j: a.copy(out=kt[:, :, j * 128:(j + 1) * 128],
                                                              in_=p[0:96, :].rearrange("p (h t) -> p h t", h=8)),
                              reads=[p], writes=[kt])
                    k0 = t0 * 128
                    fw.dma(STQ, [(kTa[:, :, k0:k0 + J * 128].rearrange("h d t -> d h t"), kt[:, :, 0:J * 128])],
                           reads=[kt], accw=[kTa])
                    fw.dma(STQ, [(vA[:, :, t0:t0 + J, :].rearrange("h p j d -> p h (j d)"), vs[:, :, 0:J, :].rearrange("p h j d -> p h (j d)"))], reads=[vs], accw=[vA])
                    gk = gkn.next()
                    fw.op('pool', lambda g: g.tensor_copy(out=gk[:, 0:J].rearrange("p j g c -> p j (g c)"),
                                                          in_=s_[:, 0:J, 256:384]), reads=[s_], writes=[gk])
                    vb = Vbst.next()
                    fw.op('pool', lambda g: g.tensor_copy(out=vb[:, :, 0:J, :].rearrange("p g j c -> p j g c"),
                                                          in_=s_[:, 0:J, 384:512].rearrange("p j (g c) -> p j g c", g=2)),
                          reads=[s_], writes=[vb])
                    sm = ssum.next()
                    rsd = rstd.next()
                    gkv = gk[:, 0:J, :, :].rearrange("p j g c -> p (j g) c")
                    grp_rstd(gkv, sq, sm, rsd, J * 2, 64, [gk])
                    fw.op('dve', lambda v: v.tensor_tensor(out=gkv, in0=gkv,
                                                           in1=rsd[:, 0:J * 2].unsqueeze(2).to_broadcast([128, J * 2, 64]),
                                                           op=ALU.mult), reads=[gk, rsd], writes=[gk])
                    fw.op('dve', lambda v: v.tensor_tensor(out=gkv, in0=gkv,
                                                           in1=gbc[:, 256:320].unsqueeze(1).to_broadcast([128, J * 2, 64]),
                                                           op=ALU.mult), reads=[gk, gbc], writes=[gk])
                    if tab is not None:
                        rope(gk, J, 2, 64, 0, 64, tab, 64, tmp1, tmp2)
                    gb_ = gkb.next()
                    fw.op('pool', lambda g: g.tensor_copy(out=gb_[:, 0:J, :], in_=gk[:, 0:J].rearrange("p j g c -> p j (g c)")),
                          reads=[gk], writes=[gb_])
                    ktb = kTbst.next()
                    p = tp.next()
                    for j in range(J):
                        fw.op('pe', lambda t, p=p, j=j: t.transpose(out=p[:, j * 128:(j + 1) * 128], in_=gb_[:, j, :],
                                                                   identity=ident[:]),
                              reads=[gb_, ident], writes=[p], inc=(j == J - 1))
                    fw.op('act', lambda a, p=p: a.copy(out=ktb[:, 0:J * 128], in_=p[:, 0:J * 128]), reads=[p], writes=[ktb])
                    fw.dma(STQ, [(kTb[g, :, k0:k0 + J * 128], ktb[g * 64:(g + 1) * 64, 0:J * 128]) for g in range(2)],
                           reads=[ktb], accw=[kTb])
                    fw.dma(STQ, [(vB[:, :, t0:t0 + J, :].rearrange("g p j d -> p g (j d)"), vb[:, :, 0:J, :].rearrange("p g j d -> p g (j d)"))], reads=[vb], accw=[vB])
                run_pipelined(body, units)
                fw.barrier()

        def phase_q0(src, units, who, rtab, qTa_d, qTb_d, gT_d):
            with ExitStack() as st:
                tp = slots(st, "tp", 2, [128, 1024], BF16, psum=True)
                pj = slots(st, "pj", 2, [128, 1024], F32, psum=True)
                fm = slots(st, "fm", 2, [128, 512], F32, psum=True)
                fr = Front(st, 2, tp)
                stage = slots(st, "wst", 2, [128, 1024], F32)
                gcq = sb(st, "gcq", [128, 2], F32)
                fw.dma('sp', [(gcq[:], cq_gain[:, :])], writes=[gcq])
                wq = load_w_bf16(st, "wq", lambda k: wq0[k * 128:(k + 1) * 128, :], 768, stage)
                wg = load_w_bf16(st, "wg", lambda k: wg0[k * 128:(k + 1) * 128, :], 1024, stage)
                wuq = load_w_bf16(st, "wuq", lambda k: w_uq[k * 128:(k + 1) * 128, :], 768, stage, gain=gcq, kchunks=2)
                gbc = sb(st, "gbc", [128, 320], F32)
                fw.dma('sp', [(gbc[:], gains0[0:1, :].partition_broadcast(128))], writes=[gbc])
                fw.op('dve', lambda v: v.tensor_scalar(out=gbc[:, 0:96], in0=gbc[:, 0:96], scalar1=96.0 ** -0.5,
                                                       scalar2=None, op0=ALU.mult), reads=[gbc], writes=[gbc])
                fw.op('dve', lambda v: v.tensor_scalar(out=gbc[:, 192:256], in0=gbc[:, 192:256], scalar1=64.0 ** -0.5,
                                                       scalar2=None, op0=ALU.mult), reads=[gbc], writes=[gbc])
                sc1 = load_bc(st, "sc1", who * 3 + 1)
                sh = load_bc(st, "sh", who * 3 + 0)
                tabs = slots(st, "tab", 2, [128, 2, 192], F32)
                stp = slots(st, "stp", 2, [128, 2, 768], F32)
                sq = sb(st, "sq", [128, 1536], F32)
                ssum = slots(st, "ssum", 4, [128, 32], F32)
                rstd = slots(st, "rstd", 4, [128, 32], F32)
                cqn = slots(st, "cqn", 2, [128, 2, 256], BF16)
                cTt = slots(st, "cTt", 2, [128, 2, 256], BF16)
                stQ = slots(st, "stQ", 2, [128, 2, 8, 96], F32)
                Qn = slots(st, "Qn", 2, [128, 2, 8, 96], BF16)
                qTst = slots(st, "qTst", 2, [96, 8, 256], BF16)
                gqn = slots(st, "gqn", 2, [128, 2, 8, 64], F32)
                gqb = slots(st, "gqb", 2, [128, 2, 512], BF16)
                qTbst = slots(st, "qTbst", 2, [128, 4, 256], BF16)
                gTst = slots(st, "gTst", 2, [128, 8, 256], BF16)
                tmp1 = sq
                tmp2 = sb(st, "tmp2", [128, 1024], F32)
                def body(unit):
                    (r0, J) = unit
                    hT = fr.run(src, r0, J, sc1, sh)
                    tab = None
                    if rtab is not None:
                        tab = tabs.next()
                        fw.dma('sp', [(tab[:, 0:J, :], rtab[r0:r0 + J * 128, :].rearrange("(j p) c -> p j c", p=128))],
                               writes=[tab])
                    yield
                    s_ = stp.next()
                    for j in range(J):
                        p = pj.next()
                        for (c0, c1) in ((0, 512), (512, 768)):
                            for k in range(KC):
                                fw.op('pe', lambda t, k=k, p=p, j=j, c0=c0, c1=c1: t.matmul(
                                    p[:, c0:c1], lhsT=hT[:, k, j * 128:(j + 1) * 128], rhs=wq[:, k, c0:c1],
                                    start=(k == 0), stop=(k == KC - 1)),
                                    reads=[hT, wq], writes=[p], inc=(c0 == 512 and k == KC - 1))
                        fw.op('act', lambda a, p=p, j=j: a.copy(out=s_[:, j, :], in_=p[:, 0:768]), reads=[p], writes=[s_])
                    gt = gTst.next()
                    for c in range(8):
                        p = fm.next()
                        for k in range(KC):
                            fw.op('pe', lambda t, k=k, p=p, c=c: t.matmul(p[:, 0:J * 128], lhsT=wg[:, k, c * 128:(c + 1) * 128],
                                                                         rhs=hT[:, k, 0:J * 128], start=(k == 0),
                                                                         stop=(k == KC - 1)),
                                  reads=[hT, wg], writes=[p], inc=(k == KC - 1))
                        fw.op('act', lambda a, p=p, c=c: a.activation(out=gt[:, c, 0:J * 128], in_=p[:, 0:J * 128], func=AF.Silu),
                              reads=[p], writes=[gt])
                    fw.dma(STQ, [(gT_d[:, :, r0:r0 + J * 128].rearrange("c p t -> p c t"), gt[:, :, 0:J * 128])],
                           reads=[gt], accw=[gT_d])
                    sm = ssum.next()
                    rsd = rstd.next()
                    sqv = sq[:, 0:J * 256].rearrange("p (j c) -> p j c", j=J)
                    fw.op('dve', lambda v: v.tensor_tensor(out=sqv, in0=s_[:, 0:J, 0:256], in1=s_[:, 0:J, 0:256], op=ALU.mult),
                          reads=[s_], writes=[sq])
                    fw.op('dve', lambda v: v.tensor_reduce(out=sm[:, 0:J], in_=sqv, axis=AX.X, op=ALU.add),
                          reads=[sq], writes=[sm])
                    fw.op('act', lambda a: a.activation(out=sm[:, 0:J], in_=sm[:, 0:J], func=AF.Sqrt, bias=EPS,
                                                        scale=1.0 / 256), reads=[sm], writes=[sm])
                    fw.op('dve', lambda v: v.reciprocal(out=rsd[:, 0:J], in_=sm[:, 0:J]), reads=[sm], writes=[rsd])
                    cn = cqn.next()
                    fw.op('dve', lambda v: v.tensor_tensor(out=cn[:, 0:J, :], in0=s_[:, 0:J, 0:256],
                                                           in1=rsd[:, 0:J].unsqueeze(2).to_broadcast([128, J, 256]),
                                                           op=ALU.mult), reads=[s_, rsd], writes=[cn])
                    ct = cTt.next()
                    for j in range(J):
                        p = tp.next()
                        for k2 in range(2):
                            fw.op('pe', lambda t, k2=k2, p=p, j=j: t.transpose(out=p[:, k2 * 128:(k2 + 1) * 128],
                                                                              in_=cn[:, j, k2 * 128:(k2 + 1) * 128],
                                                                              identity=ident[:]),
                                  reads=[cn, ident], writes=[p], inc=(k2 == 1))
                        fw.op('act', lambda a, p=p, j=j: a.copy(out=ct[:, :, j * 128:(j + 1) * 128],
                                                              in_=p[:, 0:256].rearrange("p (k t) -> p k t", k=2)),
                              reads=[p], writes=[ct])
                    sQ = stQ.next()
                    for j in range(J):
                        p = pj.next()
                        for n in range(2):
                            for k2 in range(2):
                                fw.op('pe', lambda t, k2=k2, n=n, p=p, j=j: t.matmul(
                                    p[:, n * 512:n * 512 + 384], lhsT=ct[:, k2, j * 128:(j + 1) * 128],
                                    rhs=wuq[:, k2, n * 384:(n + 1) * 384], start=(k2 == 0), stop=(k2 == 1)),
                                    reads=[ct, wuq], writes=[p], inc=(n == 1 and k2 == 1))
                        fw.op('act', lambda a, j=j, p=p: a.copy(
                            out=sQ[:, j, :, :].rearrange("p (a h) c -> p a (h c)", a=2),
                            in_=p[:].rearrange("p (a n) -> p a n", a=2)[:, :, 0:384]), reads=[p], writes=[sQ])
                    yield
                    sm = ssum.next()
                    rsd = rstd.next()
                    sQv = sQ[:, 0:J, :, :].rearrange("p j h c -> p (j h) c")
                    grp_rstd(sQv, sq, sm, rsd, J * 8, 96, [sQ])
                    fw.op('dve', lambda v: v.tensor_tensor(out=sQv, in0=sQv,
                                                           in1=rsd[:, 0:J * 8].unsqueeze(2).to_broadcast([128, J * 8, 96]),
                                                           op=ALU.mult), reads=[sQ, rsd], writes=[sQ])
                    fw.op('dve', lambda v: v.tensor_tensor(out=sQv, in0=sQv,
                                                           in1=gbc[:, 0:96].unsqueeze(1).to_broadcast([128, J * 8, 96]),
                                                           op=ALU.mult), reads=[sQ, gbc], writes=[sQ])
                    if tab is not None:
                        rope(sQ, J, 8, 96, 64, 32, tab, 0, tmp1, tmp2)
                    qn = Qn.next()
                    fw.op('pool', lambda g: g.tensor_copy(out=qn[:, 0:J], in_=sQ[:, 0:J]), reads=[sQ], writes=[qn])
                    qt = qTst.next()
                    for j in range(J):
                        p = tp.next()
                        for h in range(8):
                            fw.op('pe', lambda t, h=h, p=p, j=j: t.transpose(out=p[0:96, h * 128:(h + 1) * 128],
                                                                            in_=qn[:, j, h, :], identity=ident[:]),
                                  reads=[qn, ident], writes=[p], inc=(h == 7))
                        fw.op('act', lambda a, p=p, j=j: a.copy(out=qt[:, :, j * 128:(j + 1) * 128],
                                                              in_=p[0:96, :].rearrange("p (h t) -> p h t", h=8)),
                              reads=[p], writes=[qt])
                    fw.dma(STQ, [(qTa_d[:, :, r0:r0 + J * 128].rearrange("h d t -> d h t"), qt[:, :, 0:J * 128])],
                           reads=[qt], accw=[qTa_d])
                    gq = gqn.next()
                    fw.op('pool', lambda g: g.tensor_copy(out=gq[:, 0:J].rearrange("p j h c -> p j (h c)"),
                                                          in_=s_[:, 0:J, 256:768]), reads=[s_], writes=[gq])
                    sm = ssum.next()
                    rsd = rstd.next()
                    gqv = gq[:, 0:J, :, :].rearrange("p j h c -> p (j h) c")
                    grp_rstd(gqv, sq, sm, rsd, J * 8, 64, [gq])
                    fw.op('dve', lambda v: v.tensor_tensor(out=gqv, in0=gqv,
                                                           in1=rsd[:, 0:J * 8].unsqueeze(2).to_broadcast([128, J * 8, 64]),
                                                           op=ALU.mult), reads=[gq, rsd], writes=[gq])
                    fw.op('dve', lambda v: v.tensor_tensor(out=gqv, in0=gqv,
                                                           in1=gbc[:, 192:256].unsqueeze(1).to_broadcast([128, J * 8, 64]),
                                                           op=ALU.mult), reads=[gq, gbc], writes=[gq])
                    if tab is not None:
                        rope(gq, J, 8, 64, 0, 64, tab, 64, tmp1, tmp2)
                    gb_ = gqb.next()
                    fw.op('pool', lambda g: g.tensor_copy(out=gb_[:, 0:J, :], in_=gq[:, 0:J].rearrange("p j h c -> p j (h c)")),
                          reads=[gq], writes=[gb_])
                    qtb = qTbst.next()
                    for j in range(J):
                        p = tp.next()
                        for m in range(4):
                            fw.op('pe', lambda t, m=m, p=p, j=j: t.transpose(out=p[:, m * 128:(m + 1) * 128],
                                                                            in_=gb_[:, j, m * 128:(m + 1) * 128],
                                                                            identity=ident[:]),
                                  reads=[gb_, ident], writes=[p], inc=(m == 3))
                        fw.op('act', lambda a, p=p, j=j: a.copy(out=qtb[:, :, j * 128:(j + 1) * 128],
                                                              in_=p[:, 0:512].rearrange("p (m t) -> p m t", m=4)),
                              reads=[p], writes=[qtb])
                    fw.dma(STQ, [(qTb_d[:, :, r0:r0 + J * 128].rearrange("(m two) d t -> (two d) m t", two=2),
                                     qtb[:, :, 0:J * 128])], reads=[qtb], accw=[qTb_d])
                run_pipelined(body, units)
                fw.barrier()

        def phase_attn0(T, kt0, kt1, qTa_d, qTb_d, gT_d, mix_d):
            nkt = kt1 - kt0
            with ExitStack() as st:
                pss = slots(st, "pss", 2, [128, 3, 512], F32, psum=True)
                pso = slots(st, "pso", 2, [128, 512], F32, psum=True)
                kT = sb(st, "kT", [128, nkt * 128], BF16)
                Ve = sb(st, "Ve", [128, nkt, 128], BF16)
                Vo = sb(st, "Vo", [128, nkt, 128], BF16)
                qTs = slots(st, "qTs", 2, [128, T], BF16)
                Vstg = slots(st, "Vstg", 2, [128, nkt, 64], BF16)
                gTs = slots(st, "gTs", 2, [128, T], BF16)
                mxs = slots(st, "mxs", 2, [128, T], BF16)
                pT = slots(st, "pT", 3, [128, 3, 512], BF16)
                Rs = slots(st, "Rs", 2, [128, 512], F32)
                t32 = slots(st, "t32", 2, [128, 512], F32)
                fw.op('pool', lambda g: g.memset(Ve[:, :, 64:128], 1.0), writes=[Ve])
                fw.op('pool', lambda g: g.memset(Vo[:, :, 0:64], 1.0), writes=[Vo])
                qsup = []
                q0 = 0
                while q0 < T:
                    wq_ = min(512, T - q0)
                    qsup.append((q0, wq_))
                    q0 += wq_
                for c in range(8):
                    gt = gTs.next()
                    fw.dma('sp', [(gt[:], gT_d[c, :, :])], reads=[gT_d], writes=[gt])
                    mx = mxs.next()
                    for half in range(2):
                        hh = 2 * c + half
                        if c < 4:
                            d = 96
                            ksrc, vsrc, qsrc = kTa[hh], vA[hh], qTa_d[hh]
                            newk = True
                            newv = True
                        else:
                            d = 64
                            qh = hh - 8
                            g_ = qh // 4
                            ksrc, vsrc, qsrc = kTb[g_], vB[g_], qTb_d[qh]
                            newk = (qh % 4 == 0)
                            newv = (qh % 4 < 2)
                        V = Ve if half == 0 else Vo
                        vcol = 0 if half == 0 else 64
                        if newk:
                            nsp = 4 if nkt >= 8 else 1
                            stp_ = (nkt * 128) // nsp
                            for i in range(nsp):
                                fw.dma('sp', [(kT[0:d, i * stp_:(i + 1) * stp_], ksrc[:, kt0 * 128 + i * stp_: kt0 * 128 + (i + 1) * stp_])],
                                       reads=[kTa, kTb], writes=[kT] if i == 0 else (), accw=() if i == 0 else [kT])
                        if newv:
                            vg = Vstg.next()
                            nsv = 4 if nkt >= 8 else 1
                            stv = nkt // nsv
                            for i in range(nsv):
                                a0 = i * stv
                                a1 = nkt if i == nsv - 1 else (i + 1) * stv
                                fw.dma('sp', [(vg[:, a0:a1, :], vsrc[:, kt0 + a0:kt0 + a1, :])], reads=[vA, vB],
                                       writes=[vg] if i == 0 else (), accw=() if i == 0 else [vg])
                            fw.op('dve', lambda v, vg=vg, V=V, vcol=vcol: v.tensor_copy(out=V[:, :, vcol:vcol + 64], in_=vg[:]),
                                  reads=[vg], writes=[V])
                        qT = qTs.next()
                        fw.dma('sp', [(qT[0:d, :], qsrc[:, :])], reads=[qTa_d, qTb_d], writes=[qT])
                        for (q0, wq_) in qsup:
                            po = pso.next()
                            pairs = [(i, min(3, nkt - i)) for i in range(0, nkt, 3)]

                            def qk(pi):
                                i0, n = pairs[pi]
                                p = pss.next()
                                for i in range(n):
                                    fw.op('pe', lambda t, i=i, p=p: t.matmul(
                                        p[:, i, 0:wq_], lhsT=kT[0:d, (i0 + i) * 128:(i0 + i + 1) * 128],
                                        rhs=qT[0:d, q0:q0 + wq_], start=True, stop=True),
                                        reads=[kT, qT], writes=[p], inc=(i == n - 1))
                                return p

                            pend = qk(0)
                            for pi in range(len(pairs)):
                                i0, n = pairs[pi]
                                p = pend
                                pt = pT.next()
                                fw.op('act', lambda a, p=p, pt=pt, n=n: a.activation(out=pt[:, 0:n, 0:wq_], in_=p[:, 0:n, 0:wq_],
                                                                                    func=AF.Exp), reads=[p], writes=[pt])
                                if pi + 1 < len(pairs):
                                    pend = qk(pi + 1)
                                for i in range(n):
                                    kt_ = i0 + i
                                    fw.op('pe', lambda t, i=i, pt=pt, kt_=kt_: t.matmul(
                                        po[:, 0:wq_], lhsT=V[:, kt_, :], rhs=pt[:, i, 0:wq_],
                                        start=(kt_ == 0), stop=(kt_ == nkt - 1)),
                                        reads=[V, pt], writes=[po], inc=(i == n - 1))
                            R = Rs.next()
                            tt = t32.next()
                            o0, s0 = (0, 64) if half == 0 else (64, 0)
                            fw.op('dve', lambda v, R=R: v.reciprocal(out=R[o0:o0 + 64, 0:wq_], in_=po[s0:s0 + 64, 0:wq_]),
                                  reads=[po], writes=[R])
                            fw.op('dve', lambda v, R=R, tt=tt: v.tensor_tensor(out=tt[o0:o0 + 64, 0:wq_], in0=po[o0:o0 + 64, 0:wq_],
                                                                               in1=R[o0:o0 + 64, 0:wq_], op=ALU.mult),
                                  reads=[po, R], writes=[tt])
                            fw.op('pool', lambda g, tt=tt: g.tensor_tensor(out=mx[o0:o0 + 64, q0:q0 + wq_],
                                                                          in0=tt[o0:o0 + 64, 0:wq_],
                                                                          in1=gt[o0:o0 + 64, q0:q0 + wq_], op=ALU.mult),
                                  reads=[tt, gt], writes=[mx])
                    fw.dma(STQ, [(mix_d[c, :, :], mx[:])], reads=[mx], accw=[mix_d])
                fw.barrier()

        def phase_out(src, units, mix_d, wout_d, gate_idx, dst, dst_r0=None, st_outer=None):
            with ExitStack() as st:
                pj = slots(st, "pj", 2, [128, 1024], F32, psum=True)
                stage = slots(st, "wst", 2, [128, 1024], F32)
                wo = load_w_bf16(st, "wo", lambda k: wout_d[k * 128:(k + 1) * 128, :], 1024, stage)
                gbc_ = load_bc(st, "gatebc", gate_idx)
                mxs = slots(st, "mxs", 2, [128, 8, 512], BF16)
                xts = slots(st, "xts", 2, [128, 4, D], F32)
                t32 = slots(st, "t32", 2, [128, D], F32)
                xo = slots(st, "xo", 2, [128, 4, D], F32)
                for (r0, J) in units:
                    mx = mxs.next()
                    fw.dma('sp', [(mx[:, :, 0:J * 128], mix_d[:, :, r0:r0 + J * 128].rearrange("c p t -> p c t"))],
                           reads=[mix_d], writes=[mx])
                    xt = xts.next()
                    fw.dma('sp', [(xt[:, 0:J, :], src[r0:r0 + J * 128, :].rearrange("(j p) d -> p j d", p=128))],
                           reads=[src], writes=[xt])
                    xo_ = xo.next()
                    for j in range(J):
                        p = pj.next()
                        for n in range(2):
                            for k in range(KC):
                                fw.op('pe', lambda t, k=k, n=n, p=p, j=j: t.matmul(
                                    p[:, n * 512:(n + 1) * 512], lhsT=mx[:, k, j * 128:(j + 1) * 128],
                                    rhs=wo[:, k, n * 512:(n + 1) * 512], start=(k == 0), stop=(k == KC - 1)),
                                    reads=[mx, wo], writes=[p], inc=(n == 1 and k == KC - 1))
                        tt = t32.next()
                        fw.op('dve', lambda v, p=p, tt=tt: v.tensor_tensor(out=tt[:], in0=p[:], in1=gbc_[:], op=ALU.mult),
                              reads=[p, gbc_], writes=[tt])
                        fw.op('pool', lambda g, tt=tt, j=j: g.tensor_tensor(out=xo_[:, j, :], in0=tt[:], in1=xt[:, j, :], op=ALU.add),
                              reads=[tt, xt], writes=[xo_])
                    d0 = r0 if dst_r0 is None else r0 - dst_r0
                    fw.dma(STQ, [(dst[d0:d0 + J * 128, :].rearrange("(j p) d -> p j d", p=128), xo_[:, 0:J, :])],
                           reads=[xo_], accw=[dst])
                fw.barrier()

        units_q = [(u * 512, 4) for u in range(T_q // 512)]
        if T_q % 512:
            units_q.append(((T_q // 512) * 512, (T_q % 512) // 128))
        units_q2 = [(u * 256, 2) for u in range(T_q // 256)]
        chk('setup')
        phase_kv0()
        chk('kv0')
        phase_q0(xq, units_q2, 0, ropeq, qTa, qTb, gT0)
        phase_q0(ctxb, [(0, 2)], 1, None, qTa_c, qTb_c, gT0_c)
        chk('q0')
        phase_attn0(T_q, 0, NKT, qTa, qTb, gT0, mixT0)
        phase_attn0(L, 0, 2, qTa_c, qTb_c, gT0_c, mixT0_c)
        chk('attn0')
        phase_out(xq, units_q, mixT0, wout0, 2, x1)
        phase_out(ctxb, [(0, 2)], mixT0_c, wout0, 5, xc1)
        chk('out0')

        with ExitStack() as st:
            KT = sb(st, "KT", [128, T_q], BF16)
            VX = sb(st, "VX", [128, NQT, 2, 128], BF16)
            KTc = sb(st, "KTc", [128, L], BF16)
            VXc = sb(st, "VXc", [128, 2, 2, 128], BF16)
            g1bc = sb(st, "g1bc", [128, 128], F32)
            fw.dma('sp', [(g1bc[:], gains1[0:1, :].partition_broadcast(128))], writes=[g1bc])
            fw.op('dve', lambda v: v.tensor_scalar(out=g1bc[:, 0:64], in0=g1bc[:, 0:64], scalar1=64.0 ** -0.5,
                                                   scalar2=None, op0=ALU.mult), reads=[g1bc], writes=[g1bc])
            fw.op('pool', lambda g: g.memset(VXc[:, :, :, 64:128], 1.0), writes=[VXc])

            with ExitStack() as s2:
                tp = slots(s2, "tp", 2, [128, 1024], BF16, psum=True)
                pj = slots(s2, "pj", 2, [128, 1024], F32, psum=True)
                fm = slots(s2, "fm", 2, [128, 512], F32, psum=True)
                fr = Front(s2, 2, tp)
                stage = slots(s2, "wst", 2, [128, 1024], F32)
                wt = load_w_bf16(s2, "wt", lambda k: wt1[k * 128:(k + 1) * 128, :], 768, stage)
                wf = sb(s2, "wf", [128, KC, 2560], BF16)
                for k in range(KC):
                    for (c0, c1) in ((0, 1024), (1024, 2048), (2048, 2560)):
                        s_ = stage.next()
                        fw.dma('sp', [(s_[:, 0:c1 - c0], wf1[k * 128:(k + 1) * 128, c0:c1])], writes=[s_])
                        fw.op('pool', lambda g, k=k, s_=s_, c0=c0, c1=c1: g.tensor_copy(out=wf[:, k, c0:c1], in_=s_[:, 0:c1 - c0]),
                              reads=[s_], writes=[wf])
                bcs = {0: (load_bc(s2, "sc1t", 7), load_bc(s2, "sht", 6)), 1: (load_bc(s2, "sc1c", 10), load_bc(s2, "shc", 9))}
                tabs = slots(s2, "tab", 2, [128, 2, 192], F32)
                stp = slots(s2, "stp", 2, [128, 2, 768], F32)
                ssum = slots(s2, "ssum", 4, [128, 40], F32)
                rstd = slots(s2, "rstd", 4, [128, 40], F32)
                qkn = slots(s2, "qkn", 2, [128, 2, 10, 64], F32)
                qkb = slots(s2, "qkb", 2, [128, 2, 640], BF16)
                tmp1 = sb(s2, "tmp1", [128, 1280], F32)
                tmp2 = sb(s2, "tmp2", [128, 1280], F32)
                qst = slots(s2, "qst", 2, [128, 4, 256], BF16)
                ust = slots(s2, "ust", 2, [128, 4, 256], BF16)
                gcs = slots(s2, "gcs", 2, [128, 256], BF16)
                gbst = slots(s2, "gbst", 2, [128, 4, 256], BF16)
                sgst = slots(s2, "sgst", 2, [128, 8, 256], BF16)

                units1 = [(xc1, 0, 2, 1, None)] + [(x1, r0, J, 0, r0) for (r0, J) in units_q2]
                def body(unit):
                    (src, r0, J, who, rp0) = unit
                    sc1, sh = bcs[who]
                    hT = fr.run(src, r0, J, sc1, sh)
                    tab = None
                    if rp0 is not None:
                        tab = tabs.next()
                        fw.dma('sp', [(tab[:, 0:J, :], ropeq[rp0:rp0 + J * 128, :].rearrange("(j p) c -> p j c", p=128))],
                               writes=[tab])
                    yield
                    s_ = stp.next()
                    for j in range(J):
                        p = pj.next()
                        for (c0, c1) in ((0, 512), (512, 768)):
                            for k in range(KC):
                                fw.op('pe', lambda t, k=k, p=p, j=j, c0=c0, c1=c1: t.matmul(
                                    p[:, c0:c1], lhsT=hT[:, k, j * 128:(j + 1) * 128], rhs=wt[:, k, c0:c1],
                                    start=(k == 0), stop=(k == KC - 1)),
                                    reads=[hT, wt], writes=[p], inc=(c0 == 512 and k == KC - 1))
                        fw.op('act', lambda a, p=p, j=j: a.copy(out=s_[:, j, :], in_=p[:, 0:768]), reads=[p], writes=[s_])
                    qk = qkn.next()
                    fw.op('pool', lambda g: g.tensor_copy(out=qk[:, 0:J].rearrange("p j h c -> p j (h c)"), in_=s_[:, 0:J, 0:640]),
                          reads=[s_], writes=[qk])
                    sm = ssum.next()
                    rsd = rstd.next()
                    qkv = qk[:, 0:J, :, :].rearrange("p j h c -> p (j h) c")
                    sq2 = tmp1[:, 0:J * 640].rearrange("p (n c) -> p n c", c=64)
                    fw.op('dve', lambda v: v.tensor_tensor(out=sq2, in0=qkv, in1=qkv, op=ALU.mult), reads=[qk], writes=[tmp1])
                    fw.op('dve', lambda v: v.tensor_reduce(out=sm[:, 0:J * 10], in_=sq2, axis=AX.X, op=ALU.add),
                          reads=[tmp1], writes=[sm])
                    fw.op('act', lambda a: a.activation(out=sm[:, 0:J * 10], in_=sm[:, 0:J * 10], func=AF.Sqrt, bias=EPS,
                                                        scale=1.0 / 64), reads=[sm], writes=[sm])
                    fw.op('dve', lambda v: v.reciprocal(out=rsd[:, 0:J * 10], in_=sm[:, 0:J * 10]), reads=[sm], writes=[rsd])
                    fw.op('dve', lambda v: v.tensor_tensor(out=qkv, in0=qkv,
                                                           in1=rsd[:, 0:J * 10].unsqueeze(2).to_broadcast([128, J * 10, 64]),
                                                           op=ALU.mult), reads=[qk, rsd], writes=[qk])
                    fw.op('dve', lambda v: v.tensor_tensor(out=qk[:, 0:J, 0:8, :], in0=qk[:, 0:J, 0:8, :],
                                                           in1=g1bc[:, 0:64].unsqueeze(1).unsqueeze(1).to_broadcast([128, J, 8, 64]),
                                                           op=ALU.mult), reads=[qk, g1bc], writes=[qk])
                    fw.op('dve', lambda v: v.tensor_tensor(out=qk[:, 0:J, 8:10, :], in0=qk[:, 0:J, 8:10, :],
                                                           in1=g1bc[:, 64:128].unsqueeze(1).unsqueeze(1).to_broadcast([128, J, 2, 64]),
                                                           op=ALU.mult), reads=[qk, g1bc], writes=[qk])
                    if tab is not None:
                        rope(qk, J, 10, 64, 0, 64, tab, 64, tmp1, tmp2)
                    qb = qkb.next()
                    fw.op('pool', lambda g: g.tensor_copy(out=qb[:, 0:J, :], in_=qk[:, 0:J].rearrange("p j h c -> p j (h c)")),
                          reads=[qk], writes=[qb])
                    if who == 0:
                        tl0 = r0 // 128
                        for j in range(J):
                            tl = tl0 + j
                            fw.op('pool', lambda g, j=j, tl=tl: g.tensor_copy(
                                out=VX[:, tl, :, 0:64], in_=s_[:, j, 640:768].rearrange("p (g c) -> p g c", g=2)),
                                reads=[s_], writes=[VX])
                            fw.op('pool', lambda g, tl=tl: g.memset(VX[:, tl, :, 64:128], 1.0), reads=[], writes=[VX])
                            if tl == 0 or tl == NQT - 1:
                                vc = 0 if tl == 0 else 1
                                fw.op('dve', lambda v, tl=tl, vc=vc: v.tensor_scalar(
                                    out=VX[:, tl, :, :], in0=VX[:, tl, :, :], scalar1=validt[:, vc:vc + 1], scalar2=None,
                                    op0=ALU.mult), reads=[VX, validt], writes=[VX])
                    else:
                        for j in range(J):
                            fw.op('pool', lambda g, j=j: g.tensor_copy(
                                out=VXc[:, j, :, 0:64], in_=s_[:, j, 640:768].rearrange("p (g c) -> p g c", g=2)),
                                reads=[s_], writes=[VXc])
                    yield
                    qs = qst.next() if who == 0 else None
                    for j in range(J):
                        p = tp.next()
                        for m in range(4):
                            fw.op('pe', lambda t, m=m, p=p, j=j: t.transpose(
                                out=p[0:64, m * 128:(m + 1) * 128], in_=qb[:, j, m * 64:(m + 1) * 64], identity=ident[:]),
                                reads=[qb, ident], writes=[p], inc=False)
                            fw.op('pe', lambda t, m=m, p=p, j=j: t.transpose(
                                out=p[64:128, m * 128:(m + 1) * 128], in_=qb[:, j, (m + 4) * 64:(m + 5) * 64], identity=ident[:]),
                                reads=[qb, ident], writes=[p], inc=False)
                        fw.op('pe', lambda t, p=p, j=j: t.transpose(out=p[:, 512:640], in_=qb[:, j, 512:640], identity=ident[:]),
                              reads=[qb, ident], writes=[p], inc=True)
                        if who == 0:
                            c0 = r0 + j * 128
                            fw.op('act', lambda a, p=p, j=j: a.copy(out=qs[:, :, j * 128:(j + 1) * 128],
                                                                   in_=p[:, 0:512].rearrange("p (m t) -> p m t", m=4)),
                                  reads=[p], writes=[qs])
                            fw.op('act', lambda a, p=p, c0=c0: a.copy(out=KT[:, c0:c0 + 128], in_=p[:, 512:640]),
                                  reads=[p], writes=[KT])
                        else:
                            fw.op('act', lambda a, p=p, j=j: a.copy(out=KTc[:, j * 128:(j + 1) * 128], in_=p[:, 512:640]),
                                  reads=[p], writes=[KTc])
                    if who == 1:
                        return
                    fw.dma(STQ, [(qT1[:, :, r0:r0 + J * 128].rearrange("m p t -> p m t"), qs[:, :, 0:J * 128])],
                           reads=[qs], accw=[qT1])
                    us = ust.next()
                    gb_ = gbst.next()
                    sg_ = sgst.next()
                    W_ = J * 128
                    for c in range(4):
                        p = fm.next()
                        for k in range(KC):
                            fw.op('pe', lambda t, k=k, p=p, c=c: t.matmul(p[:, 0:W_], lhsT=wf[:, k, c * 128:(c + 1) * 128],
                                                                         rhs=hT[:, k, 0:W_], start=(k == 0), stop=(k == KC - 1)),
                                  reads=[hT, wf], writes=[p], inc=(k == KC - 1))
                        fw.op('act', lambda a, p=p, c=c: a.copy(out=gb_[:, c, 0:W_], in_=p[:, 0:W_]), reads=[p], writes=[gb_])
                        p = fm.next()
                        for k in range(KC):
                            fw.op('pe', lambda t, k=k, p=p, c=c: t.matmul(p[:, 0:W_], lhsT=wf[:, k, 512 + c * 128:512 + (c + 1) * 128],
                                                                         rhs=hT[:, k, 0:W_], start=(k == 0), stop=(k == KC - 1)),
                                  reads=[hT, wf], writes=[p], inc=(k == KC - 1))
                        gc_ = gcs.next()
                        fw.op('act', lambda a, p=p, gc_=gc_: a.copy(out=gc_[:, 0:W_], in_=p[:, 0:W_]), reads=[p], writes=[gc_])
                        p = fm.next()
                        for k in range(KC):
                            fw.op('pe', lambda t, k=k, p=p, c=c: t.matmul(p[:, 0:W_], lhsT=wf[:, k, 1024 + c * 128:1024 + (c + 1) * 128],
                                                                         rhs=hT[:, k, 0:W_], start=(k == 0), stop=(k == KC - 1)),
                                  reads=[hT, wf], writes=[p], inc=(k == KC - 1))
                        fw.op('dve', lambda v, p=p, gc_=gc_, c=c: v.tensor_tensor(out=us[:, c, 0:W_], in0=p[:, 0:W_],
                                                                                 in1=gc_[:, 0:W_], op=ALU.mult),
                              reads=[p, gc_], writes=[us])
                    for c in range(8):
                        p = fm.next()
                        for k in range(KC):
                            fw.op('pe', lambda t, k=k, p=p, c=c: t.matmul(p[:, 0:W_], lhsT=wf[:, k, 1536 + c * 128:1536 + (c + 1) * 128],
                                                                         rhs=hT[:, k, 0:W_], start=(k == 0), stop=(k == KC - 1)),
                                  reads=[hT, wf], writes=[p], inc=(k == KC - 1))
                        fw.op('act', lambda a, p=p, c=c: a.activation(out=sg_[:, c, 0:W_], in_=p[:, 0:W_], func=AF.Silu),
                              reads=[p], writes=[sg_])
                    if r0 == 0:
                        fw.op('dve', lambda v: v.tensor_scalar(out=us[:, :, 0:128], in0=us[:, :, 0:128], scalar1=validt[:, 0:1],
                                                               scalar2=None, op0=ALU.mult), reads=[us, validt], writes=[us])
                    if r0 + W_ == T_q:
                        fw.op('dve', lambda v: v.tensor_scalar(out=us[:, :, W_ - 128:W_], in0=us[:, :, W_ - 128:W_],
                                                               scalar1=validt[:, 1:2], scalar2=None, op0=ALU.mult),
                              reads=[us, validt], writes=[us])
                    fw.dma(STQ, [(uT[:, :, r0:r0 + W_].rearrange("c p t -> p c t"), us[:, :, 0:W_])], reads=[us], accw=[uT])
                    fw.dma(STQ, [(gbT[:, :, r0:r0 + W_].rearrange("c p t -> p c t"), gb_[:, :, 0:W_])], reads=[gb_], accw=[gbT])
                    fw.dma(STQ, [(sgT[:, :, r0:r0 + W_].rearrange("c p t -> p c t"), sg_[:, :, 0:W_])], reads=[sg_], accw=[sgT])
                run_pipelined(body, units1)
                fw.barrier()

            chk('d1')
            with ExitStack() as s3:
                pss = slots(s3, "pss", 2, [128, 2, 512], F32, psum=True)
                pso = slots(s3, "pso", 2, [128, 512], F32, psum=True)
                pj = slots(s3, "pj", 1, [128, 1024], F32, psum=True)
                stage = slots(s3, "wst", 2, [128, 1024], F32)
                wo = load_w_bf16(s3, "wo", lambda k: wout1[k * 128:(k + 1) * 128, :], 1024, stage)
                gate1 = load_bc(s3, "gate1", 8)
                cw = sb(s3, "cw", [128, 4, 3], F32)
                fw.dma('sp', [(cw[:], convw[:, :, :])], writes=[cw])
                esk = sb(s3, "esk", [128, 8], F32)
                fw.dma('sp', [(esk[:], sink[0:1, :].partition_broadcast(128))], writes=[esk])
                fw.op('act', lambda a: a.activation(out=esk[:], in_=esk[:], func=AF.Exp), reads=[esk], writes=[esk])
                esf = sb(s3, "esf", [128, 2, 4, 128], F32)
                fw.op('pool', lambda g: g.memset(esf[:], 0.0), writes=[esf])
                for g_ in range(2):
                    fw.op('dve', lambda v, g_=g_: v.tensor_tensor(
                        out=esf[:, g_, :, :], in0=esf[:, g_, :, :],
                        in1=esk[:, g_ * 4:(g_ + 1) * 4].unsqueeze(2).to_broadcast([128, 4, 128]), op=ALU.add),
                        reads=[esf, esk], writes=[esf])
                mprev = sb(s3, "mprev", [128, 128], BF16)
                mnext = sb(s3, "mnext", [128, 128], BF16)
                fw.op('pool', lambda g: g.memset(mprev[:], 1.0), writes=[mprev])
                fw.op('pool', lambda g: g.memset(mnext[:], 1.0), writes=[mnext])
                fw.op('pool', lambda g: g.affine_select(out=mprev[:], in_=mprev[:], pattern=[[-1, 128]], compare_op=ALU.is_ge,
                                                        fill=0.0, base=0, channel_multiplier=1), reads=[mprev], writes=[mprev])
                fw.op('pool', lambda g: g.affine_select(out=mnext[:], in_=mnext[:], pattern=[[1, 128]], compare_op=ALU.is_ge,
                                                        fill=0.0, base=0, channel_multiplier=-1), reads=[mnext], writes=[mnext])
                pT = slots(s3, "pT", 3, [128, 2, 512], BF16)
                Rs = slots(s3, "Rs", 2, [128, 512], F32)
                gbs = slots(s3, "gbs", 2, [128, 4, 512], BF16)
                sgs = slots(s3, "sgs", 2, [128, 8, 512], BF16)
                mxu = slots(s3, "mxu", 2, [128, 8, 512], BF16)
                acc = slots(s3, "acc", 2, [128, 512], F32)
                xts = slots(s3, "xts", 2, [128, 4, D], F32)
                t32 = slots(s3, "t32", 2, [128, D], F32)
                xo = slots(s3, "xo", 2, [128, 4, D], F32)
                qsu = slots(s3, "qsu", 2, [128, 4, 512], BF16)
                usu = slots(s3, "usu", 2, [128, 4, 514], BF16)
                for u in range(T_own // 512):
                    r0 = 128 + u * 512
                    gb_ = gbs.next()
                    sg_ = sgs.next()
                    fw.dma('sp', [(gb_[:], gbT[:, :, r0:r0 + 512].rearrange("c p t -> p c t"))], reads=[gbT], writes=[gb_])
                    fw.dma('sp', [(sg_[:], sgT[:, :, r0:r0 + 512].rearrange("c p t -> p c t"))], reads=[sgT], writes=[sg_])
                    xt = xts.next()
                    fw.dma('sp', [(xt[:], x1[r0:r0 + 512, :].rearrange("(j p) d -> p j d", p=128))], reads=[x1], writes=[xt])
                    mx = mxu.next()
                    qs = qsu.next()
                    us = usu.next()
                    fw.dma('sp', [(qs[:], qT1[:, :, r0:r0 + 512].rearrange("m p t -> p m t"))], reads=[qT1], writes=[qs])
                    fw.dma('sp', [(us[:], uT[:, :, r0 - 1:r0 + 513].rearrange("c p t -> p c t"))], reads=[uT], writes=[us])
                    for c in range(4):
                        a_ = acc.next()
                        fw.op('dve', lambda v, c=c, a_=a_: v.tensor_scalar(out=a_[:], in0=us[:, c, 0:512], scalar1=cw[:, c, 0:1],
                                                                          scalar2=None, op0=ALU.mult), reads=[us, cw], writes=[a_])
                        fw.op('dve', lambda v, c=c, a_=a_: v.scalar_tensor_tensor(out=a_[:], in0=us[:, c, 1:513],
                                                                                 scalar=cw[:, c, 1:2], in1=a_[:], op0=ALU.mult,
                                                                                 op1=ALU.add), reads=[us, cw, a_], writes=[a_])
                        fw.op('dve', lambda v, c=c, a_=a_: v.scalar_tensor_tensor(out=a_[:], in0=us[:, c, 2:514],
                                                                                 scalar=cw[:, c, 2:3], in1=a_[:], op0=ALU.mult,
                                                                                 op1=ALU.add), reads=[us, cw, a_], writes=[a_])
                        fw.op('pool', lambda g, c=c, a_=a_: g.tensor_tensor(out=a_[:], in0=a_[:], in1=gb_[:, c, :], op=ALU.mult),
                              reads=[a_, gb_], writes=[a_])
                        fw.op('pool', lambda g, c=c, a_=a_: g.tensor_tensor(out=mx[:, 4 + c, :], in0=a_[:], in1=sg_[:, 4 + c, :], op=ALU.mult),
                              reads=[a_, sg_], writes=[mx])
                    for jb in range(4):
                        tl = r0 // 128 + jb
                        c0 = tl * 128
                        for g_ in range(2):
                            pr = slice(g_ * 64, (g_ + 1) * 64)
                            po = pso.next()
                            batches = [[('c', 0), ('c', 1)], [('l', tl - 1), ('l', tl)], [('l', tl + 1)]]
                            nmm = 5
                            cnt = 0
                            for bt in batches:
                                p = pss.next()
                                for i, (kind, ti) in enumerate(bt):
                                    lhs = KTc[pr, ti * 128:(ti + 1) * 128] if kind == 'c' else KT[pr, ti * 128:(ti + 1) * 128]
                                    fw.op('pe', lambda t, i=i, p=p, lhs=lhs: t.matmul(
                                        p[:, i, :], lhsT=lhs, rhs=qs[pr, :, jb * 128:(jb + 1) * 128], start=True, stop=True),
                                        reads=[KTc, KT, qs], writes=[p], inc=(i == len(bt) - 1))
                                pt = pT.next()
                                n = len(bt)
                                fw.op('act', lambda a, p=p, pt=pt, n=n: a.activation(out=pt[:, 0:n, :], in_=p[:, 0:n, :], func=AF.Exp),
                                      reads=[p], writes=[pt])
                                for i, (kind, ti) in enumerate(bt):
                                    if kind == 'l' and ti != tl:
                                        msk = mprev if ti < tl else mnext
                                        fw.op('dve', lambda v, i=i, pt=pt, msk=msk: v.tensor_tensor(
                                            out=pt[:, i, :].rearrange("p (m t) -> p m t", m=4),
                                            in0=pt[:, i, :].rearrange("p (m t) -> p m t", m=4),
                                            in1=msk[:].unsqueeze(1).to_broadcast([128, 4, 128]), op=ALU.mult),
                                            reads=[pt, msk], writes=[pt])
                                for i, (kind, ti) in enumerate(bt):
                                    lhs = VXc[:, ti, g_, :] if kind == 'c' else VX[:, ti, g_, :]
                                    fw.op('pe', lambda t, i=i, pt=pt, lhs=lhs, cnt=cnt: t.matmul(
                                        po[:], lhsT=lhs, rhs=pt[:, i, :], start=(cnt == 0), stop=(cnt == nmm - 1)),
                                        reads=[VXc, VX, pt], writes=[po], inc=(i == len(bt) - 1))
                                    cnt += 1
                            R = Rs.next()
                            fw.op('dve', lambda v, R=R, po=po, g_=g_: v.tensor_tensor(
                                out=R[0:64, :], in0=po[64:128, :], in1=esf[64:128, g_, :, :].rearrange("p m t -> p (m t)"),
                                op=ALU.add), reads=[po, esf], writes=[R])
                            fw.op('dve', lambda v, R=R: v.reciprocal(out=R[0:64, :], in_=R[0:64, :]), reads=[R], writes=[R])
                            fw.op('dve', lambda v, R=R, po=po: v.tensor_tensor(out=R[0:64, :], in0=po[0:64, :], in1=R[0:64, :], op=ALU.mult),
                                  reads=[po, R], writes=[R])
                            Rv = R[0:64, :].rearrange("p (c two t) -> p c two t", c=2, two=2)
                            for hf in range(2):
                                fw.op('pool', lambda g, hf=hf, Rv=Rv, g_=g_, jb=jb: g.tensor_copy(
                                    out=mx[hf * 64:(hf + 1) * 64, 2 * g_:2 * g_ + 2, jb * 128:(jb + 1) * 128],
                                    in_=Rv[:, :, hf, :]), reads=[R], writes=[mx])
                    fw.op('dve', lambda v: v.tensor_tensor(out=mx[:, 0:4, :], in0=mx[:, 0:4, :], in1=sg_[:, 0:4, :], op=ALU.mult),
                          reads=[mx, sg_], writes=[mx])
                    xo_ = xo.next()
                    for j in range(4):
                        p = pj.next()
                        for n in range(2):
                            for k in range(KC):
                                fw.op('pe', lambda t, k=k, n=n, p=p, j=j: t.matmul(
                                    p[:, n * 512:(n + 1) * 512], lhsT=mx[:, k, j * 128:(j + 1) * 128],
                                    rhs=wo[:, k, n * 512:(n + 1) * 512], start=(k == 0), stop=(k == KC - 1)),
                                    reads=[mx, wo], writes=[p], inc=(n == 1 and k == KC - 1))
                        tt = t32.next()
                        fw.op('dve', lambda v, p=p, tt=tt: v.tensor_tensor(out=tt[:], in0=p[:], in1=gate1[:], op=ALU.mult),
                              reads=[p, gate1], writes=[tt])
                        fw.op('pool', lambda g, tt=tt, j=j: g.tensor_tensor(out=xo_[:, j, :], in0=tt[:], in1=xt[:, j, :], op=ALU.add),
                              reads=[tt, xt], writes=[xo_])
                    fw.dma(STQ, [(out[u * 512:(u + 1) * 512, :].rearrange("(j p) d -> p j d", p=128), xo_[:])],
                           reads=[xo_], accw=[out])
                fw.barrier()
        fw.barrier()
        build.nins = dict(fw.nins)
    except _Stop:
        pass
    return nc


def _rope_tables(pos_row, pos_col):
    f32 = np.float32

    def cs(pos, half):
        inv = (f32(10000.0) ** (-(np.arange(half, dtype=f32)) / f32(half))).astype(f32)
        ang = (pos.astype(f32)[:, None] * inv[None, :]).astype(f32)
        return np.cos(ang).astype(f32), np.sin(ang).astype(f32)

    cr8, sr8 = cs(pos_row, 8)
    cc8, sc8 = cs(pos_col, 8)
    cr16, sr16 = cs(pos_row, 16)
    cc16, sc16 = cs(pos_col, 16)
    C32 = np.concatenate([cr8, cr8, cc8, cc8], 1)
    S32 = np.concatenate([-sr8, sr8, -sc8, sc8], 1)
    C64 = np.concatenate([cr16, cr16, cc16, cc16], 1)
    S64 = np.concatenate([-sr16, sr16, -sc16, sc16], 1)
    return np.ascontiguousarray(np.concatenate([C32, S32, C64, S64], 1).astype(f32))


_NC_CACHE = {}


def run(inputs, S, stop=None, dbg=None, dbg_out=()):
    f32 = np.float32
    x = np.asarray(inputs['x'], f32)
    B = x.shape[0]
    assert x.shape[1] == S and B == 2
    T_own = S // 4
    T_q = T_own + 256
    if (S, stop) not in _NC_CACHE:
        _NC_CACHE[(S, stop)] = build(S, stop, dbg_out)
    nc = _NC_CACHE[(S, stop)]
    c = np.asarray(inputs['c'], f32)
    ctx = np.asarray(inputs['ctx'], f32)
    c_ctx = np.asarray(inputs['c_ctx'], f32)
    w_in0 = np.asarray(inputs['ab_w_in'], f32)[0]
    w_in1 = np.asarray(inputs['cd_w_in'], f32)[0]
    wkv0 = np.ascontiguousarray(np.concatenate([w_in0[:, 256:512], w_in0[:, 1056:1184], w_in0[:, 1184:1312], w_in0[:, 512:544]], 1))
    wq0 = np.ascontiguousarray(np.concatenate([w_in0[:, 0:256], w_in0[:, 544:1056]], 1))
    wg0 = np.ascontiguousarray(w_in0[:, 1312:2336])
    wt1 = np.ascontiguousarray(w_in1[:, 0:768])
    wf1 = np.ascontiguousarray(w_in1[:, 768:3328])
    gains0 = np.concatenate([np.asarray(inputs['mla_q_gain'], f32)[0], np.asarray(inputs['mla_k_gain'], f32)[0],
                             np.asarray(inputs['gqa_q_gain'], f32)[0], np.asarray(inputs['gqa_k_gain'], f32)[0]])[None, :]
    gains1 = np.concatenate([np.asarray(inputs['win_q_gain'], f32)[0], np.asarray(inputs['win_k_gain'], f32)[0]])[None, :]
    convw = np.ascontiguousarray(np.asarray(inputs['conv_w'], f32)[0].reshape(3, 4, 128).transpose(2, 1, 0))
    pos = np.arange(S)
    ropek = _rope_tables(pos // GRID_W, pos % GRID_W)
    shared = {
        "mod_w": np.ascontiguousarray(np.asarray(inputs['mod_w'], f32)),
        "mod_b": np.ascontiguousarray(np.asarray(inputs['mod_b'], f32)),
        "wkv0": wkv0, "wq0": wq0, "wg0": wg0,
        "wout0": np.ascontiguousarray(np.asarray(inputs['ab_w_out'], f32)[0]),
        "w_uq": np.ascontiguousarray(np.asarray(inputs['mla_w_uq'], f32)[0]),
        "w_ukv": np.ascontiguousarray(np.asarray(inputs['mla_w_ukv'], f32)[0]),
        "cq_gain": np.ascontiguousarray(np.asarray(inputs['mla_cq_gain'], f32)[0].reshape(2, 128).T),
        "ckv_gain": np.ascontiguousarray(np.asarray(inputs['mla_ckv_gain'], f32)[0].reshape(2, 128).T),
        "gains0": np.ascontiguousarray(gains0),
        "wt1": wt1, "wf1": wf1,
        "wout1": np.ascontiguousarray(np.asarray(inputs['cd_w_out'], f32)[0]),
        "gains1": np.ascontiguousarray(gains1),
        "sink": np.ascontiguousarray(np.asarray(inputs['win_sink'], f32)[0][None, :]),
        "convw": convw,
        "ropek": ropek,
    }
    in_maps = []
    for core in range(NCORES):
        b, qc = core // 4, core % 4
        start = qc * T_own
        lo, hi = start - 128, start + T_own + 128
        xq = np.zeros((T_q, D), f32)
        a, e = max(lo, 0), min(hi, S)
        xq[a - lo:e - lo] = x[b, a:e]
        pq = np.clip(np.arange(lo, hi), 0, S - 1)
        vcol = np.array([1.0 if lo >= 0 else 0.0, 1.0 if hi <= S else 0.0], f32)
        cvec = np.stack([c[b], c_ctx], 1).reshape(KC, 128, 2).transpose(1, 0, 2)
        m = dict(shared)
        m.update({
            "xq": xq, "xkv": np.ascontiguousarray(x[b]), "ctxb": np.ascontiguousarray(ctx[b]),
            "cT": np.ascontiguousarray(cvec.astype(f32)),
            "ropeq": _rope_tables(pq // GRID_W, pq % GRID_W),
            "valid": np.ascontiguousarray(np.broadcast_to(vcol[None, :], (128, 2)).astype(f32)),
        })
        in_maps.append(m)
    res = run_bass_kernel_spmd(nc, in_maps, core_ids=list(range(NCORES)))
    if dbg is not None:
        dbg.append(res)
    outp = np.zeros((B, S, D), f32)
    for core in range(NCORES):
        b, qc = core // 4, core % 4
        outp[b, qc * T_own:(qc + 1) * T_own] = res.results[core]["out"]
    return outp


def kernel(**inputs):
    return run(inputs, 16384)
```
